# Optimizing a Trainium2 kernel written in Bass

```python
import math
import jax, jax.numpy as jnp
from jax import lax
import numpy as np


D_MODEL = 1024
BATCH = 8
SEQ = 2048
DEPTH = 1

N_HEADS_A = 4
HEAD_QK = 64
HEAD_V = 2 * HEAD_QK
ATTN_WIDTH = N_HEADS_A * HEAD_V
Q_BLOCK = 128
CONV_WIDTH = 512
CONV_KERNEL = 31
Q_COLS = N_HEADS_A * 2 * HEAD_QK
K_COLS = N_HEADS_A * 2 * HEAD_QK
V_COLS = ATTN_WIDTH
GLU_COLS = 2 * CONV_WIDTH
GATE_COLS = 2 * D_MODEL
IN_COLS = Q_COLS + K_COLS + V_COLS + GLU_COLS + GATE_COLS
D_FF = ((8 * D_MODEL + 3 * 256 - 1) // (3 * 256)) * 256
N_BUCKETS = 32
MAX_EXACT = 16
MAX_DISTANCE = 128
EPS = 1e-6
NEG_INF = -1e30

kernel_name = 'hybrid_diffattn_conformer_adaln_block'


def rmsnorm(x, g):
    xf = x.astype(jnp.float32)
    y = xf * lax.rsqrt(jnp.mean(xf * xf, axis=-1, keepdims=True) + EPS)
    return y.astype(x.dtype) * g


def layernorm(x, g, b):
    xf = x.astype(jnp.float32)
    mu = jnp.mean(xf, axis=-1, keepdims=True)
    var = jnp.mean(jnp.square(xf - mu), axis=-1, keepdims=True)
    y = (xf - mu) * lax.rsqrt(var + EPS)
    return y.astype(x.dtype) * g + b


def t5_bucket(dist):
    n = jnp.maximum(dist, 0)
    large = MAX_EXACT + (jnp.log(jnp.maximum(n, 1).astype(jnp.float32) / MAX_EXACT)
                         / math.log(MAX_DISTANCE / MAX_EXACT)
                         * (N_BUCKETS - MAX_EXACT)).astype(jnp.int32)
    large = jnp.minimum(large, N_BUCKETS - 1)
    return jnp.where(n < MAX_EXACT, n, large)


def diff_attention(q, k, v, lam, rel_bias):
    b, s = q.shape[0], q.shape[1]
    nb = s // Q_BLOCK
    scale = HEAD_QK ** -0.5
    q1 = jnp.transpose(q[:, :, :, 0], (0, 2, 1, 3))
    q2 = jnp.transpose(q[:, :, :, 1], (0, 2, 1, 3))
    k1 = jnp.transpose(k[:, :, :, 0], (0, 2, 1, 3))
    k2 = jnp.transpose(k[:, :, :, 1], (0, 2, 1, 3))
    vh = jnp.transpose(v, (0, 2, 1, 3))
    q1b = jnp.moveaxis(q1.reshape(b, N_HEADS_A, nb, Q_BLOCK, HEAD_QK), 2, 0)
    q2b = jnp.moveaxis(q2.reshape(b, N_HEADS_A, nb, Q_BLOCK, HEAD_QK), 2, 0)
    starts = jnp.arange(nb, dtype=jnp.int32) * Q_BLOCK
    k_pos = jnp.arange(s, dtype=jnp.int32)

    def block(args):
        qa, qb, s0 = args
        q_pos = s0 + jnp.arange(Q_BLOCK, dtype=jnp.int32)
        dist = q_pos[:, None] - k_pos[None, :]
        bias = jnp.transpose(rel_bias[t5_bucket(dist)], (2, 0, 1)).astype(jnp.float32)
        causal = (dist >= 0)[None, None]
        s1 = jnp.einsum('bhqd,bhkd->bhqk', qa, k1).astype(jnp.float32) * scale + bias[None]
        s2 = jnp.einsum('bhqd,bhkd->bhqk', qb, k2).astype(jnp.float32) * scale + bias[None]
        p1 = jax.nn.softmax(jnp.where(causal, s1, NEG_INF), axis=-1)
        p2 = jax.nn.softmax(jnp.where(causal, s2, NEG_INF), axis=-1)
        w = (p1 - lam.astype(jnp.float32) * p2).astype(vh.dtype)
        return jnp.einsum('bhqk,bhkv->bhqv', w, vh)

    out = lax.map(block, (q1b, q2b, starts))
    out = jnp.moveaxis(out, 0, 2).reshape(b, N_HEADS_A, s, HEAD_V)
    return jnp.transpose(out, (0, 2, 1, 3))


def causal_depthwise_conv(u, w, bias):
    out = lax.conv_general_dilated(
        u, w[:, None, :], window_strides=(1,), padding=[(CONV_KERNEL - 1, 0)],
        dimension_numbers=('NWC', 'WIO', 'NWC'), feature_group_count=u.shape[-1])
    return out + bias


def setup_inputs(seed: int = 0) -> dict:
    key = jax.random.key(seed)
    ks = jax.random.split(key, 24)
    f = jnp.float32
    nrm = lambda k, shape, s: jax.random.normal(k, shape, f) * s
    L, D = DEPTH, D_MODEL
    return {
        'x': nrm(ks[0], (BATCH, SEQ, D), 1.0),
        'c': nrm(ks[1], (BATCH, D), 1.0),
        'w_ada': nrm(ks[2], (L, D, 6 * D), D ** -0.5),
        'b_ada': nrm(ks[3], (L, 6 * D), 0.02),
        'norm1_g': 1.0 + nrm(ks[4], (L, D), 0.02),
        'norm2_g': 1.0 + nrm(ks[5], (L, D), 0.02),
        'final_g': 1.0 + nrm(ks[6], (D,), 0.02),
        'w_in': nrm(ks[7], (L, D, IN_COLS), D ** -0.5),
        'lambda_q1': nrm(ks[8], (L, HEAD_QK), 0.1),
        'lambda_k1': nrm(ks[9], (L, HEAD_QK), 0.1),
        'lambda_q2': nrm(ks[10], (L, HEAD_QK), 0.1),
        'lambda_k2': nrm(ks[11], (L, HEAD_QK), 0.1),
        'rel_bias': nrm(ks[12], (N_BUCKETS, N_HEADS_A), 0.5),
        'attn_sub_g': 1.0 + nrm(ks[13], (L, HEAD_V), 0.02),
        'w_o_attn': nrm(ks[14], (L, ATTN_WIDTH, D), ATTN_WIDTH ** -0.5),
        'conv_w': nrm(ks[15], (L, CONV_KERNEL, CONV_WIDTH), CONV_KERNEL ** -0.5),
        'conv_b': nrm(ks[16], (L, CONV_WIDTH), 0.02),
        'conv_ln_g': 1.0 + nrm(ks[17], (L, CONV_WIDTH), 0.02),
        'conv_ln_b': nrm(ks[18], (L, CONV_WIDTH), 0.02),
        'w_o_conv': nrm(ks[19], (L, CONV_WIDTH, D), CONV_WIDTH ** -0.5),
        'b_o_conv': nrm(ks[20], (L, D), 0.02),
        'w_out': nrm(ks[21], (L, D, D), D ** -0.5),
        'w_ffn_in': nrm(ks[22], (L, D, 2 * D_FF), D ** -0.5),
        'w_ffn_out': nrm(ks[23], (L, D_FF, D), D_FF ** -0.5),
    }


def reference(x, c, w_ada, b_ada, norm1_g, norm2_g, final_g, w_in,
              lambda_q1, lambda_k1, lambda_q2, lambda_k2, rel_bias, attn_sub_g,
              w_o_attn, conv_w, conv_b, conv_ln_g, conv_ln_b, w_o_conv, b_o_conv,
              w_out, w_ffn_in, w_ffn_out):
    b, s, _ = x.shape
    c_act = jax.nn.silu(c)
    for l in range(DEPTH):
        ada = c_act @ w_ada[l] + b_ada[l]
        sh1, sc1, gt1, sh2, sc2, gt2 = [a[:, None, :] for a in jnp.split(ada, 6, axis=-1)]

        h = rmsnorm(x, norm1_g[l]) * (1.0 + sc1) + sh1
        proj = h @ w_in[l]
        q, k, v, glu, gates = jnp.split(
            proj, np.cumsum([Q_COLS, K_COLS, V_COLS, GLU_COLS]).tolist(), axis=-1)
        g_attn, g_conv = jnp.split(gates, 2, axis=-1)

        lam_init = 0.8 - 0.6 * math.exp(-0.3 * l)
        lam = (jnp.exp(jnp.sum(lambda_q1[l] * lambda_k1[l]))
               - jnp.exp(jnp.sum(lambda_q2[l] * lambda_k2[l])) + lam_init)
        q = q.reshape(b, s, N_HEADS_A, 2, HEAD_QK)
        k = k.reshape(b, s, N_HEADS_A, 2, HEAD_QK)
        v = v.reshape(b, s, N_HEADS_A, HEAD_V)
        a = diff_attention(q, k, v, lam, rel_bias)
        a = rmsnorm(a, attn_sub_g[l]) * (1.0 - lam_init)
        a = a.reshape(b, s, ATTN_WIDTH) @ w_o_attn[l]

        u_lin, u_gate = jnp.split(glu, 2, axis=-1)
        u = u_lin * jax.nn.sigmoid(u_gate)
        u = causal_depthwise_conv(u, conv_w[l], conv_b[l])
        u = jax.nn.silu(layernorm(u, conv_ln_g[l], conv_ln_b[l]))
        cv = u @ w_o_conv[l] + b_o_conv[l]

        y = jax.nn.sigmoid(g_attn) * a + jax.nn.sigmoid(g_conv) * cv
        x = x + gt1 * (y @ w_out[l])

        h2 = rmsnorm(x, norm2_g[l]) * (1.0 + sc2) + sh2
        fg, fu = jnp.split(h2 @ w_ffn_in[l], 2, axis=-1)
        x = x + gt2 * ((jax.nn.silu(fg) * fu) @ w_ffn_out[l])
    return rmsnorm(x, final_g)
```

```python
import math
from bisect import bisect_right
from contextlib import ExitStack

import numpy as np
import concourse.bass as bass
import concourse.mybir as mybir
from concourse.bass_utils import run_bass_kernel_spmd

F32 = mybir.dt.float32
BF16 = mybir.dt.bfloat16
U8 = mybir.dt.uint8
AF = mybir.ActivationFunctionType
ALU = mybir.AluOpType
AX = mybir.AxisListType

S = 2048
D = 1024
NT = 16
NH = 4
DFF = 2816
NF = 22
IN_COLS = 4608
EPS = 1e-6
ARENA = 207 * 1024
KB = 1024
NDSEM = 12
import os
EVAC_SPLIT = int(os.environ.get('EVAC_SPLIT', '4'))
SP0_INTER = int(os.environ.get('SP0_INTER', '1'))
TGOFF = int(os.environ.get('TGOFF', '16'))


def _esize(dt):
    return mybir.dt.size(dt)


class IMap:
    def __init__(self, size):
        self.b = [0, size]
        self.w = [None]
        self.r = [dict()]

    def _split(self, x):
        i = bisect_right(self.b, x) - 1
        if self.b[i] == x:
            return i
        self.b.insert(i + 1, x)
        self.w.insert(i + 1, self.w[i])
        self.r.insert(i + 1, dict(self.r[i]))
        return i + 1

    def access(self, lo, hi, op, write, deps):
        i = self._split(lo)
        j = self._split(hi)
        for s in range(i, j):
            wr = self.w[s]
            if wr is not None and wr is not op:
                deps[wr] = 'raw'
            if write:
                for q in self.r[s].values():
                    if q is not op and q not in deps:
                        deps[q] = 'war'
                self.w[s] = op
                self.r[s] = {}
            else:
                key = ('d', id(op)) if op.dma else op.eng
                self.r[s][key] = op


class Op:
    __slots__ = ('eng', 'fn', 'deps', 'sig', 'sigidx', 'dma', 'dslot', 'dval')

    def __init__(self, eng, fn, dma):
        self.eng = eng
        self.fn = fn
        self.dma = dma
        self.sig = False
        self.sigidx = 0
        self.deps = {}
        self.dslot = 0
        self.dval = 0


class Prog:
    ENGS = ['pe', 'act', 'dve', 'pool', 'sp']

    def __init__(self):
        self.ops = {e: [] for e in self.ENGS}
        self.maps = {'mem': IMap(ARENA), 'ps': IMap(16 * KB), 'scr': IMap(1 << 20)}

    def _regs0(self, ap):
        name = ap.tensor.name
        if name not in self.maps:
            return None, None
        es = _esize(ap.dtype)
        dims = list(ap.ap)
        if name == 'scr':
            off = ap.offset
            fd = dims
        else:
            pstep = (ARENA if name == 'mem' else 16 * KB) // es
            off = ap.offset % pstep
            fd = dims[1:]
        fd = [(s_, n_) for (s_, n_) in fd if n_ > 1]
        if not fd:
            return name, [(off * es, (off + 1) * es)]
        if fd[-1][0] == 1:
            run = fd[-1][1]
            outer = fd[:-1]
        else:
            run = 1
            outer = fd
        cnt = 1
        for _, n_ in outer:
            cnt *= n_
        if cnt > 64 or any(s_ < 0 for s_, _ in outer):
            lo = off
            hi = off + sum(s_ * (n_ - 1) for s_, n_ in fd) + 1
            return name, [(lo * es, hi * es)]
        starts = [off]
        for s_, n_ in outer:
            starts = [a + s_ * i for a in starts for i in range(n_)]
        return name, [(a * es, (a + run) * es) for a in starts]

    def _regs(self, ap):
        name, ivs = self._regs0(ap)
        if name == 'ps':
            banks = sorted(set(b for lo, hi in ivs for b in range(lo // 2048, (hi - 1) // 2048 + 1)))
            ivs = [(b * 2048, (b + 1) * 2048) for b in banks]
        return name, ivs

    def add(self, eng, fn, r=(), w=(), dma=False):
        op = Op(eng, fn, dma)
        deps = {}
        for ap in r:
            name, ivs = self._regs(ap)
            if name:
                m = self.maps[name]
                for lo, hi in ivs:
                    m.access(lo, hi, op, False, deps)
        for ap in w:
            name, ivs = self._regs(ap)
            if name:
                m = self.maps[name]
                for lo, hi in ivs:
                    m.access(lo, hi, op, True, deps)
        fdeps = []
        for d, kind in deps.items():
            if (not d.dma) and (not dma) and d.eng == eng:
                if eng == 'pe':
                    continue
            fdeps.append(d)
        op.deps = fdeps
        self.ops[eng].append(op)
        return op

    def finalize(self):
        for e in self.ENGS:
            for op in self.ops[e]:
                for d in op.deps:
                    if not d.dma:
                        d.sig = True
        for e in self.ENGS:
            n = 0
            k = 0
            for op in self.ops[e]:
                if op.dma:
                    op.dslot = k % NDSEM
                    op.dval = 16 * (k // NDSEM + 1)
                    k += 1
                elif op.sig:
                    n += 1
                    op.sigidx = n

    def emit(self, eng, e, esems, dsems):
        waited = {}
        last = {}
        for op in self.ops[eng]:
            if op.dma and op.dval > 16:
                key = ('d', eng, op.dslot)
                if waited.get(key, 0) < op.dval - 16:
                    e.wait_ge(dsems[eng][op.dslot], op.dval - 16)
                    waited[key] = op.dval - 16
            for d in op.deps:
                if d.dma:
                    key = ('d', d.eng, d.dslot)
                    val = d.dval
                    sem = dsems[d.eng][d.dslot]
                else:
                    key = d.eng
                    val = d.sigidx
                    sem = esems[d.eng]
                if waited.get(key, 0) < val:
                    e.wait_ge(sem, val)
                    waited[key] = val
            inst = op.fn(e)
            if op.dma:
                inst.then_inc(dsems[eng][op.dslot], 16)
                last[op.dslot] = op.dval
            elif op.sig:
                inst.then_inc(esems[eng], 1)
        for slot, val in last.items():
            e.wait_ge(dsems[eng][slot], val)


def _host_consts():
    cf = np.zeros((128, 256), np.float32)
    cf[np.arange(128), np.arange(128)] = 1.0
    cf[np.arange(128), 128 + 127 - np.arange(128)] = 1.0
    oh = np.zeros((33, 383), np.float32)
    for dd in range(383):
        delta = dd - 127
        if delta < 0:
            oh[32, dd] = 1.0
        else:
            n = delta
            if n < 16:
                bkt = n
            else:
                v = np.float32(np.log(np.float32(n) / np.float32(16.0)))
                v = np.float32(v / np.float32(math.log(128 / 16)))
                v = np.float32(v * np.float32(16.0))
                bkt = min(16 + int(v), 31)
            oh[bkt, dd] = 1.0
    sel = np.zeros((4, 4, 128), np.float32)
    for j in range(4):
        sel[j, j, :] = 1.0
    return cf, oh, sel.reshape(4, 512)


class _Stop(Exception):
    pass


def build_nc(stop=0):
    nc = bass.Bass("TRN2", target_bir_lowering=False)
    P = Prog()

    def ck(n):
        if stop == n:
            raise _Stop()

    def din(name, shape):
        return nc.dram_tensor(name, list(shape), F32, kind="ExternalInput").ap()

    x_d = din("x", [S, D])
    c_d = din("cT", [128, 8])
    wada_d = din("w_ada", [D, 6 * D])
    bada_d = din("b_ada", [6, D])
    n1g_d = din("norm1_g", [1, D])
    n2g_d = din("norm2_g", [1, D])
    fg_d = din("final_g", [1, D])
    win_d = din("w_in", [D, IN_COLS])
    lam_d = din("lam4", [1, 256])
    rb_d = din("rel_bias", [32, 4])
    asg_d = din("attn_sub_g", [1, 128])
    woa_d = din("w_o_attn", [512, D])
    cw_d = din("conv_w", [31, 512])
    cb_d = din("conv_b", [1, 512])
    lg_d = din("conv_ln_g", [1, 512])
    lb_d = din("conv_ln_b", [1, 512])
    woc_d = din("w_o_conv", [512, D])
    boc_d = din("b_o_conv", [1, D])
    wout_d = din("w_out", [D, D])
    wfi_d = din("w_ffn_in", [D, 2 * DFF])
    wfo_d = din("w_ffn_out", [DFF, D])
    cf_d = din("cst_f", [128, 256])
    oh_d = din("cst_oh", [33, 383])
    sel_d = din("cst_sel", [4, 512])
    out_d = nc.dram_tensor("out", [S, D], F32, kind="ExternalOutput").ap()
    scr_d = nc.dram_tensor("scr", [4, 384], F32, kind="Internal").ap()

    wada3 = wada_d.rearrange("(p kc) n -> p kc n", kc=8)
    win3 = win_d.rearrange("(kc p) n -> p kc n", p=128)
    wout3 = wout_d.rearrange("(kc p) n -> p kc n", p=128)
    woa3 = woa_d.rearrange("(c p) n -> p c n", p=128)
    woc3 = woc_d.rearrange("(c p) n -> p c n", p=128)
    wfi3 = wfi_d.rearrange("(kc p) n -> p kc n", p=128)
    wfo3 = wfo_d.rearrange("(f p) n -> p f n", p=128)

    with ExitStack() as es:
        mem = es.enter_context(nc.sbuf_tensor("mem", [128, ARENA], U8))
        ps = es.enter_context(nc.psum_tensor("ps", [128, 4096], F32))
        esems = {e: es.enter_context(nc.semaphore("s_" + e)) for e in ['pe', 'act', 'dve', 'pool']}
        dsems = {q: [es.enter_context(nc.semaphore("d_%s%d" % (q, i))) for i in range(NDSEM)]
                 for q in ['sp', 'pool']}
        block = es.enter_context(nc.Block())

        def V(off, dt, *shape, parts=128):
            n = 1
            for s_ in shape:
                n *= s_
            ap = mem[0:parts, off:off + n * _esize(dt)].bitcast(dt)
            if len(shape) == 2:
                ap = ap.rearrange("p (a b) -> p a b", b=shape[1])
            elif len(shape) == 3:
                ap = ap.rearrange("p (a b c) -> p a b c", b=shape[1], c=shape[2])
            return ap

        def bank(b):
            return ps[:, b * 512:(b + 1) * 512]

        def bank_bf(b):
            return ps[:, b * 512:(b + 1) * 512].bitcast(BF16)

        def mm(out, lhsT, rhs, start=True, stop=True, skip=False):
            if skip:
                return P.add('pe', lambda e: e.matmul(out, lhsT=lhsT, rhs=rhs, start=start, stop=stop,
                                                      skip_group_check=True), r=[lhsT, rhs], w=[out])
            return P.add('pe', lambda e: e.matmul(out, lhsT=lhsT, rhs=rhs, start=start, stop=stop),
                         r=[lhsT, rhs], w=[out])

        def tr(out, in_, ident):
            return P.add('pe', lambda e: e.transpose(out=out, in_=in_, identity=ident),
                         r=[in_, ident], w=[out])

        def act(out, in_, func, bias=None, scale=None, accum=None):
            rr = [in_]
            kw = {}
            if accum is not None:
                kw['accum_out'] = accum
            if bias is not None:
                kw['bias'] = bias
                if not isinstance(bias, float):
                    rr.append(bias)
            if scale is not None:
                kw['scale'] = scale
                if not isinstance(scale, float):
                    rr.append(scale)
            return P.add('act', lambda e: e.activation(out=out, in_=in_, func=func, **kw), r=rr,
                         w=[out] + ([accum] if accum is not None else []))

        def ts(eng, out, in0, s1, s2, op0, op1=None):
            rr = [in0] + [s_ for s_ in (s1, s2) if s_ is not None and not isinstance(s_, float)]
            if op1 is None:
                return P.add(eng, lambda e: e.tensor_scalar(out=out, in0=in0, scalar1=s1, scalar2=None, op0=op0),
                             r=rr, w=[out])
            return P.add(eng, lambda e: e.tensor_scalar(out=out, in0=in0, scalar1=s1, scalar2=s2, op0=op0, op1=op1),
                         r=rr, w=[out])

        def stt(eng, out, in0, sc, in1, op0, op1):
            rr = [in0, in1] + ([] if isinstance(sc, float) else [sc])
            return P.add(eng, lambda e: e.scalar_tensor_tensor(out=out, in0=in0, scalar=sc, in1=in1, op0=op0, op1=op1),
                         r=rr, w=[out])

        def tt(eng, out, in0, in1, op):
            return P.add(eng, lambda e: e.tensor_tensor(out=out, in0=in0, in1=in1, op=op), r=[in0, in1], w=[out])

        def cp(eng, out, in_):
            if eng == 'act':
                return P.add('act', lambda e: e.copy(out=out, in_=in_), r=[in_], w=[out])
            return P.add(eng, lambda e: e.tensor_copy(out=out, in_=in_), r=[in_], w=[out])

        def memset(eng, out, val):
            return P.add(eng, lambda e: e.memset(out, val), r=[], w=[out])

        def rsum(out, in_):
            return P.add('dve', lambda e: e.reduce_sum(out=out, in_=in_, axis=AX.X), r=[in_], w=[out])

        def dma(q, out, in_):
            return P.add(q, lambda e: e.dma_start(out=out, in_=in_), r=[in_], w=[out], dma=True)

        o = 0

        def alloc(nbytes):
            nonlocal o
            a = o
            o += (nbytes + 31) // 32 * 32
            return a

        c_cf = V(alloc(1024), F32, 256)
        ident_f = c_cf[:, 0:128]
        J_f = c_cf[:, 128:256]
        ones_f = V(alloc(512), F32, 128)
        neghalf = V(alloc(2048), F32, 512)
        ident_b = V(alloc(256), BF16, 128)
        sel_sb = V(alloc(2048), F32, 4, 128, parts=4)
        cT = V(alloc(32), F32, 8)
        cact = V(alloc(32), F32, 8)
        csig = V(alloc(32), F32, 8)
        L1 = V(alloc(8 * 2 * 2 * 2), BF16, 8, 2, 2)
        L2 = V(alloc(8 * 4 * 4 * 2), BF16, 8, 4, 4)
        C1 = V(alloc(64), F32, 8, 2)
        C2 = V(alloc(128), F32, 8, 4)
        C3 = V(alloc(128), F32, 8, 4)
        A1 = V(alloc(32), F32, 8)
        A2 = V(alloc(32), F32, 8)
        S1t = V(alloc(32), F32, 8)
        S2t = V(alloc(32), F32, 8)
        boc = V(alloc(32), F32, 8)
        CW = V(alloc(4 * 34 * 4), F32, 4, 34)
        lams = V(alloc(32), F32, 8, parts=1)
        neglam = V(alloc(32), F32, 1)
        rbaug = V(alloc(32), F32, 4, parts=33)
        chrow = V(alloc(32), F32, 4, parts=1)
        chcol = V(alloc(32), F32, 4)
        ch4 = V(alloc(32), F32, 1, parts=4)
        EB = V(alloc(4 * 256 * 2), BF16, 4, 256)
        gsubbc = V(alloc(512), F32, 128)
        gcol = V(alloc(32), F32, 1)
        gt1bc = V(alloc(4096), F32, 1024)
        gt2bc = V(alloc(4096), F32, 1024)
        fgbc = V(alloc(4096), F32, 1024)
        THF = [V(alloc(2048), F32, 512) for _ in range(2)]
        ssA = V(alloc(64), F32, 16)
        tvA = V(alloc(64), F32, 16)
        rsA = V(alloc(64), F32, 16)
        ssB = V(alloc(64), F32, 16)
        tvB = V(alloc(64), F32, 16)
        rsB = V(alloc(64), F32, 16)
        ssC = V(alloc(64), F32, 16)
        tvC = V(alloc(64), F32, 16)
        rsC = V(alloc(64), F32, 16)
        assert o <= 28 * KB, o
        o = 28 * KB
        o_HT = alloc(32 * KB)
        o_AN = alloc(16 * KB)
        o_UR = alloc(2 * 8320)
        o_PT = alloc(8 * KB)
        o_WIN = alloc(16 * KB)
        o_TMP = alloc(20 * KB)
        o_QK = alloc(16 * KB)
        o_V = alloc(16 * 4 * 130 * 2)
        o_ACC = alloc(32 * KB)
        assert o <= ARENA, o
        HT = V(o_HT, BF16, 8, S)
        ANv = V(o_AN, BF16, 4, S)
        XR = [V(o_AN + i * 4096, F32, 1024) for i in range(3)]
        UP = [V(o_UR + i * 4160, BF16, 2080) for i in range(2)]
        DG = V(o_UR + 8320, BF16, 31, 128)
        U2 = V(o_UR, BF16, 4, S)
        PT = [V(o_PT + i * 1024, BF16, 512) for i in range(8)]
        WIN = [V(o_WIN + i * 8192, BF16, 8, 512) for i in range(2)]
        WGA = [V(o_WIN + i * 2048, BF16, 8, 128) for i in range(4)]
        WGC = [V(o_WIN + 8 * KB + i * 2048, BF16, 8, 128) for i in range(4)]
        QT = [V(o_QK + i * 8192, BF16, S) for i in range(2)]
        QZ = [[QT[0], V(o_PT + 4 * KB, BF16, S)], [QT[1], V(o_TMP + 10752, BF16, S)]]
        KT = [V(o_QK + i * 8192 + 4096, BF16, S) for i in range(2)]
        VA = V(o_V, BF16, 16, 4, 130)
        YT = V(o_QK, BF16, 8, S)
        ACC = V(o_ACC, F32, 4, S)
        WOA = V(o_PT, BF16, 4, D)
        WOC = V(o_ACC + 8 * KB, BF16, 4, D)
        WOUT = V(o_ACC + 16 * KB, BF16, 8, D)
        B1 = V(o_ACC, F32, 1024, parts=2)
        R1 = V(o_ACC + 4 * KB, F32, 1024, parts=2)
        R3 = V(o_ACC + 8 * KB, F32, 1024, parts=4)
        crow = V(o_ACC + 12 * KB, F32, 512, parts=34)
        oh_sb = V(o_ACC + 14 * KB, F32, 383, parts=33)
        B2 = V(o_AN + 8 * KB, F32, 1024, parts=4)
        R2 = V(o_AN + 12 * KB, F32, 1024, parts=4)
        lamr = V(o_TMP + 14 * KB + 512, F32, 256, parts=1)
        t4 = V(o_AN + 13 * KB, F32, 384, parts=4)
        HK = [V(o_PT + i * 1024, F32, 256) for i in range(4)]
        brow = V(o_AN + 12 * KB, F32, 256, parts=1)
        lamt = V(o_TMP + 15 * KB + 512, F32, 64, parts=1)
        WA = [V(o_ACC + 16 * KB, BF16, 8, 512), V(o_ACC + 24 * KB, BF16, 8, 512),
              V(o_QK, BF16, 8, 512), V(o_QK + 8 * KB, BF16, 8, 512)]
        X1 = V(o_HT, F32, 16, D)
        assert o_HT + 64 * KB <= o_PT
        o_H2T = o_PT
        o_FW = o_H2T + 16 * KB
        o_XSF = o_FW + 16 * KB
        o_JNK = o_XSF + 4 * KB
        o_ACTT = o_JNK + 4 * KB
        o_WO = o_ACTT + 44 * KB
        o_OUTS = o_WO + 2 * 11264
        assert o_OUTS + 8 * KB <= ARENA, (o_OUTS, ARENA)
        assert o_ACTT == o_TMP + 16 * KB, (o_ACTT, o_TMP)
        H2T = V(o_H2T, BF16, 8, 1024)
        FW = [V(o_FW + i * 8192, BF16, 8, 512) for i in range(2)]
        XSF = [V(o_XSF + i * 2048, BF16, 1024) for i in range(2)]
        JNK = V(o_JNK, F32, 1024)
        ACTT = V(o_ACTT, BF16, NF, 1024)
        WO = [V(o_WO + i * 11264, BF16, NF, 256) for i in range(2)]
        OUTS = [V(o_OUTS + i * 4096, F32, 1024) for i in range(2)]

        def TMPV(off, dt, *shape):
            return V(o_TMP + off, dt, *shape)

        def record():
            dma('sp', c_cf, cf_d)
            dma('sp', cT, c_d)
            dma('sp', B1, bada_d[0:2, :])
            dma('sp', R3[0:1, :], n1g_d)
            memset('pool', ones_f, 1.0)
            memset('pool', neghalf, -0.5)
            memset('pool', L1, 0.0)
            memset('pool', L2, 0.0)
            cp('dve', ident_b, ident_f)
            ck(0.1)
            act(csig, cT, AF.Tanh, scale=0.5)
            ts('dve', csig, csig, 0.5, 0.5, ALU.mult, ALU.add)
            tt('dve', cact, cT, csig, ALU.mult)
            for kc in range(8):
                for j in range(2):
                    cp('dve', L1[:, kc, j, j:j + 1], cact[:, kc:kc + 1])
                for j in range(4):
                    cp('dve', L2[:, kc, j, j:j + 1], cact[:, kc:kc + 1])

            ck(0.2)
            bctr = [0]

            nbanks = [8]
            nblo = [0]

            def nb():
                b = nblo[0] + bctr[0] % nbanks[0]
                bctr[0] += 1
                return b

            trctr = [0]

            def nb_tr():
                if nblo[0] == 0:
                    return nb()
                b = trctr[0] % nblo[0]
                trctr[0] += 1
                return b

            def norm_a(xsrc, junk, xs, ssv, tvv, rsv, col):
                act(junk, xsrc, AF.Square, accum=ssv[:, col:col + 1])
                ts('dve', tvv[:, col:col + 1], ssv[:, col:col + 1], 1.0 / D, EPS, ALU.mult, ALU.add)
                tt('pool', rsv[:, col:col + 1], tvv[:, col:col + 1], neghalf[:, 0:1], ALU.pow)
                ts('dve', xs, xsrc, rsv[:, col:col + 1], None, ALU.mult)

            def norm_b_tr(xs):
                b = nb_tr()
                pb = bank_bf(b).rearrange("p (a b) -> p a b", b=128)
                for kc in range(8):
                    tr(pb[:, kc, :], xs[:, kc * 128:(kc + 1) * 128], ident_b)
                return pb

            def norm_b_ev(pb, Acol, Scol, dstT, tcol):
                for kc in range(8):
                    dst = dstT[:, kc, tcol * 128:(tcol + 1) * 128]
                    if tcol % 2 == 0:
                        act(dst, pb[:, kc, :], AF.Identity, bias=Scol[:, kc:kc + 1], scale=Acol[:, kc:kc + 1])
                    else:
                        ts('dve', dst, pb[:, kc, :], Acol[:, kc:kc + 1], Scol[:, kc:kc + 1], ALU.mult, ALU.add)

            def norm_b(xs, Acol, Scol, dstT, tcol):
                norm_b_ev(norm_b_tr(xs), Acol, Scol, dstT, tcol)

            junk1 = [TMPV(i * 4096, F32, 1024) for i in range(2)]
            xs1 = [TMPV(8192 + i * 2048, BF16, 1024) for i in range(3)]

            def phase1_a(t):
                xr = XR[t % 3]
                dma('sp', xr, x_d[t * 128:(t + 1) * 128, :])
                norm_a(xr, junk1[t % 2], xs1[t % 3], ssA, tvA, rsA, t)

            LEAD = 2
            pbs = {}

            def PRE_HOOK():
                for t_ in range(LEAD):
                    phase1_a(t_)

            adac = [0]

            def ada_part(jlist, Ltile, Brow, Rrow, nrows):
                bks = [nb(), nb()]
                first = [True, True]
                total = len(jlist) * 8
                cnt = [0, 0]
                for jj, j in enumerate(jlist):
                    for half in range(2):
                        wa = WA[adac[0] % 4]
                        adac[0] += 1
                        dma('pool', wa, wada3[:, :, j * 1024 + half * 512: j * 1024 + (half + 1) * 512])
                        for kc in range(8):
                            cnt[half] += 1
                            mm(bank(bks[half])[0:nrows, :], Ltile[:, kc, jj, :], wa[:, kc, :],
                               start=first[half], stop=(cnt[half] == total))
                            first[half] = False
                for half in range(2):
                    tt('dve', Rrow[:, half * 512:(half + 1) * 512], bank(bks[half])[0:nrows, :],
                       Brow[:, half * 512:(half + 1) * 512], ALU.add)

            PRE_HOOK()
            ada_part([0, 1], L1, B1, R1, 2)
            ck(0.3)

            def rows_to_cols(Rrow, nrows, Cout):
                b = nb()
                for kc in range(8):
                    mm(bank(b)[:, kc * 32:(kc + 1) * 32], Rrow[:, kc * 128:(kc + 1) * 128],
                       ident_f[0:nrows, 0:32])
                cp('dve', Cout, bank(b)[:, 0:256].rearrange("p (a b) -> p a b", b=32)[:, :, 0:nrows])

            rows_to_cols(R1, 2, C1)
            ck(1)
            dma('sp', R3[1:2, :], n2g_d)
            dma('sp', R3[2:3, :], boc_d)
            dma('sp', R3[3:4, :], fg_d)
            dma('sp', oh_sb, oh_d)
            dma('sp', sel_sb, sel_d.rearrange("j (a m) -> j a m", m=128))
            dma('sp', crow[0:31, :], cw_d)
            dma('sp', crow[31:32, :], cb_d)
            dma('sp', crow[32:33, :], lg_d)
            dma('sp', crow[33:34, :], lb_d)
            dma('sp', lamr, lam_d)
            dma('sp', rbaug[0:32, :], rb_d)
            memset('dve', brow, 0.0)
            dma('sp', brow[:, 128:132], rb_d[31:32, :])
            ch4src = bass.AP(rb_d.tensor, 124, [[1, 4], [1, 1]])
            P.add('sp', lambda e: e.dma_start(out=ch4, in_=ch4src), r=[], w=[ch4], dma=True)
            dma('sp', brow[:, 0:128], asg_d)
            gsrc = bass.AP(asg_d.tensor, 0, [[1, 128], [1, 1]])
            P.add('sp', lambda e: e.dma_start(out=gcol, in_=gsrc), r=[], w=[gcol], dma=True)
            ts('dve', gcol, gcol, 0.8, None, ALU.mult)
            ck(1.1)
            rows_to_cols(R3, 4, C3)
            ck(1.15)
            for half in range(2):
                b = nb()
                mm(bank(b), sel_sb[:, 3, :], R3[:, half * 512:(half + 1) * 512])
                cp('dve', fgbc[:, half * 512:(half + 1) * 512], bank(b))
            ck(1.2)
            stt('dve', A1, C1[:, :, 1], 1.0, C3[:, :, 0], ALU.add, ALU.mult)
            S1 = C1[:, :, 0]

            cp('dve', S1t, S1)
            ck(1.3)
            def load_stage_w(st):
                w = WIN[st % 2]
                for i, c0 in enumerate([st * 128, 512 + st * 128, 1536 + st * 128, 2048 + st * 128]):
                    dma('pool', w[:, :, i * 128:(i + 1) * 128], win3[:, :, c0:c0 + 128])

            def stage_proj_chunk(st, tcx):
                w = WIN[st % 2]
                qt, kt, up = QT[st % 2], KT[st % 2], UP[st % 2]
                if tcx == 0:
                    memset('dve', QZ[st % 2][0][64:128, :], 0.0)
                    memset('dve', QZ[st % 2][1][0:64, :], 0.0)
                for which, dstT in ((0, qt), (1, kt)):
                    b = nb()
                    for kc in range(8):
                        mm(bank(b), w[:, kc, which * 128:(which + 1) * 128], HT[:, kc, tcx * 512:(tcx + 1) * 512],
                           start=(kc == 0), stop=(kc == 7))
                    if which == 1:
                        cp('dve', dstT[:, tcx * 512:(tcx + 1) * 512], bank(b))
                    else:
                        cp('dve', QZ[st % 2][0][0:64, tcx * 512:(tcx + 1) * 512], bank(b)[0:64, :])
                        cp('dve', QZ[st % 2][1][64:128, tcx * 512:(tcx + 1) * 512], bank(b)[64:128, :])
                bl, bg = nb(), nb()
                for kc in range(8):
                    mm(bank(bl), w[:, kc, 256:384], HT[:, kc, tcx * 512:(tcx + 1) * 512], start=(kc == 0), stop=(kc == 7))
                for kc in range(8):
                    mm(bank(bg), w[:, kc, 384:512], HT[:, kc, tcx * 512:(tcx + 1) * 512], start=(kc == 0), stop=(kc == 7))
                tg = TMPV(TGOFF * KB + (tcx % 2) * 2048, F32, 512)
                act(tg, bank(bg), AF.Tanh, scale=0.5)
                stt('dve', up[:, 32 + tcx * 512: 32 + (tcx + 1) * 512], tg, 1.0, bank(bl), ALU.add, ALU.mult)

            def stage_proj_closures(st):
                w = WIN[st % 2]
                kt, up = KT[st % 2], UP[st % 2]
                LT = TMPV(16 * KB, F32, 512)
                tgs = TMPV(18 * KB, F32, 512)
                cl = []
                for tcx in range(4):
                    sl = slice(tcx * 512, (tcx + 1) * 512)

                    def grp(c0, sl=sl):
                        for kc in range(8):
                            mm(bank(6), w[:, kc, c0:c0 + 128], HT[:, kc, sl], start=(kc == 0), stop=(kc == 7))

                    def cq(tcx=tcx, sl=sl, grp=grp):
                        if tcx == 0:
                            memset('dve', QZ[st % 2][0][64:128, :], 0.0)
                            memset('dve', QZ[st % 2][1][0:64, :], 0.0)
                        grp(0)
                        cp('dve', QZ[st % 2][0][0:64, sl], bank(6)[0:64, :])
                        cp('dve', QZ[st % 2][1][64:128, sl], bank(6)[64:128, :])

                    def ckk(sl=sl, grp=grp):
                        grp(128)
                        cp('dve', kt[:, sl], bank(6))

                    def cl_(sl=sl, grp=grp):
                        grp(256)
                        cp('dve', LT, bank(6))

                    def cg(tcx=tcx, grp=grp):
                        grp(384)
                        act(tgs, bank(6), AF.Tanh, scale=0.5)
                        stt('dve', up[:, 32 + tcx * 512: 32 + (tcx + 1) * 512], tgs, 1.0, LT, ALU.add, ALU.mult)
                    cl += [cq, ckk, cl_, cg]
                return cl

            def stage_proj(st):
                bctr[0] = 6 if nbanks[0] == 7 else bctr[0]
                for tcx in range(4):
                    stage_proj_chunk(st, tcx)

            def vproj(t):
                b = nb()
                for kc in range(8):
                    mm(bank(b), HT[:, kc, t * 128:(t + 1) * 128], WVb[:, kc, :], start=(kc == 0), stop=(kc == 7))
                cp('act', VA[:, t, :, 0:128], bank(b).rearrange("p (h v) -> p h v", v=128))

            WVb = V(o_QK + 8 * KB, BF16, 8, 512)
            dma('pool', WVb, win3[:, :, 1024:1536])
            load_stage_w(0)
            load_stage_w(1)
            def conv_cols():
                b = nb()
                for c in range(4):
                    mm(bank(b)[:, c * 64:(c + 1) * 64], crow[:, c * 128:(c + 1) * 128], ident_f[0:34, 0:64])
                cwp = bank(b)[:, 0:256].rearrange("p (a b) -> p a b", b=64)
                ts('dve', CW[:, :, 0:31], cwp[:, :, 0:31], 0.5, None, ALU.mult)
                cp('dve', CW[:, :, 31:32], cwp[:, :, 31:32])
                ts('dve', CW[:, :, 32:34], cwp[:, :, 32:34], 0.5, None, ALU.mult)

            ck(4)

            def misc_consts():
                tt('dve', lamt, lamr[:, 0:64], lamr[:, 64:128], ALU.mult)
                rsum(lams[:, 0:1], lamt)
                tt('dve', lamt, lamr[:, 128:192], lamr[:, 192:256], ALU.mult)
                rsum(lams[:, 1:2], lamt)
                act(lams[:, 2:4], lams[:, 0:2], AF.Exp)
                tt('dve', lams[:, 4:5], lams[:, 3:4], lams[:, 2:3], ALU.subtract)
                ts('dve', brow[:, 132:133], lams[:, 4:5], -0.2, None, ALU.add)
                b = nb()
                mm(bank(b)[:, 0:256], ones_f[0:1, :], brow)
                ts('dve', gsubbc, bank(b)[:, 0:128], 0.8, None, ALU.mult)
                cp('dve', chcol, bank(b)[:, 128:132])
                cp('dve', neglam, bank(b)[:, 132:133])
                memset('dve', rbaug[32:33, :], -30000.0)
                b2 = nb()
                mm(bank(b2)[0:4, 0:383], rbaug, oh_sb)
                memset('dve', t4, 0.0)
                ts('dve', t4[:, 0:383], bank(b2)[0:4, 0:383], ch4[:, 0:1], None, ALU.subtract)
                dma('sp', scr_d, t4)
                for h in range(NH):
                    src = bass.AP(scr_d.tensor, h * 384, [[1, 128], [1, 256]])
                    P.add('sp', (lambda s_, o_: (lambda e: e.dma_start(out=o_, in_=s_)))(src, HK[h]), r=[scr_d], w=[HK[h]], dma=True)

            def misc_consts_b():
                for h in range(NH):
                    b3 = nb()
                    mm(bank(b3)[:, 0:256], J_f, HK[h])
                    act(EB[:, h, :], bank(b3)[:, 0:256], AF.Exp)

            memset('dve', VA[:, :, :, 128:130], 1.0)
            memset('dve', UP[0][:, 0:32], 0.0)
            memset('dve', UP[1][:, 0:32], 0.0)
            for t in range(LEAD, NT + LEAD):
                if t < NT:
                    phase1_a(t)
                tb = t - LEAD
                norm_b(xs1[tb % 3], A1, S1t, HT, tb)
                if tb >= 1:
                    vproj(tb - 1)
                if tb == 3:
                    conv_cols()
                    misc_consts()
                if SP0_INTER and tb >= 4 and tb % 4 == 0:
                    stage_proj_chunk(0, tb // 4 - 1)
            vproj(NT - 1)
            if SP0_INTER:
                stage_proj_chunk(0, 3)
            else:
                for tcx_ in range(4):
                    stage_proj_chunk(0, tcx_)
            ck(2)

            ck(3)
            def ada2_closures():
                cl = []
                chunks = [(jj, j, half) for jj, j in enumerate([2, 3, 4, 5]) for half in range(2)]

                def issue(i):
                    jj, j, half = chunks[i]
                    dma('pool', WA[i % 2], wada3[:, :, j * 1024 + half * 512: j * 1024 + (half + 1) * 512])

                def first():
                    dma('sp', B2, bada_d[2:6, :])
                    cp('dve', R2, B2)
                    issue(0)
                    issue(1)
                cl.append(first)
                for i in range(8):
                    def chunk(i=i):
                        jj, j, half = chunks[i]
                        wa = WA[i % 2]
                        for kc in range(8):
                            mm(bank(6)[0:4, :], L2[:, kc, jj, :], wa[:, kc, :], start=(kc == 0), stop=(kc == 7))
                        tt('dve', R2[:, half * 512:(half + 1) * 512], R2[:, half * 512:(half + 1) * 512],
                           bank(6)[0:4, :], ALU.add)
                        if i + 2 < 8:
                            issue(i + 2)
                    cl.append(chunk)

                def fin():
                    b = 6
                    for kc in range(8):
                        mm(bank(b)[:, kc * 32:(kc + 1) * 32], R2[:, kc * 128:(kc + 1) * 128], ident_f[0:4, 0:32])
                    cp('dve', C2, bank(b)[:, 0:256].rearrange("p (a b) -> p a b", b=32)[:, :, 0:4])
                    stt('dve', A2, C2[:, :, 2], 1.0, C3[:, :, 1], ALU.add, ALU.mult)
                    cp('dve', S2t, C2[:, :, 1])
                cl.append(fin)
                for (dst, j, scl) in [(gt1bc, 0, 0.5), (gt2bc, 3, 1.0)]:
                    for half in range(2):
                        def selc(dst=dst, j=j, scl=scl, half=half):
                            mm(bank(6), sel_sb[:, j, :], R2[:, half * 512:(half + 1) * 512])
                            ts('dve', dst[:, half * 512:(half + 1) * 512], bank(6), scl, None, ALU.mult)
                        cl.append(selc)
                return cl

            misc_consts_b()
            ck(5)


            pbg = []

            NPE = 28

            def build_diag(st):
                for j in range(NPE):
                    ts('dve', DG[:, j, :], ident_f, CW[:, st, j:j + 1], None, ALU.mult)

            def queue_conv(st):
                up = UP[st % 2]
                for tcx in range(4):
                    for j in range(NPE):
                        pbg.append((lambda j_, t_: (lambda: mm(bank(7), DG[:, j_, :],
                                                               up[:, 2 + j_ + t_ * 512: 2 + j_ + (t_ + 1) * 512],
                                                               start=(j_ == 0), stop=(j_ == NPE - 1))))(j, tcx))
                    pbg.append((lambda t_: (lambda: ts('dve', ACC[:, st, t_ * 512:(t_ + 1) * 512], bank(7),
                                                       CW[:, st, 31:32], None, ALU.add)))(tcx))
                    for j in range(NPE, 31):
                        pbg.append((lambda j_, t_: (lambda: stt('dve', ACC[:, st, t_ * 512:(t_ + 1) * 512],
                                                                up[:, 2 + j_ + t_ * 512: 2 + j_ + (t_ + 1) * 512],
                                                                CW[:, st, j_:j_ + 1],
                                                                ACC[:, st, t_ * 512:(t_ + 1) * 512],
                                                                ALU.mult, ALU.add)))(j, tcx))

            def pump(n):
                for _ in range(n):
                    if pbg:
                        pbg.pop(0)()

            SK = 2
            deferred = []
            gstep = [0]

            def attention(h):
                qt, kt = QT[h % 2], KT[h % 2]
                obanks = [[2, 3], [4, 5]]
                steps = []
                for qc in range(4):
                    for j in range(2):
                        for kb in range(4 * qc + 4):
                            steps.append((qc, j, kb))
                pts = {}
                firsts = {}

                def front(idx):
                    qc, j, kb = steps[idx]
                    i = kb - 4 * qc
                    c0 = max(i, 0) * 128
                    sps = bank(idx % 2)
                    mm(sps[:, c0:512], kt[:, kb * 128:(kb + 1) * 128],
                       QZ[h % 2][j][:, qc * 512 + c0:(qc + 1) * 512])
                    pt = PT[idx % 4]
                    pts[idx] = pt
                    act(pt[:, c0:512], sps[:, c0:512], AF.Exp, bias=chcol[:, h:h + 1], scale=0.125)
                    if i >= 0:
                        if i < 3:
                            tt('dve', pt[:, i * 128:(i + 2) * 128], pt[:, i * 128:(i + 2) * 128], EB[:, h, :], ALU.mult)
                        else:
                            tt('dve', pt[:, 384:512], pt[:, 384:512], EB[:, h, 0:128], ALU.mult)
                    elif i == -1:
                        tt('dve', pt[:, 0:128], pt[:, 0:128], EB[:, h, 128:256], ALU.mult)

                def back(idx):
                    qc, j, kb = steps[idx]
                    i = kb - 4 * qc
                    pt = pts.pop(idx)
                    if kb == 0:
                        firsts[(qc, j)] = [True, True]
                    first = firsts[(qc, j)]
                    for s_ in range(max(i, 0), 4):
                        ob = obanks[j][s_ // 2]
                        last = (kb == 4 * qc + s_)
                        mm(bank(ob)[:, (s_ % 2) * 256:(s_ % 2) * 256 + 129], pt[:, s_ * 128:(s_ + 1) * 128],
                           VA[:, kb, h, 0:129], start=first[s_ // 2], stop=last, skip=True)
                        first[s_ // 2] = False
                    if kb == 4 * qc + 3:
                        epilogue(qc, j)

                def epilogue(qc, j):
                    o1n = TMPV((qc % 2) * 2048, F32, 4, 128)
                    rr = TMPV(10 * KB + (qc % 2) * 64, F32, 8)
                    oreg = ps[:, obanks[j][0] * 512:(obanks[j][0] + 2) * 512].rearrange("p (s c) -> p s c", c=256)
                    P.add('dve', (lambda o_, i_: (lambda e: e.reciprocal(out=o_, in_=i_)))(rr[:, j * 4:(j + 1) * 4], oreg[:, :, 128]),
                          r=[oreg[:, :, 128]], w=[rr[:, j * 4:(j + 1) * 4]])
                    if j == 0:
                        for s_ in range(4):
                            ts('dve', o1n[:, s_, :], oreg[:, s_, 0:128], rr[:, s_:s_ + 1], None, ALU.mult)
                    else:
                        ts('dve', rr[:, 4:8], rr[:, 4:8], neglam[:, 0:1], None, ALU.mult)
                        dd = TMPV(4 * KB, F32, 4, 128)
                        for s_ in range(4):
                            stt('dve', dd[:, s_, :], oreg[:, s_, 0:128], rr[:, 4 + s_:5 + s_], o1n[:, s_, :], ALU.mult, ALU.add)
                        dsq = TMPV(6 * KB, F32, 4, 128)
                        tt('dve', dsq, dd, dd, ALU.mult)
                        ssq = TMPV(10 * KB + 128, F32, 4)
                        tvq = TMPV(10 * KB + 160, F32, 4)
                        rsq = TMPV(10 * KB + 192 + (qc % 2) * 32, F32, 4)
                        rsum(ssq, dsq)
                        ts('dve', tvq, ssq, 1.0 / 128, EPS, ALU.mult, ALU.add)
                        ant = TMPV(8 * KB + (qc % 2) * 1024, BF16, 4, 128)

                        def mid(ant=ant, tvq=tvq, rsq=rsq, dd=dd):
                            act(tvq, tvq, AF.Ln)
                            act(rsq, tvq, AF.Exp, scale=-0.5)
                            for s_ in range(4):
                                ts('dve', ant[:, s_, :], dd[:, s_, :], rsq[:, s_:s_ + 1], None, ALU.mult)
                        deferred.append([gstep[0] + 4, mid])

                        def tail(ant=ant, qc=qc):
                            bt = 6
                            pb = bank_bf(bt).rearrange("p (a b) -> p a b", b=128)
                            for s_ in range(4):
                                tr(pb[:, s_, :], ant[:, s_, :], ident_b)
                            cp('dve', ANv[:, h, qc * 512:(qc + 1) * 512], bank_bf(bt)[:, 0:512])
                        deferred.append([gstep[0] + 10, tail])

                n = len(steps)
                for idx in range(n + SK):
                    gstep[0] += 1
                    while deferred and deferred[0][0] <= gstep[0]:
                        deferred.pop(0)[1]()
                    if idx < n:
                        front(idx)
                    if idx - SK >= 0:
                        back(idx - SK)
                    pump(2)

            nbanks[0] = 7
            ck(6)
            for h in range(NH):
                build_diag(h)
                queue_conv(h)
                if h == 0:
                    for i_, c_ in enumerate(ada2_closures()):
                        pbg.insert(min(len(pbg), 10 + i_ * 13), c_)
                if h + 1 < NH:
                    for i_, c_ in enumerate(stage_proj_closures(h + 1)):
                        pbg.insert(min(len(pbg), 5 + i_ * 9), c_)
                    if h + 2 < NH:
                        load_stage_w(h + 2)
                attention(h)
                pump(len(pbg))
            while deferred:
                deferred.pop(0)[1]()
            nbanks[0] = 8
            ck(7)
            def load_wga(f):
                dma('pool', WGA[f % 4], win3[:, :, 2560 + f * 128: 2560 + (f + 1) * 128])

            W5B = [WGA[0], WGA[1], WGC[0], WGC[1], WGC[2], WGC[3], WGA[2], WGA[3]]

            def load_wgc(f):
                dma('pool', W5B[f], win3[:, :, 3584 + f * 128: 3584 + (f + 1) * 128])

            load_wga(0)
            load_wga(1)
            dma('pool', WOA, woa3)
            ts('dve', WOA, WOA, gcol[:, 0:1], None, ALU.mult)
            cp('dve', boc, C3[:, :, 2])

            def ln_parts(tcx):
                sl = slice(tcx * 512, (tcx + 1) * 512)
                mean = TMPV(4 * KB, F32, 512)
                msq = TMPV(6 * KB, F32, 512)
                rstd = TMPV(8 * KB, F32, 512)
                st_ = {}

                SQ = [V(o_WIN + 8 * KB + c * 2048, F32, 512) for c in range(4)]

                def p_sq():
                    for c in range(4):
                        act(SQ[c], ACC[:, c, sl], AF.Square)

                def p_stats():
                    bs, bq = nb(), nb()
                    st_['bs'], st_['bq'] = bs, bq
                    for c in range(4):
                        mm(bank(bs), ones_f, ACC[:, c, sl], start=(c == 0), stop=(c == 3))
                    for c in range(4):
                        mm(bank(bq), ones_f, SQ[c], start=(c == 0), stop=(c == 3))

                def p_var():
                    ts('dve', mean, bank(st_['bs']), 1.0 / 512, None, ALU.mult)
                    tt('dve', msq, mean, mean, ALU.mult)
                    stt('dve', msq, bank(st_['bq']), 1.0 / 512, msq, ALU.mult, ALU.subtract)
                    ts('dve', msq, msq, EPS, None, ALU.add)
                    act(rstd, msq, AF.Ln)
                    act(rstd, rstd, AF.Exp, scale=-0.5)

                def p_c(c):
                    def f_():
                        xn = TMPV(10 * KB + (c % 2) * 2048, F32, 512)
                        th = TMPV(14 * KB + (c % 2) * 2048, F32, 512)
                        zp = TMPV(0, F32, 512)
                        tt('dve', xn, ACC[:, c, sl], mean, ALU.subtract)
                        tt('dve', xn, xn, rstd, ALU.mult)
                        act(th, xn, AF.Tanh, bias=CW[:, c, 33:34], scale=CW[:, c, 32:33])
                        act(zp, xn, AF.Identity, bias=CW[:, c, 33:34], scale=CW[:, c, 32:33])
                        stt('dve', U2[:, c, sl], th, 1.0, zp, ALU.add, ALU.mult)
                    return f_
                return [p_sq, p_stats, p_var, p_c(0), p_c(1), p_c(2), p_c(3)]

            lnq = []
            parts_ = [ln_parts(tcx) for tcx in range(4)]
            lnq.append(parts_[0][0])
            for tcx in range(4):
                p = parts_[tcx]
                lnq += [p[1], p[2], p[3]]
                if tcx + 1 < 4:
                    lnq.append(parts_[tcx + 1][0])
                lnq += [p[4], p[5], p[6]]

            ck(8)
            ta_ring = [TMPV(2 * KB, F32, 512), TMPV(18 * KB, F32, 512)]
            it = 0
            for f in range(8):
                if f + 2 < 8:
                    load_wga(f + 2)
                if f == 6:
                    load_wgc(0)
                    load_wgc(1)
                w = WGA[f % 4]
                for tcx in range(4):
                    sl = slice(tcx * 512, (tcx + 1) * 512)
                    ba, bo = nb(), nb()
                    for kc in range(8):
                        mm(bank(ba), w[:, kc, :], HT[:, kc, sl], start=(kc == 0), stop=(kc == 7))
                    for c in range(4):
                        mm(bank(bo), WOA[:, c, f * 128:(f + 1) * 128], ANv[:, c, sl], start=(c == 0), stop=(c == 3))
                    ta = ta_ring[it % 2]
                    act(ta, bank(ba), AF.Tanh, scale=0.5)
                    stt('dve', YT[:, f, sl], ta, 1.0, bank(bo), ALU.add, ALU.mult)
                    it += 1
                    if it >= 2 and lnq:
                        lnq.pop(0)()

            while lnq:
                lnq.pop(0)()
            dma('pool', WOC, woc3)
            for f_ in range(2, 8):
                load_wgc(f_)
            dma('pool', WOUT, wout3)
            for f in range(8):
                w = W5B[f]
                pre_b = {}
                if f == 0:
                    for tcx in range(4):
                        sl = slice(tcx * 512, (tcx + 1) * 512)
                        pre_b[tcx] = nb()
                        for kc in range(8):
                            mm(bank(pre_b[tcx]), w[:, kc, :], HT[:, kc, sl], start=(kc == 0), stop=(kc == 7))
                for tcx in range(4):
                    sl = slice(tcx * 512, (tcx + 1) * 512)
                    if f == 0:
                        bc_, bv = pre_b[tcx], nb()
                    else:
                        bc_, bv = nb(), nb()
                        for kc in range(8):
                            mm(bank(bc_), w[:, kc, :], HT[:, kc, sl], start=(kc == 0), stop=(kc == 7))
                    for c in range(4):
                        mm(bank(bv), WOC[:, c, f * 128:(f + 1) * 128], U2[:, c, sl], start=(c == 0), stop=(c == 3))
                    k2 = (f * 4 + tcx) % 2
                    tg = TMPV(4 * KB + k2 * 2048, F32, 512)
                    cvb = TMPV(8 * KB + k2 * 2048, F32, 512)
                    act(tg, bank(bc_), AF.Tanh, scale=0.5)
                    act(cvb, bank(bv), AF.Identity, bias=boc[:, f:f + 1], scale=1.0)
                    stt('dve', tg, tg, 1.0, cvb, ALU.add, ALU.mult)
                    tt('dve', YT[:, f, sl], YT[:, f, sl], tg, ALU.add)
                if f == 3:
                    for kc in range(8):
                        tt('dve', WOUT[:, kc, :], WOUT[:, kc, :], gt1bc, ALU.mult)

            ck(9)
            S2f = S2t

            def norm2_a(hf, t):
                norm_a(X1[:, hf * 8 + t, :], JNK, XSF[t % 2], ssB, tvB, rsB, hf * 8 + t)

            def norm2_b(hf, t):
                norm_b(XSF[t % 2], A2, S2f, H2T, t)

            wtmp = [TMPV(16 * KB + i * 2048, F32, 512) for i in range(2)]
            for t in range(NT):
                dma('sp', X1[:, t, :], x_d[t * 128:(t + 1) * 128, :])
                for n in range(2):
                    b = nb()
                    for kc in range(8):
                        mm(bank(b), YT[:, kc, t * 128:(t + 1) * 128], WOUT[:, kc, n * 512:(n + 1) * 512],
                           start=(kc == 0), stop=(kc == 7))
                    tt('dve', X1[:, t, n * 512:(n + 1) * 512], X1[:, t, n * 512:(n + 1) * 512], bank(b), ALU.add)
                if t < 8:
                    norm2_a(0, t)
                if 1 <= t <= 8:
                    norm2_b(0, t - 1)

            ck(10)
            fwc = [0]

            def load_fw(g):
                w = FW[fwc[0] % 2]
                fwc[0] += 1
                dma('pool', w[:, :, 0:256], wfi3[:, :, g * 256:(g + 1) * 256])
                dma('pool', w[:, :, 256:512], wfi3[:, :, DFF + g * 256: DFF + (g + 1) * 256])
                return w

            woc_ = [0]

            def load_wo(nq):
                w = WO[woc_[0] % 2]
                woc_[0] += 1
                dma('pool', w, wfo3[:, :, nq * 256:(nq + 1) * 256])
                return w

            def final_norm(tg_):
                ob = OUTS[tg_ % 2]
                act(ob, X1[:, tg_, :], AF.Square, accum=ssC[:, tg_:tg_ + 1])
                ts('dve', tvC[:, tg_:tg_ + 1], ssC[:, tg_:tg_ + 1], 1.0 / D, EPS, ALU.mult, ALU.add)
                tt('pool', rsC[:, tg_:tg_ + 1], tvC[:, tg_:tg_ + 1], neghalf[:, 0:1], ALU.pow)
                stt('dve', ob, X1[:, tg_, :], rsC[:, tg_:tg_ + 1], fgbc, ALU.mult, ALU.mult)
                dma('sp', out_d[tg_ * 128:(tg_ + 1) * 128, :], ob)

            nxt_fws = None
            for hf in range(2):
                fws = nxt_fws if nxt_fws is not None else [load_fw(0)]
                nxt_fws = None
                wos = None
                for g in range(11):
                    if g + 1 < 11 and len(fws) < g + 2:
                        fws.append(load_fw(g + 1))
                    w = fws[g]
                    for fi in range(2):
                        f = 2 * g + fi
                        for tcx in range(2):
                            sl = slice(tcx * 512, (tcx + 1) * 512)
                            ba, bu = nb(), nb()
                            for kc in range(8):
                                mm(bank(ba), w[:, kc, fi * 128:(fi + 1) * 128], H2T[:, kc, sl], start=(kc == 0), stop=(kc == 7))
                            for kc in range(8):
                                mm(bank(bu), w[:, kc, 256 + fi * 128: 256 + (fi + 1) * 128], H2T[:, kc, sl],
                                   start=(kc == 0), stop=(kc == 7))
                            th = THF[(f * 2 + tcx) % 2]
                            act(th, bank(ba), AF.Tanh, scale=0.5)
                            stt('dve', th, th, 1.0, bank(ba), ALU.add, ALU.mult)
                            stt('dve', ACTT[:, f, sl], th, 0.5, bank(bu), ALU.mult, ALU.mult)
                    if g == 8:
                        wos = [load_wo(0)]
                nbg = []
                if hf == 0:
                    for t in range(9):
                        nbg.append((lambda t_: (lambda: ((norm2_a(1, t_) if t_ < 8 else None),
                                                         (norm2_b(1, t_ - 1) if t_ >= 1 else None))))(t))
                unit = 0
                for nq in range(4):
                    if nq + 1 < 4:
                        wos.append(load_wo(nq + 1))
                    if nq == 1 and hf == 0:
                        nxt_fws = [load_fw(0), load_fw(1)]
                    w = wos[nq]
                    for t in range(8):
                        tg_ = hf * 8 + t
                        b = nb()
                        for f in range(NF):
                            mm(bank(b)[:, 0:256], ACTT[:, f, t * 128:(t + 1) * 128], w[:, f, :],
                               start=(f == 0), stop=(f == NF - 1))
                        wt = THF[t % 2][:, 0:256]
                        tt('dve', wt, bank(b)[:, 0:256], gt2bc[:, nq * 256:(nq + 1) * 256], ALU.mult)
                        tt('dve', X1[:, tg_, nq * 256:(nq + 1) * 256], X1[:, tg_, nq * 256:(nq + 1) * 256], wt, ALU.add)
                        if nq == 3:
                            final_norm(tg_)
                        unit += 1
                        if unit % 2 == 1 and nbg:
                            nbg.pop(0)()
                while nbg:
                    nbg.pop(0)()

        try:
            record()
        except _Stop:
            pass

        P.finalize()

        @block.tensor
        def _(e):
            P.emit('pe', e, esems, dsems)

        @block.scalar
        def _(e):
            P.emit('act', e, esems, dsems)

        @block.vector
        def _(e):
            P.emit('dve', e, esems, dsems)

        @block.gpsimd
        def _(e):
            P.emit('pool', e, esems, dsems)

        @block.sync
        def _(e):
            P.emit('sp', e, esems, dsems)

    return nc


_NC_CACHE = {}


def kernel(x, c, w_ada, b_ada, norm1_g, norm2_g, final_g, w_in,
           lambda_q1, lambda_k1, lambda_q2, lambda_k2, rel_bias, attn_sub_g,
           w_o_attn, conv_w, conv_b, conv_ln_g, conv_ln_b, w_o_conv, b_o_conv,
           w_out, w_ffn_in, w_ffn_out):
    f = lambda a: np.ascontiguousarray(np.asarray(a, dtype=np.float32))
    x = f(x)
    c = f(c)
    if 'nc' not in _NC_CACHE:
        _NC_CACHE['nc'] = build_nc()
    nc = _NC_CACHE['nc']
    cf, oh, sel = _host_consts()
    lam4 = np.concatenate([f(lambda_q1)[0], f(lambda_k1)[0], f(lambda_q2)[0], f(lambda_k2)[0]])[None, :]
    shared = {
        "w_ada": f(w_ada)[0], "b_ada": f(b_ada)[0].reshape(6, D),
        "norm1_g": f(norm1_g).reshape(1, D), "norm2_g": f(norm2_g).reshape(1, D),
        "final_g": f(final_g).reshape(1, D), "w_in": f(w_in)[0],
        "lam4": np.ascontiguousarray(lam4), "rel_bias": f(rel_bias),
        "attn_sub_g": f(attn_sub_g).reshape(1, 128), "w_o_attn": f(w_o_attn)[0],
        "conv_w": f(conv_w)[0], "conv_b": f(conv_b).reshape(1, 512),
        "conv_ln_g": f(conv_ln_g).reshape(1, 512), "conv_ln_b": f(conv_ln_b).reshape(1, 512),
        "w_o_conv": f(w_o_conv)[0], "b_o_conv": f(b_o_conv).reshape(1, D),
        "w_out": f(w_out)[0], "w_ffn_in": f(w_ffn_in)[0], "w_ffn_out": f(w_ffn_out)[0],
        "cst_f": cf, "cst_oh": oh, "cst_sel": sel,
    }
    in_maps = []
    for b in range(8):
        m = dict(shared)
        m["x"] = x[b]
        m["cT"] = np.ascontiguousarray(c[b].reshape(128, 8))
        in_maps.append(m)
    res = run_bass_kernel_spmd(nc, in_maps, core_ids=list(range(8)))
    return np.stack([np.asarray(r["out"], dtype=np.float32) for r in res.results], axis=0)
```

```python
import math
from bisect import bisect_right
from contextlib import ExitStack

import numpy as np
import concourse.bass as bass
import concourse.mybir as mybir
from concourse.bass_utils import run_bass_kernel_spmd

F32 = mybir.dt.float32
BF16 = mybir.dt.bfloat16
U8 = mybir.dt.uint8
AF = mybir.ActivationFunctionType
ALU = mybir.AluOpType
AX = mybir.AxisListType

S = 2048
D = 1024
NT = 16
NH = 4
DFF = 2816
NF = 22
IN_COLS = 4608
EPS = 1e-6
ARENA = 207 * 1024
KB = 1024
NDSEM = 12
import os
EVAC_SPLIT = int(os.environ.get('EVAC_SPLIT', '4'))
SP0_INTER = int(os.environ.get('SP0_INTER', '1'))
TGOFF = int(os.environ.get('TGOFF', '16'))


def _esize(dt):
    return mybir.dt.size(dt)


class IMap:
    def __init__(self, size):
        self.b = [0, size]
        self.w = [None]
        self.r = [dict()]

    def _split(self, x):
        i = bisect_right(self.b, x) - 1
        if self.b[i] == x:
            return i
        self.b.insert(i + 1, x)
        self.w.insert(i + 1, self.w[i])
        self.r.insert(i + 1, dict(self.r[i]))
        return i + 1

    def access(self, lo, hi, op, write, deps):
        i = self._split(lo)
        j = self._split(hi)
        for s in range(i, j):
            wr = self.w[s]
            if wr is not None and wr is not op:
                deps[wr] = 'raw'
            if write:
                for q in self.r[s].values():
                    if q is not op and q not in deps:
                        deps[q] = 'war'
                self.w[s] = op
                self.r[s] = {}
            else:
                key = ('d', id(op)) if op.dma else op.eng
                self.r[s][key] = op


class Op:
    __slots__ = ('eng', 'fn', 'deps', 'sig', 'sigidx', 'dma', 'dslot', 'dval')

    def __init__(self, eng, fn, dma):
        self.eng = eng
        self.fn = fn
        self.dma = dma
        self.sig = False
        self.sigidx = 0
        self.deps = {}
        self.dslot = 0
        self.dval = 0


class Prog:
    ENGS = ['pe', 'act', 'dve', 'pool', 'sp']

    def __init__(self):
        self.ops = {e: [] for e in self.ENGS}
        self.maps = {'mem': IMap(ARENA), 'ps': IMap(16 * KB), 'scr': IMap(1 << 20)}

    def _regs0(self, ap):
        name = ap.tensor.name
        if name not in self.maps:
            return None, None
        es = _esize(ap.dtype)
        dims = list(ap.ap)
        if name == 'scr':
            off = ap.offset
            fd = dims
        else:
            pstep = (ARENA if name == 'mem' else 16 * KB) // es
            off = ap.offset % pstep
            fd = dims[1:]
        fd = [(s_, n_) for (s_, n_) in fd if n_ > 1]
        if not fd:
            return name, [(off * es, (off + 1) * es)]
        if fd[-1][0] == 1:
            run = fd[-1][1]
            outer = fd[:-1]
        else:
            run = 1
            outer = fd
        cnt = 1
        for _, n_ in outer:
            cnt *= n_
        if cnt > 64 or any(s_ < 0 for s_, _ in outer):
            lo = off
            hi = off + sum(s_ * (n_ - 1) for s_, n_ in fd) + 1
            return name, [(lo * es, hi * es)]
        starts = [off]
        for s_, n_ in outer:
            starts = [a + s_ * i for a in starts for i in range(n_)]
        return name, [(a * es, (a + run) * es) for a in starts]

    def _regs(self, ap):
        name, ivs = self._regs0(ap)
        if name == 'ps':
            banks = sorted(set(b for lo, hi in ivs for b in range(lo // 2048, (hi - 1) // 2048 + 1)))
            ivs = [(b * 2048, (b + 1) * 2048) for b in banks]
        return name, ivs

    def add(self, eng, fn, r=(), w=(), dma=False):
        op = Op(eng, fn, dma)
        deps = {}
        for ap in r:
            name, ivs = self._regs(ap)
            if name:
                m = self.maps[name]
                for lo, hi in ivs:
                    m.access(lo, hi, op, False, deps)
        for ap in w:
            name, ivs = self._regs(ap)
            if name:
                m = self.maps[name]
                for lo, hi in ivs:
                    m.access(lo, hi, op, True, deps)
        fdeps = []
        for d, kind in deps.items():
            if (not d.dma) and (not dma) and d.eng == eng:
                if eng == 'pe':
                    continue
            fdeps.append(d)
        op.deps = fdeps
        self.ops[eng].append(op)
        return op

    def finalize(self):
        for e in self.ENGS:
            for op in self.ops[e]:
                for d in op.deps:
                    if not d.dma:
                        d.sig = True
        for e in self.ENGS:
            n = 0
            k = 0
            for op in self.ops[e]:
                if op.dma:
                    op.dslot = k % NDSEM
                    op.dval = 16 * (k // NDSEM + 1)
                    k += 1
                elif op.sig:
                    n += 1
                    op.sigidx = n

    def emit(self, eng, e, esems, dsems):
        waited = {}
        last = {}
        for op in self.ops[eng]:
            if op.dma and op.dval > 16:
                key = ('d', eng, op.dslot)
                if waited.get(key, 0) < op.dval - 16:
                    e.wait_ge(dsems[eng][op.dslot], op.dval - 16)
                    waited[key] = op.dval - 16
            for d in op.deps:
                if d.dma:
                    key = ('d', d.eng, d.dslot)
                    val = d.dval
                    sem = dsems[d.eng][d.dslot]
                else:
                    key = d.eng
                    val = d.sigidx
                    sem = esems[d.eng]
                if waited.get(key, 0) < val:
                    e.wait_ge(sem, val)
                    waited[key] = val
            inst = op.fn(e)
            if op.dma:
                inst.then_inc(dsems[eng][op.dslot], 16)
                last[op.dslot] = op.dval
            elif op.sig:
                inst.then_inc(esems[eng], 1)
        for slot, val in last.items():
            e.wait_ge(dsems[eng][slot], val)


def _host_consts():
    cf = np.zeros((128, 256), np.float32)
    cf[np.arange(128), np.arange(128)] = 1.0
    cf[np.arange(128), 128 + 127 - np.arange(128)] = 1.0
    oh = np.zeros((33, 383), np.float32)
    for dd in range(383):
        delta = dd - 127
        if delta < 0:
            oh[32, dd] = 1.0
        else:
            n = delta
            if n < 16:
                bkt = n
            else:
                v = np.float32(np.log(np.float32(n) / np.float32(16.0)))
                v = np.float32(v / np.float32(math.log(128 / 16)))
                v = np.float32(v * np.float32(16.0))
                bkt = min(16 + int(v), 31)
            oh[bkt, dd] = 1.0
    sel = np.zeros((4, 4, 128), np.float32)
    for j in range(4):
        sel[j, j, :] = 1.0
    return cf, oh, sel.reshape(4, 512)


class _Stop(Exception):
    pass


def build_nc(stop=0):
    nc = bass.Bass("TRN2", target_bir_lowering=False)
    P = Prog()

    def ck(n):
        if stop == n:
            raise _Stop()

    def din(name, shape):
        return nc.dram_tensor(name, list(shape), F32, kind="ExternalInput").ap()

    x_d = din("x", [S, D])
    c_d = din("cT", [128, 8])
    wada_d = din("w_ada", [D, 6 * D])
    bada_d = din("b_ada", [6, D])
    n1g_d = din("norm1_g", [1, D])
    n2g_d = din("norm2_g", [1, D])
    fg_d = din("final_g", [1, D])
    win_d = din("w_in", [D, IN_COLS])
    lam_d = din("lam4", [1, 256])
    rb_d = din("rel_bias", [32, 4])
    asg_d = din("attn_sub_g", [1, 128])
    woa_d = din("w_o_attn", [512, D])
    cw_d = din("conv_w", [31, 512])
    cb_d = din("conv_b", [1, 512])
    lg_d = din("conv_ln_g", [1, 512])
    lb_d = din("conv_ln_b", [1, 512])
    woc_d = din("w_o_conv", [512, D])
    boc_d = din("b_o_conv", [1, D])
    wout_d = din("w_out", [D, D])
    wfi_d = din("w_ffn_in", [D, 2 * DFF])
    wfo_d = din("w_ffn_out", [DFF, D])
    cf_d = din("cst_f", [128, 256])
    oh_d = din("cst_oh", [33, 383])
    sel_d = din("cst_sel", [4, 512])
    out_d = nc.dram_tensor("out", [S, D], F32, kind="ExternalOutput").ap()
    scr_d = nc.dram_tensor("scr", [4, 384], F32, kind="Internal").ap()

    wada3 = wada_d.rearrange("(p kc) n -> p kc n", kc=8)
    win3 = win_d.rearrange("(kc p) n -> p kc n", p=128)
    wout3 = wout_d.rearrange("(kc p) n -> p kc n", p=128)
    woa3 = woa_d.rearrange("(c p) n -> p c n", p=128)
    woc3 = woc_d.rearrange("(c p) n -> p c n", p=128)
    wfi3 = wfi_d.rearrange("(kc p) n -> p kc n", p=128)
    wfo3 = wfo_d.rearrange("(f p) n -> p f n", p=128)

    with ExitStack() as es:
        mem = es.enter_context(nc.sbuf_tensor("mem", [128, ARENA], U8))
        ps = es.enter_context(nc.psum_tensor("ps", [128, 4096], F32))
        esems = {e: es.enter_context(nc.semaphore("s_" + e)) for e in ['pe', 'act', 'dve', 'pool']}
        dsems = {q: [es.enter_context(nc.semaphore("d_%s%d" % (q, i))) for i in range(NDSEM)]
                 for q in ['sp', 'pool']}
        block = es.enter_context(nc.Block())

        def V(off, dt, *shape, parts=128):
            n = 1
            for s_ in shape:
                n *= s_
            ap = mem[0:parts, off:off + n * _esize(dt)].bitcast(dt)
            if len(shape) == 2:
                ap = ap.rearrange("p (a b) -> p a b", b=shape[1])
            elif len(shape) == 3:
                ap = ap.rearrange("p (a b c) -> p a b c", b=shape[1], c=shape[2])
            return ap

        def bank(b):
            return ps[:, b * 512:(b + 1) * 512]

        def bank_bf(b):
            return ps[:, b * 512:(b + 1) * 512].bitcast(BF16)

        def mm(out, lhsT, rhs, start=True, stop=True, skip=False):
            if skip:
                return P.add('pe', lambda e: e.matmul(out, lhsT=lhsT, rhs=rhs, start=start, stop=stop,
                                                      skip_group_check=True), r=[lhsT, rhs], w=[out])
            return P.add('pe', lambda e: e.matmul(out, lhsT=lhsT, rhs=rhs, start=start, stop=stop),
                         r=[lhsT, rhs], w=[out])

        def tr(out, in_, ident):
            return P.add('pe', lambda e: e.transpose(out=out, in_=in_, identity=ident),
                         r=[in_, ident], w=[out])

        def act(out, in_, func, bias=None, scale=None, accum=None):
            rr = [in_]
            kw = {}
            if accum is not None:
                kw['accum_out'] = accum
            if bias is not None:
                kw['bias'] = bias
                if not isinstance(bias, float):
                    rr.append(bias)
            if scale is not None:
                kw['scale'] = scale
                if not isinstance(scale, float):
                    rr.append(scale)
            return P.add('act', lambda e: e.activation(out=out, in_=in_, func=func, **kw), r=rr,
                         w=[out] + ([accum] if accum is not None else []))

        def ts(eng, out, in0, s1, s2, op0, op1=None):
            rr = [in0] + [s_ for s_ in (s1, s2) if s_ is not None and not isinstance(s_, float)]
            if op1 is None:
                return P.add(eng, lambda e: e.tensor_scalar(out=out, in0=in0, scalar1=s1, scalar2=None, op0=op0),
                             r=rr, w=[out])
            return P.add(eng, lambda e: e.tensor_scalar(out=out, in0=in0, scalar1=s1, scalar2=s2, op0=op0, op1=op1),
                         r=rr, w=[out])

        def stt(eng, out, in0, sc, in1, op0, op1):
            rr = [in0, in1] + ([] if isinstance(sc, float) else [sc])
            return P.add(eng, lambda e: e.scalar_tensor_tensor(out=out, in0=in0, scalar=sc, in1=in1, op0=op0, op1=op1),
                         r=rr, w=[out])

        def tt(eng, out, in0, in1, op):
            return P.add(eng, lambda e: e.tensor_tensor(out=out, in0=in0, in1=in1, op=op), r=[in0, in1], w=[out])

        def cp(eng, out, in_):
            if eng == 'act':
                return P.add('act', lambda e: e.copy(out=out, in_=in_), r=[in_], w=[out])
            return P.add(eng, lambda e: e.tensor_copy(out=out, in_=in_), r=[in_], w=[out])

        def memset(eng, out, val):
            return P.add(eng, lambda e: e.memset(out, val), r=[], w=[out])

        def rsum(out, in_):
            return P.add('dve', lambda e: e.reduce_sum(out=out, in_=in_, axis=AX.X), r=[in_], w=[out])

        def dma(q, out, in_):
            return P.add(q, lambda e: e.dma_start(out=out, in_=in_), r=[in_], w=[out], dma=True)

        o = 0

        def alloc(nbytes):
            nonlocal o
            a = o
            o += (nbytes + 31) // 32 * 32
            return a

        c_cf = V(alloc(1024), F32, 256)
        ident_f = c_cf[:, 0:128]
        J_f = c_cf[:, 128:256]
        ones_f = V(alloc(512), F32, 128)
        neghalf = V(alloc(2048), F32, 512)
        ident_b = V(alloc(256), BF16, 128)
        sel_sb = V(alloc(2048), F32, 4, 128, parts=4)
        cT = V(alloc(32), F32, 8)
        cact = V(alloc(32), F32, 8)
        csig = V(alloc(32), F32, 8)
        L1 = V(alloc(8 * 2 * 2 * 2), BF16, 8, 2, 2)
        L2 = V(alloc(8 * 4 * 4 * 2), BF16, 8, 4, 4)
        C1 = V(alloc(64), F32, 8, 2)
        C2 = V(alloc(128), F32, 8, 4)
        C3 = V(alloc(128), F32, 8, 4)
        A1 = V(alloc(32), F32, 8)
        A2 = V(alloc(32), F32, 8)
        S1t = V(alloc(32), F32, 8)
        S2t = V(alloc(32), F32, 8)
        boc = V(alloc(32), F32, 8)
        CW = V(alloc(4 * 34 * 4), F32, 4, 34)
        lams = V(alloc(32), F32, 8, parts=1)
        neglam = V(alloc(32), F32, 1)
        rbaug = V(alloc(32), F32, 4, parts=33)
        chrow = V(alloc(32), F32, 4, parts=1)
        chcol = V(alloc(32), F32, 4)
        ch4 = V(alloc(32), F32, 1, parts=4)
        EB = V(alloc(4 * 256 * 2), BF16, 4, 256)
        gsubbc = V(alloc(512), F32, 128)
        gcol = V(alloc(32), F32, 1)
        gt1bc = V(alloc(4096), F32, 1024)
        gt2bc = V(alloc(4096), F32, 1024)
        fgbc = V(alloc(4096), F32, 1024)
        THF = [V(alloc(2048), F32, 512) for _ in range(2)]
        ssA = V(alloc(64), F32, 16)
        tvA = V(alloc(64), F32, 16)
        rsA = V(alloc(64), F32, 16)
        ssB = V(alloc(64), F32, 16)
        tvB = V(alloc(64), F32, 16)
        rsB = V(alloc(64), F32, 16)
        ssC = V(alloc(64), F32, 16)
        tvC = V(alloc(64), F32, 16)
        rsC = V(alloc(64), F32, 16)
        assert o <= 28 * KB, o
        o = 28 * KB
        o_HT = alloc(32 * KB)
        o_AN = alloc(16 * KB)
        o_UR = alloc(2 * 8320)
        o_PT = alloc(8 * KB)
        o_WIN = alloc(16 * KB)
        o_TMP = alloc(20 * KB)
        o_QK = alloc(16 * KB)
        o_V = alloc(16 * 4 * 130 * 2)
        o_ACC = alloc(32 * KB)
        assert o <= ARENA, o
        HT = V(o_HT, BF16, 8, S)
        ANv = V(o_AN, BF16, 4, S)
        XR = [V(o_AN + i * 4096, F32, 1024) for i in range(3)]
        UP = [V(o_UR + i * 4160, BF16, 2080) for i in range(2)]
        DG = V(o_UR + 8320, BF16, 31, 128)
        U2 = V(o_UR, BF16, 4, S)
        PT = [V(o_PT + i * 1024, BF16, 512) for i in range(8)]
        WIN = [V(o_WIN + i * 8192, BF16, 8, 512) for i in range(2)]
        WGA = [V(o_WIN + i * 2048, BF16, 8, 128) for i in range(4)]
        WGC = [V(o_WIN + 8 * KB + i * 2048, BF16, 8, 128) for i in range(4)]
        QT = [V(o_QK + i * 8192, BF16, S) for i in range(2)]
        QZ = [[QT[0], V(o_PT + 4 * KB, BF16, S)], [QT[1], V(o_TMP + 10752, BF16, S)]]
        KT = [V(o_QK + i * 8192 + 4096, BF16, S) for i in range(2)]
        VA = V(o_V, BF16, 16, 4, 130)
        YT = V(o_QK, BF16, 8, S)
        ACC = V(o_ACC, F32, 4, S)
        WOA = V(o_PT, BF16, 4, D)
        WOC = V(o_ACC + 8 * KB, BF16, 4, D)
        WOUT = V(o_ACC + 16 * KB, BF16, 8, D)
        B1 = V(o_ACC, F32, 1024, parts=2)
        R1 = V(o_ACC + 4 * KB, F32, 1024, parts=2)
        R3 = V(o_ACC + 8 * KB, F32, 1024, parts=4)
        crow = V(o_ACC + 12 * KB, F32, 512, parts=34)
        oh_sb = V(o_ACC + 14 * KB, F32, 383, parts=33)
        B2 = V(o_AN + 8 * KB, F32, 1024, parts=4)
        R2 = V(o_AN + 12 * KB, F32, 1024, parts=4)
        lamr = V(o_TMP + 14 * KB + 512, F32, 256, parts=1)
        t4 = V(o_AN + 13 * KB, F32, 384, parts=4)
        HK = [V(o_PT + i * 1024, F32, 256) for i in range(4)]
        brow = V(o_AN + 12 * KB, F32, 256, parts=1)
        lamt = V(o_TMP + 15 * KB + 512, F32, 64, parts=1)
        WA = [V(o_ACC + 16 * KB, BF16, 8, 512), V(o_ACC + 24 * KB, BF16, 8, 512),
              V(o_QK, BF16, 8, 512), V(o_QK + 8 * KB, BF16, 8, 512)]
        X1 = V(o_HT, F32, 16, D)
        assert o_HT + 64 * KB <= o_PT
        o_H2T = o_PT
        o_FW = o_H2T + 16 * KB
        o_XSF = o_FW + 16 * KB
        o_JNK = o_XSF + 4 * KB
        o_ACTT = o_JNK + 4 * KB
        o_WO = o_ACTT + 44 * KB
        o_OUTS = o_WO + 2 * 11264
        assert o_OUTS + 8 * KB <= ARENA, (o_OUTS, ARENA)
        assert o_ACTT == o_TMP + 16 * KB, (o_ACTT, o_TMP)
        H2T = V(o_H2T, BF16, 8, 1024)
        FW = [V(o_FW + i * 8192, BF16, 8, 512) for i in range(2)]
        XSF = [V(o_XSF + i * 2048, BF16, 1024) for i in range(2)]
        JNK = V(o_JNK, F32, 1024)
        ACTT = V(o_ACTT, BF16, NF, 1024)
        WO = [V(o_WO + i * 11264, BF16, NF, 256) for i in range(2)]
        OUTS = [V(o_OUTS + i * 4096, F32, 1024) for i in range(2)]

        def TMPV(off, dt, *shape):
            return V(o_TMP + off, dt, *shape)

        def record():
            dma('sp', c_cf, cf_d)
            dma('sp', cT, c_d)
            dma('sp', B1, bada_d[0:2, :])
            dma('sp', R3[0:1, :], n1g_d)
            memset('pool', ones_f, 1.0)
            memset('pool', neghalf, -0.5)
            memset('pool', L1, 0.0)
            memset('pool', L2, 0.0)
            cp('dve', ident_b, ident_f)
            ck(0.1)
            act(csig, cT, AF.Tanh, scale=0.5)
            ts('dve', csig, csig, 0.5, 0.5, ALU.mult, ALU.add)
            tt('dve', cact, cT, csig, ALU.mult)
            for kc in range(8):
                for j in range(2):
                    cp('dve', L1[:, kc, j, j:j + 1], cact[:, kc:kc + 1])
                for j in range(4):
                    cp('dve', L2[:, kc, j, j:j + 1], cact[:, kc:kc + 1])

            ck(0.2)
            bctr = [0]

            nbanks = [8]
            nblo = [0]

            def nb():
                b = nblo[0] + bctr[0] % nbanks[0]
                bctr[0] += 1
                return b

            trctr = [0]

            def nb_tr():
                if nblo[0] == 0:
                    return nb()
                b = trctr[0] % nblo[0]
                trctr[0] += 1
                return b

            def norm_a(xsrc, junk, xs, ssv, tvv, rsv, col):
                act(junk, xsrc, AF.Square, accum=ssv[:, col:col + 1])
                ts('dve', tvv[:, col:col + 1], ssv[:, col:col + 1], 1.0 / D, EPS, ALU.mult, ALU.add)
                tt('pool', rsv[:, col:col + 1], tvv[:, col:col + 1], neghalf[:, 0:1], ALU.pow)
                ts('dve', xs, xsrc, rsv[:, col:col + 1], None, ALU.mult)

            def norm_b_tr(xs):
                b = nb_tr()
                pb = bank_bf(b).rearrange("p (a b) -> p a b", b=128)
                for kc in range(8):
                    tr(pb[:, kc, :], xs[:, kc * 128:(kc + 1) * 128], ident_b)
                return pb

            def norm_b_ev(pb, Acol, Scol, dstT, tcol):
                for kc in range(8):
                    dst = dstT[:, kc, tcol * 128:(tcol + 1) * 128]
                    if tcol % 2 == 0:
                        act(dst, pb[:, kc, :], AF.Identity, bias=Scol[:, kc:kc + 1], scale=Acol[:, kc:kc + 1])
                    else:
                        ts('dve', dst, pb[:, kc, :], Acol[:, kc:kc + 1], Scol[:, kc:kc + 1], ALU.mult, ALU.add)

            def norm_b(xs, Acol, Scol, dstT, tcol):
                norm_b_ev(norm_b_tr(xs), Acol, Scol, dstT, tcol)

            junk1 = [TMPV(i * 4096, F32, 1024) for i in range(2)]
            xs1 = [TMPV(8192 + i * 2048, BF16, 1024) for i in range(3)]

            def phase1_a(t):
                xr = XR[t % 3]
                dma('sp', xr, x_d[t * 128:(t + 1) * 128, :])
                norm_a(xr, junk1[t % 2], xs1[t % 3], ssA, tvA, rsA, t)

            LEAD = 2
            pbs = {}

            def PRE_HOOK():
                for t_ in range(LEAD):
                    phase1_a(t_)

            adac = [0]

            def ada_part(jlist, Ltile, Brow, Rrow, nrows):
                bks = [nb(), nb()]
                first = [True, True]
                total = len(jlist) * 8
                cnt = [0, 0]
                for jj, j in enumerate(jlist):
                    for half in range(2):
                        wa = WA[adac[0] % 4]
                        adac[0] += 1
                        dma('pool', wa, wada3[:, :, j * 1024 + half * 512: j * 1024 + (half + 1) * 512])
                        for kc in range(8):
                            cnt[half] += 1
                            mm(bank(bks[half])[0:nrows, :], Ltile[:, kc, jj, :], wa[:, kc, :],
                               start=first[half], stop=(cnt[half] == total))
                            first[half] = False
                for half in range(2):
                    tt('dve', Rrow[:, half * 512:(half + 1) * 512], bank(bks[half])[0:nrows, :],
                       Brow[:, half * 512:(half + 1) * 512], ALU.add)

            PRE_HOOK()
            ada_part([0, 1], L1, B1, R1, 2)
            ck(0.3)

            def rows_to_cols(Rrow, nrows, Cout):
                b = nb()
                for kc in range(8):
                    mm(bank(b)[:, kc * 32:(kc + 1) * 32], Rrow[:, kc * 128:(kc + 1) * 128],
                       ident_f[0:nrows, 0:32])
                cp('dve', Cout, bank(b)[:, 0:256].rearrange("p (a b) -> p a b", b=32)[:, :, 0:nrows])

            rows_to_cols(R1, 2, C1)
            ck(1)
            dma('sp', R3[1:2, :], n2g_d)
            dma('sp', R3[2:3, :], boc_d)
            dma('sp', R3[3:4, :], fg_d)
            dma('sp', oh_sb, oh_d)
            dma('sp', sel_sb, sel_d.rearrange("j (a m) -> j a m", m=128))
            dma('sp', crow[0:31, :], cw_d)
            dma('sp', crow[31:32, :], cb_d)
            dma('sp', crow[32:33, :], lg_d)
            dma('sp', crow[33:34, :], lb_d)
            dma('sp', lamr, lam_d)
            dma('sp', rbaug[0:32, :], rb_d)
            memset('dve', brow, 0.0)
            dma('sp', brow[:, 128:132], rb_d[31:32, :])
            ch4src = bass.AP(rb_d.tensor, 124, [[1, 4], [1, 1]])
            P.add('sp', lambda e: e.dma_start(out=ch4, in_=ch4src), r=[], w=[ch4], dma=True)
            dma('sp', brow[:, 0:128], asg_d)
            gsrc = bass.AP(asg_d.tensor, 0, [[1, 128], [1, 1]])
            P.add('sp', lambda e: e.dma_start(out=gcol, in_=gsrc), r=[], w=[gcol], dma=True)
            ts('dve', gcol, gcol, 0.8, None, ALU.mult)
            ck(1.1)
            rows_to_cols(R3, 4, C3)
            ck(1.15)
            for half in range(2):
                b = nb()
                mm(bank(b), sel_sb[:, 3, :], R3[:, half * 512:(half + 1) * 512])
                cp('dve', fgbc[:, half * 512:(half + 1) * 512], bank(b))
            ck(1.2)
            stt('dve', A1, C1[:, :, 1], 1.0, C3[:, :, 0], ALU.add, ALU.mult)
            S1 = C1[:, :, 0]

            cp('dve', S1t, S1)
            ck(1.3)
            def load_stage_w(st):
                w = WIN[st % 2]
                for i, c0 in enumerate([st * 128, 512 + st * 128, 1536 + st * 128, 2048 + st * 128]):
                    dma('pool', w[:, :, i * 128:(i + 1) * 128], win3[:, :, c0:c0 + 128])

            def stage_proj_chunk(st, tcx):
                w = WIN[st % 2]
                qt, kt, up = QT[st % 2], KT[st % 2], UP[st % 2]
                if tcx == 0:
                    memset('dve', QZ[st % 2][0][64:128, :], 0.0)
                    memset('dve', QZ[st % 2][1][0:64, :], 0.0)
                for which, dstT in ((0, qt), (1, kt)):
                    b = nb()
                    for kc in range(8):
                        mm(bank(b), w[:, kc, which * 128:(which + 1) * 128], HT[:, kc, tcx * 512:(tcx + 1) * 512],
                           start=(kc == 0), stop=(kc == 7))
                    if which == 1:
                        cp('dve', dstT[:, tcx * 512:(tcx + 1) * 512], bank(b))
                    else:
                        cp('dve', QZ[st % 2][0][0:64, tcx * 512:(tcx + 1) * 512], bank(b)[0:64, :])
                        cp('dve', QZ[st % 2][1][64:128, tcx * 512:(tcx + 1) * 512], bank(b)[64:128, :])
                bl, bg = nb(), nb()
                for kc in range(8):
                    mm(bank(bl), w[:, kc, 256:384], HT[:, kc, tcx * 512:(tcx + 1) * 512], start=(kc == 0), stop=(kc == 7))
                for kc in range(8):
                    mm(bank(bg), w[:, kc, 384:512], HT[:, kc, tcx * 512:(tcx + 1) * 512], start=(kc == 0), stop=(kc == 7))
                tg = TMPV(TGOFF * KB + (tcx % 2) * 2048, F32, 512)
                act(tg, bank(bg), AF.Tanh, scale=0.5)
                stt('dve', up[:, 32 + tcx * 512: 32 + (tcx + 1) * 512], tg, 1.0, bank(bl), ALU.add, ALU.mult)

            def stage_proj(st):
                bctr[0] = 6 if nbanks[0] == 7 else bctr[0]
                for tcx in range(4):
                    stage_proj_chunk(st, tcx)

            def vproj(t):
                b = nb()
                for kc in range(8):
                    mm(bank(b), HT[:, kc, t * 128:(t + 1) * 128], WVb[:, kc, :], start=(kc == 0), stop=(kc == 7))
                cp('act', VA[:, t, :, 0:128], bank(b).rearrange("p (h v) -> p h v", v=128))

            WVb = V(o_QK + 8 * KB, BF16, 8, 512)
            dma('pool', WVb, win3[:, :, 1024:1536])
            load_stage_w(0)
            load_stage_w(1)
            def conv_cols():
                b = nb()
                for c in range(4):
                    mm(bank(b)[:, c * 64:(c + 1) * 64], crow[:, c * 128:(c + 1) * 128], ident_f[0:34, 0:64])
                cwp = bank(b)[:, 0:256].rearrange("p (a b) -> p a b", b=64)
                ts('dve', CW[:, :, 0:31], cwp[:, :, 0:31], 0.5, None, ALU.mult)
                cp('dve', CW[:, :, 31:32], cwp[:, :, 31:32])
                ts('dve', CW[:, :, 32:34], cwp[:, :, 32:34], 0.5, None, ALU.mult)

            ck(4)

            def misc_consts():
                tt('dve', lamt, lamr[:, 0:64], lamr[:, 64:128], ALU.mult)
                rsum(lams[:, 0:1], lamt)
                tt('dve', lamt, lamr[:, 128:192], lamr[:, 192:256], ALU.mult)
                rsum(lams[:, 1:2], lamt)
                act(lams[:, 2:4], lams[:, 0:2], AF.Exp)
                tt('dve', lams[:, 4:5], lams[:, 3:4], lams[:, 2:3], ALU.subtract)
                ts('dve', brow[:, 132:133], lams[:, 4:5], -0.2, None, ALU.add)
                b = nb()
                mm(bank(b)[:, 0:256], ones_f[0:1, :], brow)
                ts('dve', gsubbc, bank(b)[:, 0:128], 0.8, None, ALU.mult)
                cp('dve', chcol, bank(b)[:, 128:132])
                cp('dve', neglam, bank(b)[:, 132:133])
                memset('dve', rbaug[32:33, :], -30000.0)
                b2 = nb()
                mm(bank(b2)[0:4, 0:383], rbaug, oh_sb)
                memset('dve', t4, 0.0)
                ts('dve', t4[:, 0:383], bank(b2)[0:4, 0:383], ch4[:, 0:1], None, ALU.subtract)
                dma('sp', scr_d, t4)
                for h in range(NH):
                    src = bass.AP(scr_d.tensor, h * 384, [[1, 128], [1, 256]])
                    P.add('sp', (lambda s_, o_: (lambda e: e.dma_start(out=o_, in_=s_)))(src, HK[h]), r=[scr_d], w=[HK[h]], dma=True)

            def misc_consts_b():
                for h in range(NH):
                    b3 = nb()
                    mm(bank(b3)[:, 0:256], J_f, HK[h])
                    act(EB[:, h, :], bank(b3)[:, 0:256], AF.Exp)

            memset('dve', VA[:, :, :, 128:130], 1.0)
            memset('dve', UP[0][:, 0:32], 0.0)
            memset('dve', UP[1][:, 0:32], 0.0)
            for t in range(LEAD, NT + LEAD):
                if t < NT:
                    phase1_a(t)
                tb = t - LEAD
                norm_b(xs1[tb % 3], A1, S1t, HT, tb)
                if tb >= 1:
                    vproj(tb - 1)
                if tb == 3:
                    conv_cols()
                    misc_consts()
                if SP0_INTER and tb >= 4 and tb % 4 == 0:
                    stage_proj_chunk(0, tb // 4 - 1)
            vproj(NT - 1)
            if SP0_INTER:
                stage_proj_chunk(0, 3)
            else:
                for tcx_ in range(4):
                    stage_proj_chunk(0, tcx_)
            ck(2)

            ck(3)
            SR = [0, 1, 6]
            sctr = [0]

            def sb():
                b_ = SR[sctr[0] % 3]
                sctr[0] += 1
                return b_

            def ada2_closures():
                cl = []
                chunks = [(jj, j, half) for jj, j in enumerate([2, 3, 4, 5]) for half in range(2)]

                def issue(i):
                    jj, j, half = chunks[i]
                    dma('pool', WA[i % 2], wada3[:, :, j * 1024 + half * 512: j * 1024 + (half + 1) * 512])

                def first():
                    dma('sp', B2, bada_d[2:6, :])
                    cp('dve', R2, B2)
                    issue(0)
                    issue(1)
                cl.append(first)
                for i in range(8):
                    def chunk(i=i):
                        jj, j, half = chunks[i]
                        wa = WA[i % 2]
                        bx = sb()
                        for kc in range(8):
                            mm(bank(bx)[0:4, :], L2[:, kc, jj, :], wa[:, kc, :], start=(kc == 0), stop=(kc == 7))
                        tt('dve', R2[:, half * 512:(half + 1) * 512], R2[:, half * 512:(half + 1) * 512],
                           bank(bx)[0:4, :], ALU.add)
                        if i + 2 < 8:
                            issue(i + 2)
                    cl.append(chunk)

                def fin():
                    b = sb()
                    for kc in range(8):
                        mm(bank(b)[:, kc * 32:(kc + 1) * 32], R2[:, kc * 128:(kc + 1) * 128], ident_f[0:4, 0:32])
                    cp('dve', C2, bank(b)[:, 0:256].rearrange("p (a b) -> p a b", b=32)[:, :, 0:4])
                    stt('dve', A2, C2[:, :, 2], 1.0, C3[:, :, 1], ALU.add, ALU.mult)
                    cp('dve', S2t, C2[:, :, 1])
                cl.append(fin)
                for (dst, j, scl) in [(gt1bc, 0, 0.5), (gt2bc, 3, 1.0)]:
                    for half in range(2):
                        def selc(dst=dst, j=j, scl=scl, half=half):
                            bx = sb()
                            mm(bank(bx), sel_sb[:, j, :], R2[:, half * 512:(half + 1) * 512])
                            ts('dve', dst[:, half * 512:(half + 1) * 512], bank(bx), scl, None, ALU.mult)
                        cl.append(selc)
                return cl

            misc_consts_b()
            ck(5)


            pbg = []

            NPE = 28

            def build_diag(st):
                for j in range(NPE):
                    ts('dve', DG[:, j, :], ident_f, CW[:, st, j:j + 1], None, ALU.mult)

            def queue_conv(st):
                up = UP[st % 2]
                for tcx in range(4):
                    for j in range(NPE):
                        pbg.append((lambda j_, t_: (lambda: mm(bank(7), DG[:, j_, :],
                                                               up[:, 2 + j_ + t_ * 512: 2 + j_ + (t_ + 1) * 512],
                                                               start=(j_ == 0), stop=(j_ == NPE - 1))))(j, tcx))
                    pbg.append((lambda t_: (lambda: ts('dve', ACC[:, st, t_ * 512:(t_ + 1) * 512], bank(7),
                                                       CW[:, st, 31:32], None, ALU.add)))(tcx))
                    for j in range(NPE, 31):
                        pbg.append((lambda j_, t_: (lambda: stt('dve', ACC[:, st, t_ * 512:(t_ + 1) * 512],
                                                                up[:, 2 + j_ + t_ * 512: 2 + j_ + (t_ + 1) * 512],
                                                                CW[:, st, j_:j_ + 1],
                                                                ACC[:, st, t_ * 512:(t_ + 1) * 512],
                                                                ALU.mult, ALU.add)))(j, tcx))

            def pump(n):
                for _ in range(n):
                    if pbg:
                        pbg.pop(0)()

            SK = 3
            deferred = []
            gstep = [0]

            def attention(h):
                qt, kt = QT[h % 2], KT[h % 2]
                obanks = [[2, 3], [4, 5]]
                steps = []
                for qc in range(4):
                    for j in range(2):
                        for kb in range(4 * qc + 4):
                            steps.append((qc, j, kb))
                pts = {}
                firsts = {}

                def front(idx):
                    qc, j, kb = steps[idx]
                    i = kb - 4 * qc
                    c0 = max(i, 0) * 128
                    sps = bank(sb())
                    mm(sps[:, c0:512], kt[:, kb * 128:(kb + 1) * 128],
                       QZ[h % 2][j][:, qc * 512 + c0:(qc + 1) * 512])
                    pt = PT[idx % 4]
                    pts[idx] = pt
                    act(pt[:, c0:512], sps[:, c0:512], AF.Exp, bias=chcol[:, h:h + 1], scale=0.125)
                    if i >= 0:
                        if i < 3:
                            tt('dve', pt[:, i * 128:(i + 2) * 128], pt[:, i * 128:(i + 2) * 128], EB[:, h, :], ALU.mult)
                        else:
                            tt('dve', pt[:, 384:512], pt[:, 384:512], EB[:, h, 0:128], ALU.mult)
                    elif i == -1:
                        tt('dve', pt[:, 0:128], pt[:, 0:128], EB[:, h, 128:256], ALU.mult)

                def back(idx):
                    qc, j, kb = steps[idx]
                    i = kb - 4 * qc
                    pt = pts.pop(idx)
                    if kb == 0:
                        firsts[(qc, j)] = [True, True]
                    first = firsts[(qc, j)]
                    for s_ in range(max(i, 0), 4):
                        ob = obanks[j][s_ // 2]
                        last = (kb == 4 * qc + s_)
                        mm(bank(ob)[:, (s_ % 2) * 256:(s_ % 2) * 256 + 129], pt[:, s_ * 128:(s_ + 1) * 128],
                           VA[:, kb, h, 0:129], start=first[s_ // 2], stop=last, skip=True)
                        first[s_ // 2] = False
                    if kb == 4 * qc + 3:
                        epilogue(qc, j)

                def epilogue(qc, j):
                    o1n = TMPV((qc % 2) * 2048, F32, 4, 128)
                    rr = TMPV(10 * KB + (qc % 2) * 64, F32, 8)
                    oreg = ps[:, obanks[j][0] * 512:(obanks[j][0] + 2) * 512].rearrange("p (s c) -> p s c", c=256)
                    P.add('dve', (lambda o_, i_: (lambda e: e.reciprocal(out=o_, in_=i_)))(rr[:, j * 4:(j + 1) * 4], oreg[:, :, 128]),
                          r=[oreg[:, :, 128]], w=[rr[:, j * 4:(j + 1) * 4]])
                    if j == 0:
                        for s_ in range(4):
                            ts('dve', o1n[:, s_, :], oreg[:, s_, 0:128], rr[:, s_:s_ + 1], None, ALU.mult)
                    else:
                        ts('dve', rr[:, 4:8], rr[:, 4:8], neglam[:, 0:1], None, ALU.mult)
                        dd = TMPV(4 * KB, F32, 4, 128)
                        for s_ in range(4):
                            stt('dve', dd[:, s_, :], oreg[:, s_, 0:128], rr[:, 4 + s_:5 + s_], o1n[:, s_, :], ALU.mult, ALU.add)
                        dsq = TMPV(6 * KB, F32, 4, 128)
                        tt('dve', dsq, dd, dd, ALU.mult)
                        ssq = TMPV(10 * KB + 128, F32, 4)
                        tvq = TMPV(10 * KB + 160, F32, 4)
                        rsq = TMPV(10 * KB + 192 + (qc % 2) * 32, F32, 4)
                        rsum(ssq, dsq)
                        ts('dve', tvq, ssq, 1.0 / 128, EPS, ALU.mult, ALU.add)
                        ant = TMPV(8 * KB + (qc % 2) * 1024, BF16, 4, 128)

                        def mid(ant=ant, tvq=tvq, rsq=rsq, dd=dd):
                            act(tvq, tvq, AF.Ln)
                            act(rsq, tvq, AF.Exp, scale=-0.5)
                            for s_ in range(4):
                                ts('dve', ant[:, s_, :], dd[:, s_, :], rsq[:, s_:s_ + 1], None, ALU.mult)
                        deferred.append([gstep[0] + 4, mid])

                        def tail(ant=ant, qc=qc):
                            bt = sb()
                            pb = bank_bf(bt).rearrange("p (a b) -> p a b", b=128)
                            for s_ in range(4):
                                tr(pb[:, s_, :], ant[:, s_, :], ident_b)
                            cp('dve', ANv[:, h, qc * 512:(qc + 1) * 512], bank_bf(bt)[:, 0:512])
                        deferred.append([gstep[0] + 10, tail])

                n = len(steps)
                for idx in range(n + SK):
                    gstep[0] += 1
                    while deferred and deferred[0][0] <= gstep[0]:
                        deferred.pop(0)[1]()
                    if idx < n:
                        front(idx)
                    if idx - SK >= 0:
                        back(idx - SK)
                    pump(1)

            nbanks[0] = 7
            ck(6)
            for h in range(NH):
                build_diag(h)
                queue_conv(h)
                if h == 0:
                    for i_, c_ in enumerate(ada2_closures()):
                        pbg.insert(min(len(pbg), 10 + i_ * 13), c_)
                if h + 1 < NH:
                    stage_proj(h + 1)
                    if h + 2 < NH:
                        load_stage_w(h + 2)
                attention(h)
                pump(len(pbg))
            while deferred:
                deferred.pop(0)[1]()
            nbanks[0] = 8
            ck(7)
            def load_wga(f):
                dma('pool', WGA[f % 4], win3[:, :, 2560 + f * 128: 2560 + (f + 1) * 128])

            W5B = [WGA[0], WGA[1], WGC[0], WGC[1], WGC[2], WGC[3], WGA[2], WGA[3]]

            def load_wgc(f):
                dma('pool', W5B[f], win3[:, :, 3584 + f * 128: 3584 + (f + 1) * 128])

            load_wga(0)
            load_wga(1)
            dma('pool', WOA, woa3)
            ts('dve', WOA, WOA, gcol[:, 0:1], None, ALU.mult)
            cp('dve', boc, C3[:, :, 2])

            def ln_parts(tcx):
                sl = slice(tcx * 512, (tcx + 1) * 512)
                mean = TMPV(4 * KB, F32, 512)
                msq = TMPV(6 * KB, F32, 512)
                rstd = TMPV(8 * KB, F32, 512)
                st_ = {}

                SQ = [V(o_WIN + 8 * KB + c * 2048, F32, 512) for c in range(4)]

                def p_sq():
                    for c in range(4):
                        act(SQ[c], ACC[:, c, sl], AF.Square)

                def p_stats():
                    bs, bq = nb(), nb()
                    st_['bs'], st_['bq'] = bs, bq
                    for c in range(4):
                        mm(bank(bs), ones_f, ACC[:, c, sl], start=(c == 0), stop=(c == 3))
                    for c in range(4):
                        mm(bank(bq), ones_f, SQ[c], start=(c == 0), stop=(c == 3))

                def p_var():
                    ts('dve', mean, bank(st_['bs']), 1.0 / 512, None, ALU.mult)
                    tt('dve', msq, mean, mean, ALU.mult)
                    stt('dve', msq, bank(st_['bq']), 1.0 / 512, msq, ALU.mult, ALU.subtract)
                    ts('dve', msq, msq, EPS, None, ALU.add)
                    act(rstd, msq, AF.Ln)
                    act(rstd, rstd, AF.Exp, scale=-0.5)

                def p_c(c):
                    def f_():
                        xn = TMPV(10 * KB + (c % 2) * 2048, F32, 512)
                        th = TMPV(14 * KB + (c % 2) * 2048, F32, 512)
                        zp = TMPV(0, F32, 512)
                        tt('dve', xn, ACC[:, c, sl], mean, ALU.subtract)
                        tt('dve', xn, xn, rstd, ALU.mult)
                        act(th, xn, AF.Tanh, bias=CW[:, c, 33:34], scale=CW[:, c, 32:33])
                        act(zp, xn, AF.Identity, bias=CW[:, c, 33:34], scale=CW[:, c, 32:33])
                        stt('dve', U2[:, c, sl], th, 1.0, zp, ALU.add, ALU.mult)
                    return f_
                return [p_sq, p_stats, p_var, p_c(0), p_c(1), p_c(2), p_c(3)]

            lnq = []
            parts_ = [ln_parts(tcx) for tcx in range(4)]
            lnq.append(parts_[0][0])
            for tcx in range(4):
                p = parts_[tcx]
                lnq += [p[1], p[2], p[3]]
                if tcx + 1 < 4:
                    lnq.append(parts_[tcx + 1][0])
                lnq += [p[4], p[5], p[6]]

            ck(8)
            ta_ring = [TMPV(2 * KB, F32, 512), TMPV(18 * KB, F32, 512)]
            it = 0
            for f in range(8):
                if f + 2 < 8:
                    load_wga(f + 2)
                if f == 6:
                    load_wgc(0)
                    load_wgc(1)
                w = WGA[f % 4]
                for tcx in range(4):
                    sl = slice(tcx * 512, (tcx + 1) * 512)
                    ba, bo = nb(), nb()
                    for kc in range(8):
                        mm(bank(ba), w[:, kc, :], HT[:, kc, sl], start=(kc == 0), stop=(kc == 7))
                    for c in range(4):
                        mm(bank(bo), WOA[:, c, f * 128:(f + 1) * 128], ANv[:, c, sl], start=(c == 0), stop=(c == 3))
                    ta = ta_ring[it % 2]
                    act(ta, bank(ba), AF.Tanh, scale=0.5)
                    stt('dve', YT[:, f, sl], ta, 1.0, bank(bo), ALU.add, ALU.mult)
                    it += 1
                    if it >= 2 and lnq:
                        lnq.pop(0)()

            while lnq:
                lnq.pop(0)()
            dma('pool', WOC, woc3)
            for f_ in range(2, 8):
                load_wgc(f_)
            dma('pool', WOUT, wout3)
            for f in range(8):
                w = W5B[f]
                pre_b = {}
                if f == 0:
                    for tcx in range(4):
                        sl = slice(tcx * 512, (tcx + 1) * 512)
                        pre_b[tcx] = nb()
                        for kc in range(8):
                            mm(bank(pre_b[tcx]), w[:, kc, :], HT[:, kc, sl], start=(kc == 0), stop=(kc == 7))
                for tcx in range(4):
                    sl = slice(tcx * 512, (tcx + 1) * 512)
                    if f == 0:
                        bc_, bv = pre_b[tcx], nb()
                    else:
                        bc_, bv = nb(), nb()
                        for kc in range(8):
                            mm(bank(bc_), w[:, kc, :], HT[:, kc, sl], start=(kc == 0), stop=(kc == 7))
                    for c in range(4):
                        mm(bank(bv), WOC[:, c, f * 128:(f + 1) * 128], U2[:, c, sl], start=(c == 0), stop=(c == 3))
                    k2 = (f * 4 + tcx) % 2
                    tg = TMPV(4 * KB + k2 * 2048, F32, 512)
                    cvb = TMPV(8 * KB + k2 * 2048, F32, 512)
                    act(tg, bank(bc_), AF.Tanh, scale=0.5)
                    act(cvb, bank(bv), AF.Identity, bias=boc[:, f:f + 1], scale=1.0)
                    stt('dve', tg, tg, 1.0, cvb, ALU.add, ALU.mult)
                    tt('dve', YT[:, f, sl], YT[:, f, sl], tg, ALU.add)
                if f == 3:
                    for kc in range(8):
                        tt('dve', WOUT[:, kc, :], WOUT[:, kc, :], gt1bc, ALU.mult)

            ck(9)
            S2f = S2t

            def norm2_a(hf, t):
                norm_a(X1[:, hf * 8 + t, :], JNK, XSF[t % 2], ssB, tvB, rsB, hf * 8 + t)

            def norm2_b(hf, t):
                norm_b(XSF[t % 2], A2, S2f, H2T, t)

            wtmp = [TMPV(16 * KB + i * 2048, F32, 512) for i in range(2)]
            for t in range(NT):
                dma('sp', X1[:, t, :], x_d[t * 128:(t + 1) * 128, :])
                for n in range(2):
                    b = nb()
                    for kc in range(8):
                        mm(bank(b), YT[:, kc, t * 128:(t + 1) * 128], WOUT[:, kc, n * 512:(n + 1) * 512],
                           start=(kc == 0), stop=(kc == 7))
                    tt('dve', X1[:, t, n * 512:(n + 1) * 512], X1[:, t, n * 512:(n + 1) * 512], bank(b), ALU.add)
                if t < 8:
                    norm2_a(0, t)
                if 1 <= t <= 8:
                    norm2_b(0, t - 1)

            ck(10)
            fwc = [0]

            def load_fw(g):
                w = FW[fwc[0] % 2]
                fwc[0] += 1
                dma('pool', w[:, :, 0:256], wfi3[:, :, g * 256:(g + 1) * 256])
                dma('pool', w[:, :, 256:512], wfi3[:, :, DFF + g * 256: DFF + (g + 1) * 256])
                return w

            woc_ = [0]

            def load_wo(nq):
                w = WO[woc_[0] % 2]
                woc_[0] += 1
                dma('pool', w, wfo3[:, :, nq * 256:(nq + 1) * 256])
                return w

            def final_norm(tg_):
                ob = OUTS[tg_ % 2]
                act(ob, X1[:, tg_, :], AF.Square, accum=ssC[:, tg_:tg_ + 1])
                ts('dve', tvC[:, tg_:tg_ + 1], ssC[:, tg_:tg_ + 1], 1.0 / D, EPS, ALU.mult, ALU.add)
                tt('pool', rsC[:, tg_:tg_ + 1], tvC[:, tg_:tg_ + 1], neghalf[:, 0:1], ALU.pow)
                stt('dve', ob, X1[:, tg_, :], rsC[:, tg_:tg_ + 1], fgbc, ALU.mult, ALU.mult)
                dma('sp', out_d[tg_ * 128:(tg_ + 1) * 128, :], ob)

            nxt_fws = None
            for hf in range(2):
                fws = nxt_fws if nxt_fws is not None else [load_fw(0)]
                nxt_fws = None
                wos = None
                for g in range(11):
                    if g + 1 < 11 and len(fws) < g + 2:
                        fws.append(load_fw(g + 1))
                    w = fws[g]
                    for fi in range(2):
                        f = 2 * g + fi
                        for tcx in range(2):
                            sl = slice(tcx * 512, (tcx + 1) * 512)
                            ba, bu = nb(), nb()
                            for kc in range(8):
                                mm(bank(ba), w[:, kc, fi * 128:(fi + 1) * 128], H2T[:, kc, sl], start=(kc == 0), stop=(kc == 7))
                            for kc in range(8):
                                mm(bank(bu), w[:, kc, 256 + fi * 128: 256 + (fi + 1) * 128], H2T[:, kc, sl],
                                   start=(kc == 0), stop=(kc == 7))
                            th = THF[(f * 2 + tcx) % 2]
                            act(th, bank(ba), AF.Tanh, scale=0.5)
                            stt('dve', th, th, 1.0, bank(ba), ALU.add, ALU.mult)
                            stt('dve', ACTT[:, f, sl], th, 0.5, bank(bu), ALU.mult, ALU.mult)
                    if g == 8:
                        wos = [load_wo(0)]
                nbg = []
                if hf == 0:
                    for t in range(9):
                        nbg.append((lambda t_: (lambda: ((norm2_a(1, t_) if t_ < 8 else None),
                                                         (norm2_b(1, t_ - 1) if t_ >= 1 else None))))(t))
                unit = 0
                for nq in range(4):
                    if nq + 1 < 4:
                        wos.append(load_wo(nq + 1))
                    if nq == 1 and hf == 0:
                        nxt_fws = [load_fw(0), load_fw(1)]
                    w = wos[nq]
                    for t in range(8):
                        tg_ = hf * 8 + t
                        b = nb()
                        for f in range(NF):
                            mm(bank(b)[:, 0:256], ACTT[:, f, t * 128:(t + 1) * 128], w[:, f, :],
                               start=(f == 0), stop=(f == NF - 1))
                        wt = THF[t % 2][:, 0:256]
                        tt('dve', wt, bank(b)[:, 0:256], gt2bc[:, nq * 256:(nq + 1) * 256], ALU.mult)
                        tt('dve', X1[:, tg_, nq * 256:(nq + 1) * 256], X1[:, tg_, nq * 256:(nq + 1) * 256], wt, ALU.add)
                        if nq == 3:
                            final_norm(tg_)
                        unit += 1
                        if unit % 2 == 1 and nbg:
                            nbg.pop(0)()
                while nbg:
                    nbg.pop(0)()

        try:
            record()
        except _Stop:
            pass

        P.finalize()

        @block.tensor
        def _(e):
            P.emit('pe', e, esems, dsems)

        @block.scalar
        def _(e):
            P.emit('act', e, esems, dsems)

        @block.vector
        def _(e):
            P.emit('dve', e, esems, dsems)

        @block.gpsimd
        def _(e):
            P.emit('pool', e, esems, dsems)

        @block.sync
        def _(e):
            P.emit('sp', e, esems, dsems)

    return nc


_NC_CACHE = {}


def kernel(x, c, w_ada, b_ada, norm1_g, norm2_g, final_g, w_in,
           lambda_q1, lambda_k1, lambda_q2, lambda_k2, rel_bias, attn_sub_g,
           w_o_attn, conv_w, conv_b, conv_ln_g, conv_ln_b, w_o_conv, b_o_conv,
           w_out, w_ffn_in, w_ffn_out):
    f = lambda a: np.ascontiguousarray(np.asarray(a, dtype=np.float32))
    x = f(x)
    c = f(c)
    if 'nc' not in _NC_CACHE:
        _NC_CACHE['nc'] = build_nc()
    nc = _NC_CACHE['nc']
    cf, oh, sel = _host_consts()
    lam4 = np.concatenate([f(lambda_q1)[0], f(lambda_k1)[0], f(lambda_q2)[0], f(lambda_k2)[0]])[None, :]
    shared = {
        "w_ada": f(w_ada)[0], "b_ada": f(b_ada)[0].reshape(6, D),
        "norm1_g": f(norm1_g).reshape(1, D), "norm2_g": f(norm2_g).reshape(1, D),
        "final_g": f(final_g).reshape(1, D), "w_in": f(w_in)[0],
        "lam4": np.ascontiguousarray(lam4), "rel_bias": f(rel_bias),
        "attn_sub_g": f(attn_sub_g).reshape(1, 128), "w_o_attn": f(w_o_attn)[0],
        "conv_w": f(conv_w)[0], "conv_b": f(conv_b).reshape(1, 512),
        "conv_ln_g": f(conv_ln_g).reshape(1, 512), "conv_ln_b": f(conv_ln_b).reshape(1, 512),
        "w_o_conv": f(w_o_conv)[0], "b_o_conv": f(b_o_conv).reshape(1, D),
        "w_out": f(w_out)[0], "w_ffn_in": f(w_ffn_in)[0], "w_ffn_out": f(w_ffn_out)[0],
        "cst_f": cf, "cst_oh": oh, "cst_sel": sel,
    }
    in_maps = []
    for b in range(8):
        m = dict(shared)
        m["x"] = x[b]
        m["cT"] = np.ascontiguousarray(c[b].reshape(128, 8))
        in_maps.append(m)
    res = run_bass_kernel_spmd(nc, in_maps, core_ids=list(range(8)))
    return np.stack([np.asarray(r["out"], dtype=np.float32) for r in res.results], axis=0)
```

```python
import math
from bisect import bisect_right
from contextlib import ExitStack

import numpy as np
import concourse.bass as bass
import concourse.mybir as mybir
from concourse.bass_utils import run_bass_kernel_spmd

F32 = mybir.dt.float32
BF16 = mybir.dt.bfloat16
U8 = mybir.dt.uint8
AF = mybir.ActivationFunctionType
ALU = mybir.AluOpType
AX = mybir.AxisListType

S = 2048
D = 1024
NT = 16
NH = 4
DFF = 2816
NF = 22
IN_COLS = 4608
EPS = 1e-6
ARENA = 207 * 1024
KB = 1024
NDSEM = 12
import os
EVAC_SPLIT = int(os.environ.get('EVAC_SPLIT', '4'))
SP0_INTER = int(os.environ.get('SP0_INTER', '1'))
TGOFF = int(os.environ.get('TGOFF', '16'))


def _esize(dt):
    return mybir.dt.size(dt)


class IMap:
    def __init__(self, size):
        self.b = [0, size]
        self.w = [None]
        self.r = [dict()]

    def _split(self, x):
        i = bisect_right(self.b, x) - 1
        if self.b[i] == x:
            return i
        self.b.insert(i + 1, x)
        self.w.insert(i + 1, self.w[i])
        self.r.insert(i + 1, dict(self.r[i]))
        return i + 1

    def access(self, lo, hi, op, write, deps):
        i = self._split(lo)
        j = self._split(hi)
        for s in range(i, j):
            wr = self.w[s]
            if wr is not None and wr is not op:
                deps[wr] = 'raw'
            if write:
                for q in self.r[s].values():
                    if q is not op and q not in deps:
                        deps[q] = 'war'
                self.w[s] = op
                self.r[s] = {}
            else:
                key = ('d', id(op)) if op.dma else op.eng
                self.r[s][key] = op


class Op:
    __slots__ = ('eng', 'fn', 'deps', 'sig', 'sigidx', 'dma', 'dslot', 'dval')

    def __init__(self, eng, fn, dma):
        self.eng = eng
        self.fn = fn
        self.dma = dma
        self.sig = False
        self.sigidx = 0
        self.deps = {}
        self.dslot = 0
        self.dval = 0


class Prog:
    ENGS = ['pe', 'act', 'dve', 'pool', 'sp']

    def __init__(self):
        self.ops = {e: [] for e in self.ENGS}
        self.maps = {'mem': IMap(ARENA), 'ps': IMap(16 * KB), 'scr': IMap(1 << 20)}

    def _regs0(self, ap):
        name = ap.tensor.name
        if name not in self.maps:
            return None, None
        es = _esize(ap.dtype)
        dims = list(ap.ap)
        if name == 'scr':
            off = ap.offset
            fd = dims
        else:
            pstep = (ARENA if name == 'mem' else 16 * KB) // es
            off = ap.offset % pstep
            fd = dims[1:]
        fd = [(s_, n_) for (s_, n_) in fd if n_ > 1]
        if not fd:
            return name, [(off * es, (off + 1) * es)]
        if fd[-1][0] == 1:
            run = fd[-1][1]
            outer = fd[:-1]
        else:
            run = 1
            outer = fd
        cnt = 1
        for _, n_ in outer:
            cnt *= n_
        if cnt > 64 or any(s_ < 0 for s_, _ in outer):
            lo = off
            hi = off + sum(s_ * (n_ - 1) for s_, n_ in fd) + 1
            return name, [(lo * es, hi * es)]
        starts = [off]
        for s_, n_ in outer:
            starts = [a + s_ * i for a in starts for i in range(n_)]
        return name, [(a * es, (a + run) * es) for a in starts]

    def _regs(self, ap):
        name, ivs = self._regs0(ap)
        if name == 'ps':
            banks = sorted(set(b for lo, hi in ivs for b in range(lo // 2048, (hi - 1) // 2048 + 1)))
            ivs = [(b * 2048, (b + 1) * 2048) for b in banks]
        return name, ivs

    def add(self, eng, fn, r=(), w=(), dma=False):
        op = Op(eng, fn, dma)
        deps = {}
        for ap in r:
            name, ivs = self._regs(ap)
            if name:
                m = self.maps[name]
                for lo, hi in ivs:
                    m.access(lo, hi, op, False, deps)
        for ap in w:
            name, ivs = self._regs(ap)
            if name:
                m = self.maps[name]
                for lo, hi in ivs:
                    m.access(lo, hi, op, True, deps)
        fdeps = []
        for d, kind in deps.items():
            if (not d.dma) and (not dma) and d.eng == eng:
                if eng == 'pe':
                    continue
            fdeps.append(d)
        op.deps = fdeps
        self.ops[eng].append(op)
        return op

    def finalize(self):
        for e in self.ENGS:
            for op in self.ops[e]:
                for d in op.deps:
                    if not d.dma:
                        d.sig = True
        for e in self.ENGS:
            n = 0
            k = 0
            for op in self.ops[e]:
                if op.dma:
                    op.dslot = k % NDSEM
                    op.dval = 16 * (k // NDSEM + 1)
                    k += 1
                elif op.sig:
                    n += 1
                    op.sigidx = n

    def emit(self, eng, e, esems, dsems):
        waited = {}
        last = {}
        for op in self.ops[eng]:
            if op.dma and op.dval > 16:
                key = ('d', eng, op.dslot)
                if waited.get(key, 0) < op.dval - 16:
                    e.wait_ge(dsems[eng][op.dslot], op.dval - 16)
                    waited[key] = op.dval - 16
            for d in op.deps:
                if d.dma:
                    key = ('d', d.eng, d.dslot)
                    val = d.dval
                    sem = dsems[d.eng][d.dslot]
                else:
                    key = d.eng
                    val = d.sigidx
                    sem = esems[d.eng]
                if waited.get(key, 0) < val:
                    e.wait_ge(sem, val)
                    waited[key] = val
            inst = op.fn(e)
            if op.dma:
                inst.then_inc(dsems[eng][op.dslot], 16)
                last[op.dslot] = op.dval
            elif op.sig:
                inst.then_inc(esems[eng], 1)
        for slot, val in last.items():
            e.wait_ge(dsems[eng][slot], val)


def _host_consts():
    cf = np.zeros((128, 256), np.float32)
    cf[np.arange(128), np.arange(128)] = 1.0
    cf[np.arange(128), 128 + 127 - np.arange(128)] = 1.0
    oh = np.zeros((33, 383), np.float32)
    for dd in range(383):
        delta = dd - 127
        if delta < 0:
            oh[32, dd] = 1.0
        else:
            n = delta
            if n < 16:
                bkt = n
            else:
                v = np.float32(np.log(np.float32(n) / np.float32(16.0)))
                v = np.float32(v / np.float32(math.log(128 / 16)))
                v = np.float32(v * np.float32(16.0))
                bkt = min(16 + int(v), 31)
            oh[bkt, dd] = 1.0
    sel = np.zeros((4, 4, 128), np.float32)
    for j in range(4):
        sel[j, j, :] = 1.0
    return cf, oh, sel.reshape(4, 512)


class _Stop(Exception):
    pass


def build_nc(stop=0):
    nc = bass.Bass("TRN2", target_bir_lowering=False)
    P = Prog()

    def ck(n):
        if stop == n:
            raise _Stop()

    def din(name, shape):
        return nc.dram_tensor(name, list(shape), F32, kind="ExternalInput").ap()

    x_d = din("x", [S, D])
    c_d = din("cT", [128, 8])
    wada_d = din("w_ada", [D, 6 * D])
    bada_d = din("b_ada", [6, D])
    n1g_d = din("norm1_g", [1, D])
    n2g_d = din("norm2_g", [1, D])
    fg_d = din("final_g", [1, D])
    win_d = din("w_in", [D, IN_COLS])
    lam_d = din("lam4", [1, 256])
    rb_d = din("rel_bias", [32, 4])
    asg_d = din("attn_sub_g", [1, 128])
    woa_d = din("w_o_attn", [512, D])
    cw_d = din("conv_w", [31, 512])
    cb_d = din("conv_b", [1, 512])
    lg_d = din("conv_ln_g", [1, 512])
    lb_d = din("conv_ln_b", [1, 512])
    woc_d = din("w_o_conv", [512, D])
    boc_d = din("b_o_conv", [1, D])
    wout_d = din("w_out", [D, D])
    wfi_d = din("w_ffn_in", [D, 2 * DFF])
    wfo_d = din("w_ffn_out", [DFF, D])
    cf_d = din("cst_f", [128, 256])
    oh_d = din("cst_oh", [33, 383])
    sel_d = din("cst_sel", [4, 512])
    out_d = nc.dram_tensor("out", [S, D], F32, kind="ExternalOutput").ap()
    scr_d = nc.dram_tensor("scr", [4, 384], F32, kind="Internal").ap()

    wada3 = wada_d.rearrange("(p kc) n -> p kc n", kc=8)
    win3 = win_d.rearrange("(kc p) n -> p kc n", p=128)
    wout3 = wout_d.rearrange("(kc p) n -> p kc n", p=128)
    woa3 = woa_d.rearrange("(c p) n -> p c n", p=128)
    woc3 = woc_d.rearrange("(c p) n -> p c n", p=128)
    wfi3 = wfi_d.rearrange("(kc p) n -> p kc n", p=128)
    wfo3 = wfo_d.rearrange("(f p) n -> p f n", p=128)

    with ExitStack() as es:
        mem = es.enter_context(nc.sbuf_tensor("mem", [128, ARENA], U8))
        ps = es.enter_context(nc.psum_tensor("ps", [128, 4096], F32))
        esems = {e: es.enter_context(nc.semaphore("s_" + e)) for e in ['pe', 'act', 'dve', 'pool']}
        dsems = {q: [es.enter_context(nc.semaphore("d_%s%d" % (q, i))) for i in range(NDSEM)]
                 for q in ['sp', 'pool']}
        block = es.enter_context(nc.Block())

        def V(off, dt, *shape, parts=128):
            n = 1
            for s_ in shape:
                n *= s_
            ap = mem[0:parts, off:off + n * _esize(dt)].bitcast(dt)
            if len(shape) == 2:
                ap = ap.rearrange("p (a b) -> p a b", b=shape[1])
            elif len(shape) == 3:
                ap = ap.rearrange("p (a b c) -> p a b c", b=shape[1], c=shape[2])
            return ap

        def bank(b):
            return ps[:, b * 512:(b + 1) * 512]

        def bank_bf(b):
            return ps[:, b * 512:(b + 1) * 512].bitcast(BF16)

        def mm(out, lhsT, rhs, start=True, stop=True, skip=False):
            if skip:
                return P.add('pe', lambda e: e.matmul(out, lhsT=lhsT, rhs=rhs, start=start, stop=stop,
                                                      skip_group_check=True), r=[lhsT, rhs], w=[out])
            return P.add('pe', lambda e: e.matmul(out, lhsT=lhsT, rhs=rhs, start=start, stop=stop),
                         r=[lhsT, rhs], w=[out])

        def tr(out, in_, ident):
            return P.add('pe', lambda e: e.transpose(out=out, in_=in_, identity=ident),
                         r=[in_, ident], w=[out])

        def act(out, in_, func, bias=None, scale=None, accum=None):
            rr = [in_]
            kw = {}
            if accum is not None:
                kw['accum_out'] = accum
            if bias is not None:
                kw['bias'] = bias
                if not isinstance(bias, float):
                    rr.append(bias)
            if scale is not None:
                kw['scale'] = scale
                if not isinstance(scale, float):
                    rr.append(scale)
            return P.add('act', lambda e: e.activation(out=out, in_=in_, func=func, **kw), r=rr,
                         w=[out] + ([accum] if accum is not None else []))

        def ts(eng, out, in0, s1, s2, op0, op1=None):
            rr = [in0] + [s_ for s_ in (s1, s2) if s_ is not None and not isinstance(s_, float)]
            if op1 is None:
                return P.add(eng, lambda e: e.tensor_scalar(out=out, in0=in0, scalar1=s1, scalar2=None, op0=op0),
                             r=rr, w=[out])
            return P.add(eng, lambda e: e.tensor_scalar(out=out, in0=in0, scalar1=s1, scalar2=s2, op0=op0, op1=op1),
                         r=rr, w=[out])

        def stt(eng, out, in0, sc, in1, op0, op1):
            rr = [in0, in1] + ([] if isinstance(sc, float) else [sc])
            return P.add(eng, lambda e: e.scalar_tensor_tensor(out=out, in0=in0, scalar=sc, in1=in1, op0=op0, op1=op1),
                         r=rr, w=[out])

        def tt(eng, out, in0, in1, op):
            return P.add(eng, lambda e: e.tensor_tensor(out=out, in0=in0, in1=in1, op=op), r=[in0, in1], w=[out])

        def cp(eng, out, in_):
            if eng == 'act':
                return P.add('act', lambda e: e.copy(out=out, in_=in_), r=[in_], w=[out])
            return P.add(eng, lambda e: e.tensor_copy(out=out, in_=in_), r=[in_], w=[out])

        def memset(eng, out, val):
            return P.add(eng, lambda e: e.memset(out, val), r=[], w=[out])

        def rsum(out, in_):
            return P.add('dve', lambda e: e.reduce_sum(out=out, in_=in_, axis=AX.X), r=[in_], w=[out])

        def dma(q, out, in_):
            return P.add(q, lambda e: e.dma_start(out=out, in_=in_), r=[in_], w=[out], dma=True)

        o = 0

        def alloc(nbytes):
            nonlocal o
            a = o
            o += (nbytes + 31) // 32 * 32
            return a

        c_cf = V(alloc(1024), F32, 256)
        ident_f = c_cf[:, 0:128]
        J_f = c_cf[:, 128:256]
        ones_f = V(alloc(512), F32, 128)
        neghalf = V(alloc(2048), F32, 512)
        ident_b = V(alloc(256), BF16, 128)
        sel_sb = V(alloc(2048), F32, 4, 128, parts=4)
        cT = V(alloc(32), F32, 8)
        cact = V(alloc(32), F32, 8)
        csig = V(alloc(32), F32, 8)
        L1 = V(alloc(8 * 2 * 2 * 2), BF16, 8, 2, 2)
        L2 = V(alloc(8 * 4 * 4 * 2), BF16, 8, 4, 4)
        C1 = V(alloc(64), F32, 8, 2)
        C2 = V(alloc(128), F32, 8, 4)
        C3 = V(alloc(128), F32, 8, 4)
        A1 = V(alloc(32), F32, 8)
        A2 = V(alloc(32), F32, 8)
        S1t = V(alloc(32), F32, 8)
        S2t = V(alloc(32), F32, 8)
        boc = V(alloc(32), F32, 8)
        CW = V(alloc(4 * 34 * 4), F32, 4, 34)
        lams = V(alloc(32), F32, 8, parts=1)
        neglam = V(alloc(32), F32, 1)
        rbaug = V(alloc(32), F32, 4, parts=33)
        chrow = V(alloc(32), F32, 4, parts=1)
        chcol = V(alloc(32), F32, 4)
        ch4 = V(alloc(32), F32, 1, parts=4)
        EB = V(alloc(4 * 256 * 2), BF16, 4, 256)
        gsubbc = V(alloc(512), F32, 128)
        gcol = V(alloc(32), F32, 1)
        gt1bc = V(alloc(4096), F32, 1024)
        gt2bc = V(alloc(4096), F32, 1024)
        fgbc = V(alloc(4096), F32, 1024)
        THF = [V(alloc(2048), F32, 512) for _ in range(2)]
        ssA = V(alloc(64), F32, 16)
        tvA = V(alloc(64), F32, 16)
        rsA = V(alloc(64), F32, 16)
        ssB = V(alloc(64), F32, 16)
        tvB = V(alloc(64), F32, 16)
        rsB = V(alloc(64), F32, 16)
        ssC = V(alloc(64), F32, 16)
        tvC = V(alloc(64), F32, 16)
        rsC = V(alloc(64), F32, 16)
        assert o <= 28 * KB, o
        o = 28 * KB
        o_HT = alloc(32 * KB)
        o_AN = alloc(16 * KB)
        o_UR = alloc(2 * 8320)
        o_PT = alloc(8 * KB)
        o_WIN = alloc(16 * KB)
        o_TMP = alloc(20 * KB)
        o_QK = alloc(16 * KB)
        o_V = alloc(16 * 4 * 130 * 2)
        o_ACC = alloc(32 * KB)
        assert o <= ARENA, o
        HT = V(o_HT, BF16, 8, S)
        ANv = V(o_AN, BF16, 4, S)
        XR = [V(o_AN + i * 4096, F32, 1024) for i in range(3)]
        UP = [V(o_UR + i * 4160, BF16, 2080) for i in range(2)]
        DG = V(o_UR + 8320, BF16, 31, 128)
        U2 = V(o_UR, BF16, 4, S)
        PT = [V(o_PT + i * 1024, BF16, 512) for i in range(8)]
        WIN = [V(o_WIN + i * 8192, BF16, 8, 512) for i in range(2)]
        WGA = [V(o_WIN + i * 2048, BF16, 8, 128) for i in range(4)]
        WGC = [V(o_WIN + 8 * KB + i * 2048, BF16, 8, 128) for i in range(4)]
        QT = [V(o_QK + i * 8192, BF16, S) for i in range(2)]
        QZ = [[QT[0], V(o_PT + 4 * KB, BF16, S)], [QT[1], V(o_TMP + 10752, BF16, S)]]
        KT = [V(o_QK + i * 8192 + 4096, BF16, S) for i in range(2)]
        VA = V(o_V, BF16, 16, 4, 130)
        YT = V(o_QK, BF16, 8, S)
        ACC = V(o_ACC, F32, 4, S)
        WOA = V(o_PT, BF16, 4, D)
        WOC = V(o_ACC + 8 * KB, BF16, 4, D)
        WOUT = V(o_ACC + 16 * KB, BF16, 8, D)
        B1 = V(o_ACC, F32, 1024, parts=2)
        R1 = V(o_ACC + 4 * KB, F32, 1024, parts=2)
        R3 = V(o_ACC + 8 * KB, F32, 1024, parts=4)
        crow = V(o_ACC + 12 * KB, F32, 512, parts=34)
        oh_sb = V(o_ACC + 14 * KB, F32, 383, parts=33)
        B2 = V(o_AN + 8 * KB, F32, 1024, parts=4)
        R2 = V(o_AN + 12 * KB, F32, 1024, parts=4)
        lamr = V(o_TMP + 14 * KB + 512, F32, 256, parts=1)
        t4 = V(o_AN + 13 * KB, F32, 384, parts=4)
        HK = [V(o_PT + i * 1024, F32, 256) for i in range(4)]
        brow = V(o_AN + 12 * KB, F32, 256, parts=1)
        lamt = V(o_TMP + 15 * KB + 512, F32, 64, parts=1)
        WA = [V(o_ACC + 16 * KB, BF16, 8, 512), V(o_ACC + 24 * KB, BF16, 8, 512),
              V(o_QK, BF16, 8, 512), V(o_QK + 8 * KB, BF16, 8, 512)]
        X1 = V(o_HT, F32, 16, D)
        assert o_HT + 64 * KB <= o_PT
        o_H2T = o_PT
        o_FW = o_H2T + 16 * KB
        o_XSF = o_FW + 16 * KB
        o_JNK = o_XSF + 4 * KB
        o_ACTT = o_JNK + 4 * KB
        o_WO = o_ACTT + 44 * KB
        o_OUTS = o_WO + 2 * 11264
        assert o_OUTS + 8 * KB <= ARENA, (o_OUTS, ARENA)
        assert o_ACTT == o_TMP + 16 * KB, (o_ACTT, o_TMP)
        H2T = V(o_H2T, BF16, 8, 1024)
        FW = [V(o_FW + i * 8192, BF16, 8, 512) for i in range(2)]
        XSF = [V(o_XSF + i * 2048, BF16, 1024) for i in range(2)]
        JNK = V(o_JNK, F32, 1024)
        ACTT = V(o_ACTT, BF16, NF, 1024)
        WO = [V(o_WO + i * 11264, BF16, NF, 256) for i in range(2)]
        OUTS = [V(o_OUTS + i * 4096, F32, 1024) for i in range(2)]

        def TMPV(off, dt, *shape):
            return V(o_TMP + off, dt, *shape)

        def record():
            dma('sp', c_cf, cf_d)
            dma('sp', cT, c_d)
            dma('sp', B1, bada_d[0:2, :])
            dma('sp', R3[0:1, :], n1g_d)
            memset('pool', ones_f, 1.0)
            memset('pool', neghalf, -0.5)
            memset('pool', L1, 0.0)
            memset('pool', L2, 0.0)
            cp('dve', ident_b, ident_f)
            ck(0.1)
            act(csig, cT, AF.Tanh, scale=0.5)
            ts('dve', csig, csig, 0.5, 0.5, ALU.mult, ALU.add)
            tt('dve', cact, cT, csig, ALU.mult)
            for kc in range(8):
                for j in range(2):
                    cp('dve', L1[:, kc, j, j:j + 1], cact[:, kc:kc + 1])
                for j in range(4):
                    cp('dve', L2[:, kc, j, j:j + 1], cact[:, kc:kc + 1])

            ck(0.2)
            bctr = [0]

            nbanks = [8]
            nblo = [0]

            def nb():
                b = nblo[0] + bctr[0] % nbanks[0]
                bctr[0] += 1
                return b

            trctr = [0]

            def nb_tr():
                if nblo[0] == 0:
                    return nb()
                b = trctr[0] % nblo[0]
                trctr[0] += 1
                return b

            def norm_a(xsrc, junk, xs, ssv, tvv, rsv, col):
                act(junk, xsrc, AF.Square, accum=ssv[:, col:col + 1])
                ts('dve', tvv[:, col:col + 1], ssv[:, col:col + 1], 1.0 / D, EPS, ALU.mult, ALU.add)
                tt('pool', rsv[:, col:col + 1], tvv[:, col:col + 1], neghalf[:, 0:1], ALU.pow)
                ts('dve', xs, xsrc, rsv[:, col:col + 1], None, ALU.mult)

            def norm_b_tr(xs):
                b = nb_tr()
                pb = bank_bf(b).rearrange("p (a b) -> p a b", b=128)
                for kc in range(8):
                    tr(pb[:, kc, :], xs[:, kc * 128:(kc + 1) * 128], ident_b)
                return pb

            def norm_b_ev(pb, Acol, Scol, dstT, tcol):
                for kc in range(8):
                    dst = dstT[:, kc, tcol * 128:(tcol + 1) * 128]
                    if tcol % 2 == 0:
                        act(dst, pb[:, kc, :], AF.Identity, bias=Scol[:, kc:kc + 1], scale=Acol[:, kc:kc + 1])
                    else:
                        ts('dve', dst, pb[:, kc, :], Acol[:, kc:kc + 1], Scol[:, kc:kc + 1], ALU.mult, ALU.add)

            def norm_b(xs, Acol, Scol, dstT, tcol):
                norm_b_ev(norm_b_tr(xs), Acol, Scol, dstT, tcol)

            junk1 = [TMPV(i * 4096, F32, 1024) for i in range(2)]
            xs1 = [TMPV(8192 + i * 2048, BF16, 1024) for i in range(3)]

            def phase1_a(t):
                xr = XR[t % 3]
                dma('sp', xr, x_d[t * 128:(t + 1) * 128, :])
                norm_a(xr, junk1[t % 2], xs1[t % 3], ssA, tvA, rsA, t)

            LEAD = 2
            pbs = {}

            def PRE_HOOK():
                for t_ in range(LEAD):
                    phase1_a(t_)

            adac = [0]

            def ada_part(jlist, Ltile, Brow, Rrow, nrows):
                bks = [nb(), nb()]
                first = [True, True]
                total = len(jlist) * 8
                cnt = [0, 0]
                for jj, j in enumerate(jlist):
                    for half in range(2):
                        wa = WA[adac[0] % 4]
                        adac[0] += 1
                        dma('pool', wa, wada3[:, :, j * 1024 + half * 512: j * 1024 + (half + 1) * 512])
                        for kc in range(8):
                            cnt[half] += 1
                            mm(bank(bks[half])[0:nrows, :], Ltile[:, kc, jj, :], wa[:, kc, :],
                               start=first[half], stop=(cnt[half] == total))
                            first[half] = False
                for half in range(2):
                    tt('dve', Rrow[:, half * 512:(half + 1) * 512], bank(bks[half])[0:nrows, :],
                       Brow[:, half * 512:(half + 1) * 512], ALU.add)

            PRE_HOOK()
            ada_part([0, 1], L1, B1, R1, 2)
            ck(0.3)

            def rows_to_cols(Rrow, nrows, Cout):
                b = nb()
                for kc in range(8):
                    mm(bank(b)[:, kc * 32:(kc + 1) * 32], Rrow[:, kc * 128:(kc + 1) * 128],
                       ident_f[0:nrows, 0:32])
                cp('dve', Cout, bank(b)[:, 0:256].rearrange("p (a b) -> p a b", b=32)[:, :, 0:nrows])

            rows_to_cols(R1, 2, C1)
            ck(1)
            dma('sp', R3[1:2, :], n2g_d)
            dma('sp', R3[2:3, :], boc_d)
            dma('sp', R3[3:4, :], fg_d)
            dma('sp', oh_sb, oh_d)
            dma('sp', sel_sb, sel_d.rearrange("j (a m) -> j a m", m=128))
            dma('sp', crow[0:31, :], cw_d)
            dma('sp', crow[31:32, :], cb_d)
            dma('sp', crow[32:33, :], lg_d)
            dma('sp', crow[33:34, :], lb_d)
            dma('sp', lamr, lam_d)
            dma('sp', rbaug[0:32, :], rb_d)
            memset('dve', brow, 0.0)
            dma('sp', brow[:, 128:132], rb_d[31:32, :])
            ch4src = bass.AP(rb_d.tensor, 124, [[1, 4], [1, 1]])
            P.add('sp', lambda e: e.dma_start(out=ch4, in_=ch4src), r=[], w=[ch4], dma=True)
            dma('sp', brow[:, 0:128], asg_d)
            gsrc = bass.AP(asg_d.tensor, 0, [[1, 128], [1, 1]])
            P.add('sp', lambda e: e.dma_start(out=gcol, in_=gsrc), r=[], w=[gcol], dma=True)
            ts('dve', gcol, gcol, 0.8, None, ALU.mult)
            ck(1.1)
            rows_to_cols(R3, 4, C3)
            ck(1.15)
            for half in range(2):
                b = nb()
                mm(bank(b), sel_sb[:, 3, :], R3[:, half * 512:(half + 1) * 512])
                cp('dve', fgbc[:, half * 512:(half + 1) * 512], bank(b))
            ck(1.2)
            stt('dve', A1, C1[:, :, 1], 1.0, C3[:, :, 0], ALU.add, ALU.mult)
            S1 = C1[:, :, 0]

            cp('dve', S1t, S1)
            ck(1.3)
            def load_stage_w(st):
                w = WIN[st % 2]
                for i, c0 in enumerate([st * 128, 512 + st * 128, 1536 + st * 128, 2048 + st * 128]):
                    dma('pool', w[:, :, i * 128:(i + 1) * 128], win3[:, :, c0:c0 + 128])

            def stage_proj_chunk(st, tcx):
                w = WIN[st % 2]
                qt, kt, up = QT[st % 2], KT[st % 2], UP[st % 2]
                if tcx == 0:
                    memset('dve', QZ[st % 2][0][64:128, :], 0.0)
                    memset('dve', QZ[st % 2][1][0:64, :], 0.0)
                for which, dstT in ((0, qt), (1, kt)):
                    b = nb()
                    for kc in range(8):
                        mm(bank(b), w[:, kc, which * 128:(which + 1) * 128], HT[:, kc, tcx * 512:(tcx + 1) * 512],
                           start=(kc == 0), stop=(kc == 7))
                    if which == 1:
                        cp('dve', dstT[:, tcx * 512:(tcx + 1) * 512], bank(b))
                    else:
                        cp('dve', QZ[st % 2][0][0:64, tcx * 512:(tcx + 1) * 512], bank(b)[0:64, :])
                        cp('dve', QZ[st % 2][1][64:128, tcx * 512:(tcx + 1) * 512], bank(b)[64:128, :])
                bl, bg = nb(), nb()
                for kc in range(8):
                    mm(bank(bl), w[:, kc, 256:384], HT[:, kc, tcx * 512:(tcx + 1) * 512], start=(kc == 0), stop=(kc == 7))
                for kc in range(8):
                    mm(bank(bg), w[:, kc, 384:512], HT[:, kc, tcx * 512:(tcx + 1) * 512], start=(kc == 0), stop=(kc == 7))
                tg = TMPV(TGOFF * KB + (tcx % 2) * 2048, F32, 512)
                act(tg, bank(bg), AF.Tanh, scale=0.5)
                stt('dve', up[:, 32 + tcx * 512: 32 + (tcx + 1) * 512], tg, 1.0, bank(bl), ALU.add, ALU.mult)

            def stage_proj(st):
                bctr[0] = 6 if nbanks[0] == 7 else bctr[0]
                for tcx in range(4):
                    stage_proj_chunk(st, tcx)

            def vproj(t):
                b = nb()
                for kc in range(8):
                    mm(bank(b), HT[:, kc, t * 128:(t + 1) * 128], WVb[:, kc, :], start=(kc == 0), stop=(kc == 7))
                cp('dve', VA[:, t, :, 0:128], bank(b).rearrange("p (h v) -> p h v", v=128))

            WVb = V(o_QK + 8 * KB, BF16, 8, 512)
            dma('pool', WVb, win3[:, :, 1024:1536])
            load_stage_w(0)
            load_stage_w(1)
            def conv_cols():
                b = nb()
                for c in range(4):
                    mm(bank(b)[:, c * 64:(c + 1) * 64], crow[:, c * 128:(c + 1) * 128], ident_f[0:34, 0:64])
                cwp = bank(b)[:, 0:256].rearrange("p (a b) -> p a b", b=64)
                ts('dve', CW[:, :, 0:31], cwp[:, :, 0:31], 0.5, None, ALU.mult)
                cp('dve', CW[:, :, 31:32], cwp[:, :, 31:32])
                ts('dve', CW[:, :, 32:34], cwp[:, :, 32:34], 0.5, None, ALU.mult)

            ck(4)

            def misc_consts():
                tt('dve', lamt, lamr[:, 0:64], lamr[:, 64:128], ALU.mult)
                rsum(lams[:, 0:1], lamt)
                tt('dve', lamt, lamr[:, 128:192], lamr[:, 192:256], ALU.mult)
                rsum(lams[:, 1:2], lamt)
                act(lams[:, 2:4], lams[:, 0:2], AF.Exp)
                tt('dve', lams[:, 4:5], lams[:, 3:4], lams[:, 2:3], ALU.subtract)
                ts('dve', brow[:, 132:133], lams[:, 4:5], -0.2, None, ALU.add)
                b = nb()
                mm(bank(b)[:, 0:256], ones_f[0:1, :], brow)
                ts('dve', gsubbc, bank(b)[:, 0:128], 0.8, None, ALU.mult)
                cp('dve', chcol, bank(b)[:, 128:132])
                cp('dve', neglam, bank(b)[:, 132:133])
                memset('dve', rbaug[32:33, :], -30000.0)
                b2 = nb()
                mm(bank(b2)[0:4, 0:383], rbaug, oh_sb)
                memset('dve', t4, 0.0)
                ts('dve', t4[:, 0:383], bank(b2)[0:4, 0:383], ch4[:, 0:1], None, ALU.subtract)
                dma('sp', scr_d, t4)
                for h in range(NH):
                    src = bass.AP(scr_d.tensor, h * 384, [[1, 128], [1, 256]])
                    P.add('sp', (lambda s_, o_: (lambda e: e.dma_start(out=o_, in_=s_)))(src, HK[h]), r=[scr_d], w=[HK[h]], dma=True)

            def misc_consts_b():
                for h in range(NH):
                    b3 = nb()
                    mm(bank(b3)[:, 0:256], J_f, HK[h])
                    act(EB[:, h, :], bank(b3)[:, 0:256], AF.Exp)

            memset('dve', VA[:, :, :, 128:130], 1.0)
            memset('dve', UP[0][:, 0:32], 0.0)
            memset('dve', UP[1][:, 0:32], 0.0)
            for t in range(LEAD, NT + LEAD):
                if t < NT:
                    phase1_a(t)
                tb = t - LEAD
                norm_b(xs1[tb % 3], A1, S1t, HT, tb)
                if tb >= 1:
                    vproj(tb - 1)
                if tb == 3:
                    conv_cols()
                    misc_consts()
                if SP0_INTER and tb >= 4 and tb % 4 == 0:
                    stage_proj_chunk(0, tb // 4 - 1)
            vproj(NT - 1)
            if SP0_INTER:
                stage_proj_chunk(0, 3)
            else:
                for tcx_ in range(4):
                    stage_proj_chunk(0, tcx_)
            ck(2)

            ck(3)
            SR = [0, 1, 6]
            sctr = [0]

            def sb():
                b_ = SR[sctr[0] % 3]
                sctr[0] += 1
                return b_

            def ada2_closures():
                cl = []
                chunks = [(jj, j, half) for jj, j in enumerate([2, 3, 4, 5]) for half in range(2)]

                def issue(i):
                    jj, j, half = chunks[i]
                    dma('pool', WA[i % 2], wada3[:, :, j * 1024 + half * 512: j * 1024 + (half + 1) * 512])

                def first():
                    dma('sp', B2, bada_d[2:6, :])
                    cp('dve', R2, B2)
                    issue(0)
                    issue(1)
                cl.append(first)
                for i in range(8):
                    def chunk(i=i):
                        jj, j, half = chunks[i]
                        wa = WA[i % 2]
                        bx = sb()
                        for kc in range(8):
                            mm(bank(bx)[0:4, :], L2[:, kc, jj, :], wa[:, kc, :], start=(kc == 0), stop=(kc == 7))
                        tt('dve', R2[:, half * 512:(half + 1) * 512], R2[:, half * 512:(half + 1) * 512],
                           bank(bx)[0:4, :], ALU.add)
                        if i + 2 < 8:
                            issue(i + 2)
                    cl.append(chunk)

                def fin():
                    b = sb()
                    for kc in range(8):
                        mm(bank(b)[:, kc * 32:(kc + 1) * 32], R2[:, kc * 128:(kc + 1) * 128], ident_f[0:4, 0:32])
                    cp('dve', C2, bank(b)[:, 0:256].rearrange("p (a b) -> p a b", b=32)[:, :, 0:4])
                    stt('dve', A2, C2[:, :, 2], 1.0, C3[:, :, 1], ALU.add, ALU.mult)
                    cp('dve', S2t, C2[:, :, 1])
                cl.append(fin)
                for (dst, j, scl) in [(gt1bc, 0, 0.5), (gt2bc, 3, 1.0)]:
                    for half in range(2):
                        def selc(dst=dst, j=j, scl=scl, half=half):
                            bx = sb()
                            mm(bank(bx), sel_sb[:, j, :], R2[:, half * 512:(half + 1) * 512])
                            ts('dve', dst[:, half * 512:(half + 1) * 512], bank(bx), scl, None, ALU.mult)
                        cl.append(selc)
                return cl

            misc_consts_b()
            ck(5)


            pbg = []

            NPE = 28

            def build_diag(st):
                for j in range(NPE):
                    ts('dve', DG[:, j, :], ident_f, CW[:, st, j:j + 1], None, ALU.mult)

            def queue_conv(st):
                up = UP[st % 2]
                for tcx in range(4):
                    for j in range(NPE):
                        pbg.append((lambda j_, t_: (lambda: mm(bank(7), DG[:, j_, :],
                                                               up[:, 2 + j_ + t_ * 512: 2 + j_ + (t_ + 1) * 512],
                                                               start=(j_ == 0), stop=(j_ == NPE - 1))))(j, tcx))
                    pbg.append((lambda t_: (lambda: ts('dve', ACC[:, st, t_ * 512:(t_ + 1) * 512], bank(7),
                                                       CW[:, st, 31:32], None, ALU.add)))(tcx))
                    for j in range(NPE, 31):
                        pbg.append((lambda j_, t_: (lambda: stt('dve', ACC[:, st, t_ * 512:(t_ + 1) * 512],
                                                                up[:, 2 + j_ + t_ * 512: 2 + j_ + (t_ + 1) * 512],
                                                                CW[:, st, j_:j_ + 1],
                                                                ACC[:, st, t_ * 512:(t_ + 1) * 512],
                                                                ALU.mult, ALU.add)))(j, tcx))

            def pump(n):
                for _ in range(n):
                    if pbg:
                        pbg.pop(0)()

            SK = 3
            deferred = []
            gstep = [0]

            def attention(h):
                qt, kt = QT[h % 2], KT[h % 2]
                obanks = [[2, 3], [4, 5]]
                steps = []
                for qc in range(4):
                    for j in range(2):
                        for kb in range(4 * qc + 4):
                            steps.append((qc, j, kb))
                pts = {}
                firsts = {}

                def front(idx):
                    qc, j, kb = steps[idx]
                    i = kb - 4 * qc
                    c0 = max(i, 0) * 128
                    sps = bank(sb())
                    mm(sps[:, c0:512], kt[:, kb * 128:(kb + 1) * 128],
                       QZ[h % 2][j][:, qc * 512 + c0:(qc + 1) * 512])
                    pt = PT[idx % 4]
                    pts[idx] = pt
                    act(pt[:, c0:512], sps[:, c0:512], AF.Exp, bias=chcol[:, h:h + 1], scale=0.125)
                    if i >= 0:
                        if i < 3:
                            tt('dve', pt[:, i * 128:(i + 2) * 128], pt[:, i * 128:(i + 2) * 128], EB[:, h, :], ALU.mult)
                        else:
                            tt('dve', pt[:, 384:512], pt[:, 384:512], EB[:, h, 0:128], ALU.mult)
                    elif i == -1:
                        tt('dve', pt[:, 0:128], pt[:, 0:128], EB[:, h, 128:256], ALU.mult)

                def back(idx):
                    qc, j, kb = steps[idx]
                    i = kb - 4 * qc
                    pt = pts.pop(idx)
                    if kb == 0:
                        firsts[(qc, j)] = [True, True]
                    first = firsts[(qc, j)]
                    for s_ in range(max(i, 0), 4):
                        ob = obanks[j][s_ // 2]
                        last = (kb == 4 * qc + s_)
                        mm(bank(ob)[:, (s_ % 2) * 256:(s_ % 2) * 256 + 129], pt[:, s_ * 128:(s_ + 1) * 128],
                           VA[:, kb, h, 0:129], start=first[s_ // 2], stop=last, skip=True)
                        first[s_ // 2] = False
                    if kb == 4 * qc + 3:
                        epilogue(qc, j)

                def epilogue(qc, j):
                    o1n = TMPV((qc % 2) * 2048, F32, 4, 128)
                    rr = TMPV(10 * KB + (qc % 2) * 64, F32, 8)
                    oreg = ps[:, obanks[j][0] * 512:(obanks[j][0] + 2) * 512].rearrange("p (s c) -> p s c", c=256)
                    P.add('dve', (lambda o_, i_: (lambda e: e.reciprocal(out=o_, in_=i_)))(rr[:, j * 4:(j + 1) * 4], oreg[:, :, 128]),
                          r=[oreg[:, :, 128]], w=[rr[:, j * 4:(j + 1) * 4]])
                    if j == 0:
                        for s_ in range(4):
                            ts('dve', o1n[:, s_, :], oreg[:, s_, 0:128], rr[:, s_:s_ + 1], None, ALU.mult)
                    else:
                        ts('dve', rr[:, 4:8], rr[:, 4:8], neglam[:, 0:1], None, ALU.mult)
                        dd = TMPV(4 * KB, F32, 4, 128)
                        for s_ in range(4):
                            stt('dve', dd[:, s_, :], oreg[:, s_, 0:128], rr[:, 4 + s_:5 + s_], o1n[:, s_, :], ALU.mult, ALU.add)
                        dsq = TMPV(6 * KB, F32, 4, 128)
                        tt('dve', dsq, dd, dd, ALU.mult)
                        ssq = TMPV(10 * KB + 128, F32, 4)
                        tvq = TMPV(10 * KB + 160, F32, 4)
                        rsq = TMPV(10 * KB + 192 + (qc % 2) * 32, F32, 4)
                        rsum(ssq, dsq)
                        ts('dve', tvq, ssq, 1.0 / 128, EPS, ALU.mult, ALU.add)
                        ant = TMPV(8 * KB + (qc % 2) * 1024, BF16, 4, 128)

                        def mid(ant=ant, tvq=tvq, rsq=rsq, dd=dd):
                            act(tvq, tvq, AF.Ln)
                            act(rsq, tvq, AF.Exp, scale=-0.5)
                            for s_ in range(4):
                                ts('dve', ant[:, s_, :], dd[:, s_, :], rsq[:, s_:s_ + 1], None, ALU.mult)
                        deferred.append([gstep[0] + 4, mid])

                        def tail(ant=ant, qc=qc):
                            bt = sb()
                            pb = bank_bf(bt).rearrange("p (a b) -> p a b", b=128)
                            for s_ in range(4):
                                tr(pb[:, s_, :], ant[:, s_, :], ident_b)
                            cp('dve', ANv[:, h, qc * 512:(qc + 1) * 512], bank_bf(bt)[:, 0:512])
                        deferred.append([gstep[0] + 10, tail])

                n = len(steps)
                for idx in range(n + SK):
                    gstep[0] += 1
                    while deferred and deferred[0][0] <= gstep[0]:
                        deferred.pop(0)[1]()
                    if idx < n:
                        front(idx)
                    if idx - SK >= 0:
                        back(idx - SK)
                    pump(1)

            nbanks[0] = 7
            ck(6)
            for h in range(NH):
                build_diag(h)
                queue_conv(h)
                if h == 0:
                    for i_, c_ in enumerate(ada2_closures()):
                        pbg.insert(min(len(pbg), 10 + i_ * 13), c_)
                if h + 1 < NH:
                    stage_proj(h + 1)
                    if h + 2 < NH:
                        load_stage_w(h + 2)
                attention(h)
                pump(len(pbg))
            while deferred:
                deferred.pop(0)[1]()
            nbanks[0] = 8
            ck(7)
            def load_wga(f):
                dma('pool', WGA[f % 4], win3[:, :, 2560 + f * 128: 2560 + (f + 1) * 128])

            W5B = [WGA[0], WGA[1], WGC[0], WGC[1], WGC[2], WGC[3], WGA[2], WGA[3]]

            def load_wgc(f):
                dma('pool', W5B[f], win3[:, :, 3584 + f * 128: 3584 + (f + 1) * 128])

            load_wga(0)
            load_wga(1)
            dma('pool', WOA, woa3)
            ts('dve', WOA, WOA, gcol[:, 0:1], None, ALU.mult)
            cp('dve', boc, C3[:, :, 2])

            def ln_parts(tcx):
                sl = slice(tcx * 512, (tcx + 1) * 512)
                mean = TMPV(4 * KB, F32, 512)
                msq = TMPV(6 * KB, F32, 512)
                rstd = TMPV(8 * KB, F32, 512)
                st_ = {}

                SQ = [V(o_WIN + 8 * KB + c * 2048, F32, 512) for c in range(4)]

                def p_sq():
                    for c in range(4):
                        act(SQ[c], ACC[:, c, sl], AF.Square)

                def p_stats():
                    bs, bq = nb(), nb()
                    st_['bs'], st_['bq'] = bs, bq
                    for c in range(4):
                        mm(bank(bs), ones_f, ACC[:, c, sl], start=(c == 0), stop=(c == 3))
                    for c in range(4):
                        mm(bank(bq), ones_f, SQ[c], start=(c == 0), stop=(c == 3))

                def p_var():
                    ts('dve', mean, bank(st_['bs']), 1.0 / 512, None, ALU.mult)
                    tt('dve', msq, mean, mean, ALU.mult)
                    stt('dve', msq, bank(st_['bq']), 1.0 / 512, msq, ALU.mult, ALU.subtract)
                    ts('dve', msq, msq, EPS, None, ALU.add)
                    act(rstd, msq, AF.Ln)
                    act(rstd, rstd, AF.Exp, scale=-0.5)

                def p_c(c):
                    def f_():
                        xn = TMPV(10 * KB + (c % 2) * 2048, F32, 512)
                        th = TMPV(14 * KB + (c % 2) * 2048, F32, 512)
                        zp = TMPV(0, F32, 512)
                        tt('dve', xn, ACC[:, c, sl], mean, ALU.subtract)
                        tt('dve', xn, xn, rstd, ALU.mult)
                        act(th, xn, AF.Tanh, bias=CW[:, c, 33:34], scale=CW[:, c, 32:33])
                        act(zp, xn, AF.Identity, bias=CW[:, c, 33:34], scale=CW[:, c, 32:33])
                        stt('dve', U2[:, c, sl], th, 1.0, zp, ALU.add, ALU.mult)
                    return f_
                return [p_sq, p_stats, p_var, p_c(0), p_c(1), p_c(2), p_c(3)]

            lnq = []
            parts_ = [ln_parts(tcx) for tcx in range(4)]
            lnq.append(parts_[0][0])
            for tcx in range(4):
                p = parts_[tcx]
                lnq += [p[1], p[2], p[3]]
                if tcx + 1 < 4:
                    lnq.append(parts_[tcx + 1][0])
                lnq += [p[4], p[5], p[6]]

            ck(8)
            ta_ring = [TMPV(2 * KB, F32, 512), TMPV(18 * KB, F32, 512)]
            it = 0
            for f in range(8):
                if f + 2 < 8:
                    load_wga(f + 2)
                if f == 6:
                    load_wgc(0)
                    load_wgc(1)
                w = WGA[f % 4]
                for tcx in range(4):
                    sl = slice(tcx * 512, (tcx + 1) * 512)
                    ba, bo = nb(), nb()
                    for kc in range(8):
                        mm(bank(ba), w[:, kc, :], HT[:, kc, sl], start=(kc == 0), stop=(kc == 7))
                    for c in range(4):
                        mm(bank(bo), WOA[:, c, f * 128:(f + 1) * 128], ANv[:, c, sl], start=(c == 0), stop=(c == 3))
                    ta = ta_ring[it % 2]
                    act(ta, bank(ba), AF.Tanh, scale=0.5)
                    stt('dve', YT[:, f, sl], ta, 1.0, bank(bo), ALU.add, ALU.mult)
                    it += 1
                    if it >= 2 and lnq:
                        lnq.pop(0)()

            while lnq:
                lnq.pop(0)()
            dma('pool', WOC, woc3)
            for f_ in range(2, 8):
                load_wgc(f_)
            dma('pool', WOUT, wout3)
            for f in range(8):
                w = W5B[f]
                pre_b = {}
                if f == 0:
                    for tcx in range(4):
                        sl = slice(tcx * 512, (tcx + 1) * 512)
                        pre_b[tcx] = nb()
                        for kc in range(8):
                            mm(bank(pre_b[tcx]), w[:, kc, :], HT[:, kc, sl], start=(kc == 0), stop=(kc == 7))
                for tcx in range(4):
                    sl = slice(tcx * 512, (tcx + 1) * 512)
                    if f == 0:
                        bc_, bv = pre_b[tcx], nb()
                    else:
                        bc_, bv = nb(), nb()
                        for kc in range(8):
                            mm(bank(bc_), w[:, kc, :], HT[:, kc, sl], start=(kc == 0), stop=(kc == 7))
                    for c in range(4):
                        mm(bank(bv), WOC[:, c, f * 128:(f + 1) * 128], U2[:, c, sl], start=(c == 0), stop=(c == 3))
                    k2 = (f * 4 + tcx) % 2
                    tg = TMPV(4 * KB + k2 * 2048, F32, 512)
                    cvb = TMPV(8 * KB + k2 * 2048, F32, 512)
                    act(tg, bank(bc_), AF.Tanh, scale=0.5)
                    act(cvb, bank(bv), AF.Identity, bias=boc[:, f:f + 1], scale=1.0)
                    stt('dve', tg, tg, 1.0, cvb, ALU.add, ALU.mult)
                    tt('dve', YT[:, f, sl], YT[:, f, sl], tg, ALU.add)
                if f == 3:
                    for kc in range(8):
                        tt('dve', WOUT[:, kc, :], WOUT[:, kc, :], gt1bc, ALU.mult)

            ck(9)
            S2f = S2t

            def norm2_a(hf, t):
                norm_a(X1[:, hf * 8 + t, :], JNK, XSF[t % 2], ssB, tvB, rsB, hf * 8 + t)

            def norm2_b(hf, t):
                norm_b(XSF[t % 2], A2, S2f, H2T, t)

            wtmp = [TMPV(16 * KB + i * 2048, F32, 512) for i in range(2)]
            for t in range(NT):
                dma('sp', X1[:, t, :], x_d[t * 128:(t + 1) * 128, :])
                for n in range(2):
                    b = nb()
                    for kc in range(8):
                        mm(bank(b), YT[:, kc, t * 128:(t + 1) * 128], WOUT[:, kc, n * 512:(n + 1) * 512],
                           start=(kc == 0), stop=(kc == 7))
                    tt('dve', X1[:, t, n * 512:(n + 1) * 512], X1[:, t, n * 512:(n + 1) * 512], bank(b), ALU.add)
                if t < 8:
                    norm2_a(0, t)
                if 1 <= t <= 8:
                    norm2_b(0, t - 1)

            ck(10)
            fwc = [0]

            def load_fw(g):
                w = FW[fwc[0] % 2]
                fwc[0] += 1
                dma('pool', w[:, :, 0:256], wfi3[:, :, g * 256:(g + 1) * 256])
                dma('pool', w[:, :, 256:512], wfi3[:, :, DFF + g * 256: DFF + (g + 1) * 256])
                return w

            woc_ = [0]

            def load_wo(nq):
                w = WO[woc_[0] % 2]
                woc_[0] += 1
                dma('pool', w, wfo3[:, :, nq * 256:(nq + 1) * 256])
                return w

            def final_norm(tg_):
                ob = OUTS[tg_ % 2]
                act(ob, X1[:, tg_, :], AF.Square, accum=ssC[:, tg_:tg_ + 1])
                ts('dve', tvC[:, tg_:tg_ + 1], ssC[:, tg_:tg_ + 1], 1.0 / D, EPS, ALU.mult, ALU.add)
                tt('pool', rsC[:, tg_:tg_ + 1], tvC[:, tg_:tg_ + 1], neghalf[:, 0:1], ALU.pow)
                stt('dve', ob, X1[:, tg_, :], rsC[:, tg_:tg_ + 1], fgbc, ALU.mult, ALU.mult)
                dma('sp', out_d[tg_ * 128:(tg_ + 1) * 128, :], ob)

            nxt_fws = None
            for hf in range(2):
                fws = nxt_fws if nxt_fws is not None else [load_fw(0)]
                nxt_fws = None
                wos = None
                for g in range(11):
                    if g + 1 < 11 and len(fws) < g + 2:
                        fws.append(load_fw(g + 1))
                    w = fws[g]
                    for fi in range(2):
                        f = 2 * g + fi
                        for tcx in range(2):
                            sl = slice(tcx * 512, (tcx + 1) * 512)
                            ba, bu = nb(), nb()
                            for kc in range(8):
                                mm(bank(ba), w[:, kc, fi * 128:(fi + 1) * 128], H2T[:, kc, sl], start=(kc == 0), stop=(kc == 7))
                            for kc in range(8):
                                mm(bank(bu), w[:, kc, 256 + fi * 128: 256 + (fi + 1) * 128], H2T[:, kc, sl],
                                   start=(kc == 0), stop=(kc == 7))
                            th = THF[(f * 2 + tcx) % 2]
                            act(th, bank(ba), AF.Tanh, scale=0.5)
                            stt('dve', th, th, 1.0, bank(ba), ALU.add, ALU.mult)
                            stt('dve', ACTT[:, f, sl], th, 0.5, bank(bu), ALU.mult, ALU.mult)
                    if g == 8:
                        wos = [load_wo(0)]
                nbg = []
                if hf == 0:
                    for t in range(9):
                        nbg.append((lambda t_: (lambda: ((norm2_a(1, t_) if t_ < 8 else None),
                                                         (norm2_b(1, t_ - 1) if t_ >= 1 else None))))(t))
                unit = 0
                for nq in range(4):
                    if nq + 1 < 4:
                        wos.append(load_wo(nq + 1))
                    if nq == 1 and hf == 0:
                        nxt_fws = [load_fw(0), load_fw(1)]
                    w = wos[nq]
                    for t in range(8):
                        tg_ = hf * 8 + t
                        b = nb()
                        for f in range(NF):
                            mm(bank(b)[:, 0:256], ACTT[:, f, t * 128:(t + 1) * 128], w[:, f, :],
                               start=(f == 0), stop=(f == NF - 1))
                        wt = THF[t % 2][:, 0:256]
                        tt('dve', wt, bank(b)[:, 0:256], gt2bc[:, nq * 256:(nq + 1) * 256], ALU.mult)
                        tt('dve', X1[:, tg_, nq * 256:(nq + 1) * 256], X1[:, tg_, nq * 256:(nq + 1) * 256], wt, ALU.add)
                        if nq == 3:
                            final_norm(tg_)
                        unit += 1
                        if unit % 2 == 1 and nbg:
                            nbg.pop(0)()
                while nbg:
                    nbg.pop(0)()

        try:
            record()
        except _Stop:
            pass

        P.finalize()

        @block.tensor
        def _(e):
            P.emit('pe', e, esems, dsems)

        @block.scalar
        def _(e):
            P.emit('act', e, esems, dsems)

        @block.vector
        def _(e):
            P.emit('dve', e, esems, dsems)

        @block.gpsimd
        def _(e):
            P.emit('pool', e, esems, dsems)

        @block.sync
        def _(e):
            P.emit('sp', e, esems, dsems)

    return nc


_NC_CACHE = {}


def kernel(x, c, w_ada, b_ada, norm1_g, norm2_g, final_g, w_in,
           lambda_q1, lambda_k1, lambda_q2, lambda_k2, rel_bias, attn_sub_g,
           w_o_attn, conv_w, conv_b, conv_ln_g, conv_ln_b, w_o_conv, b_o_conv,
           w_out, w_ffn_in, w_ffn_out):
    f = lambda a: np.ascontiguousarray(np.asarray(a, dtype=np.float32))
    x = f(x)
    c = f(c)
    if 'nc' not in _NC_CACHE:
        _NC_CACHE['nc'] = build_nc()
    nc = _NC_CACHE['nc']
    cf, oh, sel = _host_consts()
    lam4 = np.concatenate([f(lambda_q1)[0], f(lambda_k1)[0], f(lambda_q2)[0], f(lambda_k2)[0]])[None, :]
    shared = {
        "w_ada": f(w_ada)[0], "b_ada": f(b_ada)[0].reshape(6, D),
        "norm1_g": f(norm1_g).reshape(1, D), "norm2_g": f(norm2_g).reshape(1, D),
        "final_g": f(final_g).reshape(1, D), "w_in": f(w_in)[0],
        "lam4": np.ascontiguousarray(lam4), "rel_bias": f(rel_bias),
        "attn_sub_g": f(attn_sub_g).reshape(1, 128), "w_o_attn": f(w_o_attn)[0],
        "conv_w": f(conv_w)[0], "conv_b": f(conv_b).reshape(1, 512),
        "conv_ln_g": f(conv_ln_g).reshape(1, 512), "conv_ln_b": f(conv_ln_b).reshape(1, 512),
        "w_o_conv": f(w_o_conv)[0], "b_o_conv": f(b_o_conv).reshape(1, D),
        "w_out": f(w_out)[0], "w_ffn_in": f(w_ffn_in)[0], "w_ffn_out": f(w_ffn_out)[0],
        "cst_f": cf, "cst_oh": oh, "cst_sel": sel,
    }
    in_maps = []
    for b in range(8):
        m = dict(shared)
        m["x"] = x[b]
        m["cT"] = np.ascontiguousarray(c[b].reshape(128, 8))
        in_maps.append(m)
    res = run_bass_kernel_spmd(nc, in_maps, core_ids=list(range(8)))
    return np.stack([np.asarray(r["out"], dtype=np.float32) for r in res.results], axis=0)
```

```python
import math
from bisect import bisect_right
from contextlib import ExitStack

import numpy as np
import concourse.bass as bass
import concourse.mybir as mybir
from concourse.bass_utils import run_bass_kernel_spmd

F32 = mybir.dt.float32
BF16 = mybir.dt.bfloat16
U8 = mybir.dt.uint8
AF = mybir.ActivationFunctionType
ALU = mybir.AluOpType
AX = mybir.AxisListType

S = 2048
D = 1024
NT = 16
NH = 4
DFF = 2816
NF = 22
IN_COLS = 4608
EPS = 1e-6
ARENA = 207 * 1024
KB = 1024
NDSEM = 12
import os
EVAC_SPLIT = int(os.environ.get('EVAC_SPLIT', '4'))
SP0_INTER = int(os.environ.get('SP0_INTER', '1'))
TGOFF = int(os.environ.get('TGOFF', '16'))


def _esize(dt):
    return mybir.dt.size(dt)


class IMap:
    def __init__(self, size):
        self.b = [0, size]
        self.w = [None]
        self.r = [dict()]

    def _split(self, x):
        i = bisect_right(self.b, x) - 1
        if self.b[i] == x:
            return i
        self.b.insert(i + 1, x)
        self.w.insert(i + 1, self.w[i])
        self.r.insert(i + 1, dict(self.r[i]))
        return i + 1

    def access(self, lo, hi, op, write, deps):
        i = self._split(lo)
        j = self._split(hi)
        for s in range(i, j):
            wr = self.w[s]
            if wr is not None and wr is not op:
                deps[wr] = 'raw'
            if write:
                for q in self.r[s].values():
                    if q is not op and q not in deps:
                        deps[q] = 'war'
                self.w[s] = op
                self.r[s] = {}
            else:
                key = ('d', id(op)) if op.dma else op.eng
                self.r[s][key] = op


class Op:
    __slots__ = ('eng', 'fn', 'deps', 'sig', 'sigidx', 'dma', 'dslot', 'dval')

    def __init__(self, eng, fn, dma):
        self.eng = eng
        self.fn = fn
        self.dma = dma
        self.sig = False
        self.sigidx = 0
        self.deps = {}
        self.dslot = 0
        self.dval = 0


class Prog:
    ENGS = ['pe', 'act', 'dve', 'pool', 'sp']

    def __init__(self):
        self.ops = {e: [] for e in self.ENGS}
        self.maps = {'mem': IMap(ARENA), 'ps': IMap(16 * KB), 'scr': IMap(1 << 20)}

    def _regs0(self, ap):
        name = ap.tensor.name
        if name not in self.maps:
            return None, None
        es = _esize(ap.dtype)
        dims = list(ap.ap)
        if name == 'scr':
            off = ap.offset
            fd = dims
        else:
            pstep = (ARENA if name == 'mem' else 16 * KB) // es
            off = ap.offset % pstep
            fd = dims[1:]
        fd = [(s_, n_) for (s_, n_) in fd if n_ > 1]
        if not fd:
            return name, [(off * es, (off + 1) * es)]
        if fd[-1][0] == 1:
            run = fd[-1][1]
            outer = fd[:-1]
        else:
            run = 1
            outer = fd
        cnt = 1
        for _, n_ in outer:
            cnt *= n_
        if cnt > 64 or any(s_ < 0 for s_, _ in outer):
            lo = off
            hi = off + sum(s_ * (n_ - 1) for s_, n_ in fd) + 1
            return name, [(lo * es, hi * es)]
        starts = [off]
        for s_, n_ in outer:
            starts = [a + s_ * i for a in starts for i in range(n_)]
        return name, [(a * es, (a + run) * es) for a in starts]

    def _regs(self, ap):
        name, ivs = self._regs0(ap)
        if name == 'ps':
            banks = sorted(set(b for lo, hi in ivs for b in range(lo // 2048, (hi - 1) // 2048 + 1)))
            ivs = [(b * 2048, (b + 1) * 2048) for b in banks]
        return name, ivs

    def add(self, eng, fn, r=(), w=(), dma=False):
        op = Op(eng, fn, dma)
        deps = {}
        for ap in r:
            name, ivs = self._regs(ap)
            if name:
                m = self.maps[name]
                for lo, hi in ivs:
                    m.access(lo, hi, op, False, deps)
        for ap in w:
            name, ivs = self._regs(ap)
            if name:
                m = self.maps[name]
                for lo, hi in ivs:
                    m.access(lo, hi, op, True, deps)
        fdeps = []
        for d, kind in deps.items():
            if (not d.dma) and (not dma) and d.eng == eng:
                if eng == 'pe':
                    continue
            fdeps.append(d)
        op.deps = fdeps
        self.ops[eng].append(op)
        return op

    def finalize(self):
        for e in self.ENGS:
            for op in self.ops[e]:
                for d in op.deps:
                    if not d.dma:
                        d.sig = True
        for e in self.ENGS:
            n = 0
            k = 0
            for op in self.ops[e]:
                if op.dma:
                    op.dslot = k % NDSEM
                    op.dval = 16 * (k // NDSEM + 1)
                    k += 1
                elif op.sig:
                    n += 1
                    op.sigidx = n

    def emit(self, eng, e, esems, dsems):
        waited = {}
        last = {}
        for op in self.ops[eng]:
            if op.dma and op.dval > 16:
                key = ('d', eng, op.dslot)
                if waited.get(key, 0) < op.dval - 16:
                    e.wait_ge(dsems[eng][op.dslot], op.dval - 16)
                    waited[key] = op.dval - 16
            for d in op.deps:
                if d.dma:
                    key = ('d', d.eng, d.dslot)
                    val = d.dval
                    sem = dsems[d.eng][d.dslot]
                else:
                    key = d.eng
                    val = d.sigidx
                    sem = esems[d.eng]
                if waited.get(key, 0) < val:
                    e.wait_ge(sem, val)
                    waited[key] = val
            inst = op.fn(e)
            if op.dma:
                inst.then_inc(dsems[eng][op.dslot], 16)
                last[op.dslot] = op.dval
            elif op.sig:
                inst.then_inc(esems[eng], 1)
        for slot, val in last.items():
            e.wait_ge(dsems[eng][slot], val)


def _host_consts():
    cf = np.zeros((128, 256), np.float32)
    cf[np.arange(128), np.arange(128)] = 1.0
    cf[np.arange(128), 128 + 127 - np.arange(128)] = 1.0
    oh = np.zeros((33, 383), np.float32)
    for dd in range(383):
        delta = dd - 127
        if delta < 0:
            oh[32, dd] = 1.0
        else:
            n = delta
            if n < 16:
                bkt = n
            else:
                v = np.float32(np.log(np.float32(n) / np.float32(16.0)))
                v = np.float32(v / np.float32(math.log(128 / 16)))
                v = np.float32(v * np.float32(16.0))
                bkt = min(16 + int(v), 31)
            oh[bkt, dd] = 1.0
    sel = np.zeros((4, 4, 128), np.float32)
    for j in range(4):
        sel[j, j, :] = 1.0
    return cf, oh, sel.reshape(4, 512)


class _Stop(Exception):
    pass


def build_nc(stop=0):
    nc = bass.Bass("TRN2", target_bir_lowering=False)
    P = Prog()

    def ck(n):
        if stop == n:
            raise _Stop()

    def din(name, shape):
        return nc.dram_tensor(name, list(shape), F32, kind="ExternalInput").ap()

    x_d = din("x", [S, D])
    c_d = din("cT", [128, 8])
    wada_d = din("w_ada", [D, 6 * D])
    bada_d = din("b_ada", [6, D])
    n1g_d = din("norm1_g", [1, D])
    n2g_d = din("norm2_g", [1, D])
    fg_d = din("final_g", [1, D])
    win_d = din("w_in", [D, IN_COLS])
    lam_d = din("lam4", [1, 256])
    rb_d = din("rel_bias", [32, 4])
    asg_d = din("attn_sub_g", [1, 128])
    woa_d = din("w_o_attn", [512, D])
    cw_d = din("conv_w", [31, 512])
    cb_d = din("conv_b", [1, 512])
    lg_d = din("conv_ln_g", [1, 512])
    lb_d = din("conv_ln_b", [1, 512])
    woc_d = din("w_o_conv", [512, D])
    boc_d = din("b_o_conv", [1, D])
    wout_d = din("w_out", [D, D])
    wfi_d = din("w_ffn_in", [D, 2 * DFF])
    wfo_d = din("w_ffn_out", [DFF, D])
    cf_d = din("cst_f", [128, 256])
    oh_d = din("cst_oh", [33, 383])
    sel_d = din("cst_sel", [4, 512])
    out_d = nc.dram_tensor("out", [S, D], F32, kind="ExternalOutput").ap()
    scr_d = nc.dram_tensor("scr", [4, 384], F32, kind="Internal").ap()

    wada3 = wada_d.rearrange("(p kc) n -> p kc n", kc=8)
    win3 = win_d.rearrange("(kc p) n -> p kc n", p=128)
    wout3 = wout_d.rearrange("(kc p) n -> p kc n", p=128)
    woa3 = woa_d.rearrange("(c p) n -> p c n", p=128)
    woc3 = woc_d.rearrange("(c p) n -> p c n", p=128)
    wfi3 = wfi_d.rearrange("(kc p) n -> p kc n", p=128)
    wfo3 = wfo_d.rearrange("(f p) n -> p f n", p=128)

    with ExitStack() as es:
        mem = es.enter_context(nc.sbuf_tensor("mem", [128, ARENA], U8))
        ps = es.enter_context(nc.psum_tensor("ps", [128, 4096], F32))
        esems = {e: es.enter_context(nc.semaphore("s_" + e)) for e in ['pe', 'act', 'dve', 'pool']}
        dsems = {q: [es.enter_context(nc.semaphore("d_%s%d" % (q, i))) for i in range(NDSEM)]
                 for q in ['sp', 'pool']}
        block = es.enter_context(nc.Block())

        def V(off, dt, *shape, parts=128):
            n = 1
            for s_ in shape:
                n *= s_
            ap = mem[0:parts, off:off + n * _esize(dt)].bitcast(dt)
            if len(shape) == 2:
                ap = ap.rearrange("p (a b) -> p a b", b=shape[1])
            elif len(shape) == 3:
                ap = ap.rearrange("p (a b c) -> p a b c", b=shape[1], c=shape[2])
            return ap

        def bank(b):
            return ps[:, b * 512:(b + 1) * 512]

        def bank_bf(b):
            return ps[:, b * 512:(b + 1) * 512].bitcast(BF16)

        def mm(out, lhsT, rhs, start=True, stop=True, skip=False):
            if skip:
                return P.add('pe', lambda e: e.matmul(out, lhsT=lhsT, rhs=rhs, start=start, stop=stop,
                                                      skip_group_check=True), r=[lhsT, rhs], w=[out])
            return P.add('pe', lambda e: e.matmul(out, lhsT=lhsT, rhs=rhs, start=start, stop=stop),
                         r=[lhsT, rhs], w=[out])

        def tr(out, in_, ident):
            return P.add('pe', lambda e: e.transpose(out=out, in_=in_, identity=ident),
                         r=[in_, ident], w=[out])

        def act(out, in_, func, bias=None, scale=None, accum=None):
            rr = [in_]
            kw = {}
            if accum is not None:
                kw['accum_out'] = accum
            if bias is not None:
                kw['bias'] = bias
                if not isinstance(bias, float):
                    rr.append(bias)
            if scale is not None:
                kw['scale'] = scale
                if not isinstance(scale, float):
                    rr.append(scale)
            return P.add('act', lambda e: e.activation(out=out, in_=in_, func=func, **kw), r=rr,
                         w=[out] + ([accum] if accum is not None else []))

        def ts(eng, out, in0, s1, s2, op0, op1=None):
            rr = [in0] + [s_ for s_ in (s1, s2) if s_ is not None and not isinstance(s_, float)]
            if op1 is None:
                return P.add(eng, lambda e: e.tensor_scalar(out=out, in0=in0, scalar1=s1, scalar2=None, op0=op0),
                             r=rr, w=[out])
            return P.add(eng, lambda e: e.tensor_scalar(out=out, in0=in0, scalar1=s1, scalar2=s2, op0=op0, op1=op1),
                         r=rr, w=[out])

        def stt(eng, out, in0, sc, in1, op0, op1):
            rr = [in0, in1] + ([] if isinstance(sc, float) else [sc])
            return P.add(eng, lambda e: e.scalar_tensor_tensor(out=out, in0=in0, scalar=sc, in1=in1, op0=op0, op1=op1),
                         r=rr, w=[out])

        def tt(eng, out, in0, in1, op):
            return P.add(eng, lambda e: e.tensor_tensor(out=out, in0=in0, in1=in1, op=op), r=[in0, in1], w=[out])

        def cp(eng, out, in_):
            if eng == 'act':
                return P.add('act', lambda e: e.copy(out=out, in_=in_), r=[in_], w=[out])
            return P.add(eng, lambda e: e.tensor_copy(out=out, in_=in_), r=[in_], w=[out])

        def memset(eng, out, val):
            return P.add(eng, lambda e: e.memset(out, val), r=[], w=[out])

        def rsum(out, in_):
            return P.add('dve', lambda e: e.reduce_sum(out=out, in_=in_, axis=AX.X), r=[in_], w=[out])

        def dma(q, out, in_):
            return P.add(q, lambda e: e.dma_start(out=out, in_=in_), r=[in_], w=[out], dma=True)

        o = 0

        def alloc(nbytes):
            nonlocal o
            a = o
            o += (nbytes + 31) // 32 * 32
            return a

        c_cf = V(alloc(1024), F32, 256)
        ident_f = c_cf[:, 0:128]
        J_f = c_cf[:, 128:256]
        ones_f = V(alloc(512), F32, 128)
        neghalf = V(alloc(2048), F32, 512)
        ident_b = V(alloc(256), BF16, 128)
        sel_sb = V(alloc(2048), F32, 4, 128, parts=4)
        cT = V(alloc(32), F32, 8)
        cact = V(alloc(32), F32, 8)
        csig = V(alloc(32), F32, 8)
        L1 = V(alloc(8 * 2 * 2 * 2), BF16, 8, 2, 2)
        L2 = V(alloc(8 * 4 * 4 * 2), BF16, 8, 4, 4)
        C1 = V(alloc(64), F32, 8, 2)
        C2 = V(alloc(128), F32, 8, 4)
        C3 = V(alloc(128), F32, 8, 4)
        A1 = V(alloc(32), F32, 8)
        A2 = V(alloc(32), F32, 8)
        S1t = V(alloc(32), F32, 8)
        S2t = V(alloc(32), F32, 8)
        boc = V(alloc(32), F32, 8)
        CW = V(alloc(4 * 34 * 4), F32, 4, 34)
        lams = V(alloc(32), F32, 8, parts=1)
        neglam = V(alloc(32), F32, 1)
        rbaug = V(alloc(32), F32, 4, parts=33)
        chrow = V(alloc(32), F32, 4, parts=1)
        chcol = V(alloc(32), F32, 4)
        ch4 = V(alloc(32), F32, 1, parts=4)
        EB = V(alloc(4 * 256 * 2), BF16, 4, 256)
        gsubbc = V(alloc(512), F32, 128)
        gcol = V(alloc(32), F32, 1)
        gt1bc = V(alloc(4096), F32, 1024)
        gt2bc = V(alloc(4096), F32, 1024)
        fgbc = V(alloc(4096), F32, 1024)
        THF = [V(alloc(2048), F32, 512) for _ in range(2)]
        ssA = V(alloc(64), F32, 16)
        tvA = V(alloc(64), F32, 16)
        rsA = V(alloc(64), F32, 16)
        ssB = V(alloc(64), F32, 16)
        tvB = V(alloc(64), F32, 16)
        rsB = V(alloc(64), F32, 16)
        ssC = V(alloc(64), F32, 16)
        tvC = V(alloc(64), F32, 16)
        rsC = V(alloc(64), F32, 16)
        assert o <= 28 * KB, o
        o = 28 * KB
        o_HT = alloc(32 * KB)
        o_AN = alloc(16 * KB)
        o_UR = alloc(2 * 8320)
        o_PT = alloc(8 * KB)
        o_WIN = alloc(16 * KB)
        o_TMP = alloc(20 * KB)
        o_QK = alloc(16 * KB)
        o_V = alloc(16 * 4 * 130 * 2)
        o_ACC = alloc(32 * KB)
        assert o <= ARENA, o
        HT = V(o_HT, BF16, 8, S)
        ANv = V(o_AN, BF16, 4, S)
        XR = [V(o_AN + i * 4096, F32, 1024) for i in range(3)]
        UP = [V(o_UR + i * 4160, BF16, 2080) for i in range(2)]
        DG = V(o_UR + 8320, BF16, 31, 128)
        U2 = V(o_UR, BF16, 4, S)
        PT = [V(o_PT + i * 1024, BF16, 512) for i in range(8)]
        WIN = [V(o_WIN + i * 8192, BF16, 8, 512) for i in range(2)]
        WGA = [V(o_WIN + i * 2048, BF16, 8, 128) for i in range(4)]
        WGC = [V(o_WIN + 8 * KB + i * 2048, BF16, 8, 128) for i in range(4)]
        QT = [V(o_QK + i * 8192, BF16, S) for i in range(2)]
        QZ = [[QT[0], V(o_PT + 4 * KB, BF16, S)], [QT[1], V(o_TMP + 10752, BF16, S)]]
        KT = [V(o_QK + i * 8192 + 4096, BF16, S) for i in range(2)]
        VA = V(o_V, BF16, 16, 4, 130)
        YT = V(o_QK, BF16, 8, S)
        ACC = V(o_ACC, F32, 4, S)
        WOA = V(o_PT, BF16, 4, D)
        WOC = V(o_ACC + 8 * KB, BF16, 4, D)
        WOUT = V(o_ACC + 16 * KB, BF16, 8, D)
        B1 = V(o_ACC, F32, 1024, parts=2)
        R1 = V(o_ACC + 4 * KB, F32, 1024, parts=2)
        R3 = V(o_ACC + 8 * KB, F32, 1024, parts=4)
        crow = V(o_ACC + 12 * KB, F32, 512, parts=34)
        oh_sb = V(o_ACC + 14 * KB, F32, 383, parts=33)
        B2 = V(o_AN + 8 * KB, F32, 1024, parts=4)
        R2 = V(o_AN + 12 * KB, F32, 1024, parts=4)
        lamr = V(o_TMP + 14 * KB + 512, F32, 256, parts=1)
        t4 = V(o_AN + 13 * KB, F32, 384, parts=4)
        HK = [V(o_PT + i * 1024, F32, 256) for i in range(4)]
        brow = V(o_AN + 12 * KB, F32, 256, parts=1)
        lamt = V(o_TMP + 15 * KB + 512, F32, 64, parts=1)
        WA = [V(o_ACC + 16 * KB, BF16, 8, 512), V(o_ACC + 24 * KB, BF16, 8, 512),
              V(o_QK, BF16, 8, 512), V(o_QK + 8 * KB, BF16, 8, 512)]
        X1 = V(o_HT, F32, 16, D)
        assert o_HT + 64 * KB <= o_PT
        o_H2T = o_PT
        o_FW = o_H2T + 16 * KB
        o_XSF = o_FW + 16 * KB
        o_JNK = o_XSF + 4 * KB
        o_ACTT = o_JNK + 4 * KB
        o_WO = o_ACTT + 44 * KB
        o_OUTS = o_WO + 2 * 11264
        assert o_OUTS + 8 * KB <= ARENA, (o_OUTS, ARENA)
        assert o_ACTT == o_TMP + 16 * KB, (o_ACTT, o_TMP)
        H2T = V(o_H2T, BF16, 8, 1024)
        FW = [V(o_FW + i * 8192, BF16, 8, 512) for i in range(2)]
        XSF = [V(o_XSF + i * 2048, BF16, 1024) for i in range(3)]
        JNK = V(o_JNK + 2048, BF16, 1024)
        ACTT = V(o_ACTT, BF16, NF, 1024)
        WO = [V(o_WO + i * 11264, BF16, NF, 256) for i in range(2)]
        OUTS = [V(o_OUTS + i * 4096, F32, 1024) for i in range(2)]

        def TMPV(off, dt, *shape):
            return V(o_TMP + off, dt, *shape)

        def record():
            dma('sp', c_cf, cf_d)
            dma('sp', cT, c_d)
            dma('sp', B1, bada_d[0:2, :])
            dma('sp', R3[0:1, :], n1g_d)
            memset('pool', ones_f, 1.0)
            memset('pool', neghalf, -0.5)
            memset('pool', L1, 0.0)
            memset('pool', L2, 0.0)
            cp('dve', ident_b, ident_f)
            ck(0.1)
            act(csig, cT, AF.Tanh, scale=0.5)
            ts('dve', csig, csig, 0.5, 0.5, ALU.mult, ALU.add)
            tt('dve', cact, cT, csig, ALU.mult)
            for kc in range(8):
                for j in range(2):
                    cp('dve', L1[:, kc, j, j:j + 1], cact[:, kc:kc + 1])
                for j in range(4):
                    cp('dve', L2[:, kc, j, j:j + 1], cact[:, kc:kc + 1])

            ck(0.2)
            bctr = [0]

            nbanks = [8]
            nblo = [0]

            def nb():
                b = nblo[0] + bctr[0] % nbanks[0]
                bctr[0] += 1
                return b

            trctr = [0]

            def nb_tr():
                if nblo[0] == 0:
                    return nb()
                b = trctr[0] % nblo[0]
                trctr[0] += 1
                return b

            def norm_a(xsrc, junk, xs, ssv, tvv, rsv, col):
                act(junk, xsrc, AF.Square, accum=ssv[:, col:col + 1])
                ts('dve', tvv[:, col:col + 1], ssv[:, col:col + 1], 1.0 / D, EPS, ALU.mult, ALU.add)
                tt('pool', rsv[:, col:col + 1], tvv[:, col:col + 1], neghalf[:, 0:1], ALU.pow)
                ts('dve', xs, xsrc, rsv[:, col:col + 1], None, ALU.mult)

            def norm_b_tr(xs):
                b = nb_tr()
                pb = bank_bf(b).rearrange("p (a b) -> p a b", b=128)
                for kc in range(8):
                    tr(pb[:, kc, :], xs[:, kc * 128:(kc + 1) * 128], ident_b)
                return pb

            def norm_b_ev(pb, Acol, Scol, dstT, tcol):
                for kc in range(8):
                    dst = dstT[:, kc, tcol * 128:(tcol + 1) * 128]
                    if tcol % 2 == 0:
                        act(dst, pb[:, kc, :], AF.Identity, bias=Scol[:, kc:kc + 1], scale=Acol[:, kc:kc + 1])
                    else:
                        ts('dve', dst, pb[:, kc, :], Acol[:, kc:kc + 1], Scol[:, kc:kc + 1], ALU.mult, ALU.add)

            def norm_b(xs, Acol, Scol, dstT, tcol):
                norm_b_ev(norm_b_tr(xs), Acol, Scol, dstT, tcol)

            junk1 = [TMPV(i * 4096, F32, 1024) for i in range(2)]
            xs1 = [TMPV(8192 + i * 2048, BF16, 1024) for i in range(3)]

            def phase1_a(t):
                xr = XR[t % 3]
                dma('sp', xr, x_d[t * 128:(t + 1) * 128, :])
                norm_a(xr, junk1[t % 2], xs1[t % 3], ssA, tvA, rsA, t)

            LEAD = 2
            pbs = {}

            def PRE_HOOK():
                for t_ in range(LEAD):
                    phase1_a(t_)

            adac = [0]

            def ada_part(jlist, Ltile, Brow, Rrow, nrows):
                bks = [nb(), nb()]
                first = [True, True]
                total = len(jlist) * 8
                cnt = [0, 0]
                for jj, j in enumerate(jlist):
                    for half in range(2):
                        wa = WA[adac[0] % 4]
                        adac[0] += 1
                        dma('pool', wa, wada3[:, :, j * 1024 + half * 512: j * 1024 + (half + 1) * 512])
                        for kc in range(8):
                            cnt[half] += 1
                            mm(bank(bks[half])[0:nrows, :], Ltile[:, kc, jj, :], wa[:, kc, :],
                               start=first[half], stop=(cnt[half] == total))
                            first[half] = False
                for half in range(2):
                    tt('dve', Rrow[:, half * 512:(half + 1) * 512], bank(bks[half])[0:nrows, :],
                       Brow[:, half * 512:(half + 1) * 512], ALU.add)

            PRE_HOOK()
            ada_part([0, 1], L1, B1, R1, 2)
            ck(0.3)

            def rows_to_cols(Rrow, nrows, Cout):
                b = nb()
                for kc in range(8):
                    mm(bank(b)[:, kc * 32:(kc + 1) * 32], Rrow[:, kc * 128:(kc + 1) * 128],
                       ident_f[0:nrows, 0:32])
                cp('dve', Cout, bank(b)[:, 0:256].rearrange("p (a b) -> p a b", b=32)[:, :, 0:nrows])

            rows_to_cols(R1, 2, C1)
            ck(1)
            dma('sp', R3[1:2, :], n2g_d)
            dma('sp', R3[2:3, :], boc_d)
            dma('sp', R3[3:4, :], fg_d)
            dma('sp', oh_sb, oh_d)
            dma('sp', sel_sb, sel_d.rearrange("j (a m) -> j a m", m=128))
            dma('sp', crow[0:31, :], cw_d)
            dma('sp', crow[31:32, :], cb_d)
            dma('sp', crow[32:33, :], lg_d)
            dma('sp', crow[33:34, :], lb_d)
            dma('sp', lamr, lam_d)
            dma('sp', rbaug[0:32, :], rb_d)
            memset('dve', brow, 0.0)
            dma('sp', brow[:, 128:132], rb_d[31:32, :])
            ch4src = bass.AP(rb_d.tensor, 124, [[1, 4], [1, 1]])
            P.add('sp', lambda e: e.dma_start(out=ch4, in_=ch4src), r=[], w=[ch4], dma=True)
            dma('sp', brow[:, 0:128], asg_d)
            gsrc = bass.AP(asg_d.tensor, 0, [[1, 128], [1, 1]])
            P.add('sp', lambda e: e.dma_start(out=gcol, in_=gsrc), r=[], w=[gcol], dma=True)
            ts('dve', gcol, gcol, 0.8, None, ALU.mult)
            ck(1.1)
            rows_to_cols(R3, 4, C3)
            ck(1.15)
            for half in range(2):
                b = nb()
                mm(bank(b), sel_sb[:, 3, :], R3[:, half * 512:(half + 1) * 512])
                cp('dve', fgbc[:, half * 512:(half + 1) * 512], bank(b))
            ck(1.2)
            stt('dve', A1, C1[:, :, 1], 1.0, C3[:, :, 0], ALU.add, ALU.mult)
            S1 = C1[:, :, 0]

            cp('dve', S1t, S1)
            ck(1.3)
            def load_stage_w(st):
                w = WIN[st % 2]
                for i, c0 in enumerate([st * 128, 512 + st * 128, 1536 + st * 128, 2048 + st * 128]):
                    dma('pool', w[:, :, i * 128:(i + 1) * 128], win3[:, :, c0:c0 + 128])

            def stage_proj_chunk(st, tcx):
                w = WIN[st % 2]
                qt, kt, up = QT[st % 2], KT[st % 2], UP[st % 2]
                if tcx == 0:
                    memset('dve', QZ[st % 2][0][64:128, :], 0.0)
                    memset('dve', QZ[st % 2][1][0:64, :], 0.0)
                for which, dstT in ((0, qt), (1, kt)):
                    b = nb()
                    for kc in range(8):
                        mm(bank(b), w[:, kc, which * 128:(which + 1) * 128], HT[:, kc, tcx * 512:(tcx + 1) * 512],
                           start=(kc == 0), stop=(kc == 7))
                    if which == 1:
                        cp('dve', dstT[:, tcx * 512:(tcx + 1) * 512], bank(b))
                    else:
                        cp('dve', QZ[st % 2][0][0:64, tcx * 512:(tcx + 1) * 512], bank(b)[0:64, :])
                        cp('dve', QZ[st % 2][1][64:128, tcx * 512:(tcx + 1) * 512], bank(b)[64:128, :])
                bl, bg = nb(), nb()
                for kc in range(8):
                    mm(bank(bl), w[:, kc, 256:384], HT[:, kc, tcx * 512:(tcx + 1) * 512], start=(kc == 0), stop=(kc == 7))
                for kc in range(8):
                    mm(bank(bg), w[:, kc, 384:512], HT[:, kc, tcx * 512:(tcx + 1) * 512], start=(kc == 0), stop=(kc == 7))
                tg = TMPV(TGOFF * KB + (tcx % 2) * 2048, F32, 512)
                act(tg, bank(bg), AF.Tanh, scale=0.5)
                stt('dve', up[:, 32 + tcx * 512: 32 + (tcx + 1) * 512], tg, 1.0, bank(bl), ALU.add, ALU.mult)

            def stage_proj(st):
                bctr[0] = 6 if nbanks[0] == 7 else bctr[0]
                for tcx in range(4):
                    stage_proj_chunk(st, tcx)

            def vproj(t):
                b = nb()
                for kc in range(8):
                    mm(bank(b), HT[:, kc, t * 128:(t + 1) * 128], WVb[:, kc, :], start=(kc == 0), stop=(kc == 7))
                cp('dve', VA[:, t, :, 0:128], bank(b).rearrange("p (h v) -> p h v", v=128))

            WVb = V(o_QK + 8 * KB, BF16, 8, 512)
            dma('pool', WVb, win3[:, :, 1024:1536])
            load_stage_w(0)
            load_stage_w(1)
            def conv_cols():
                b = nb()
                for c in range(4):
                    mm(bank(b)[:, c * 64:(c + 1) * 64], crow[:, c * 128:(c + 1) * 128], ident_f[0:34, 0:64])
                cwp = bank(b)[:, 0:256].rearrange("p (a b) -> p a b", b=64)
                ts('dve', CW[:, :, 0:31], cwp[:, :, 0:31], 0.5, None, ALU.mult)
                cp('dve', CW[:, :, 31:32], cwp[:, :, 31:32])
                ts('dve', CW[:, :, 32:34], cwp[:, :, 32:34], 0.5, None, ALU.mult)

            ck(4)

            def misc_consts():
                tt('dve', lamt, lamr[:, 0:64], lamr[:, 64:128], ALU.mult)
                rsum(lams[:, 0:1], lamt)
                tt('dve', lamt, lamr[:, 128:192], lamr[:, 192:256], ALU.mult)
                rsum(lams[:, 1:2], lamt)
                act(lams[:, 2:4], lams[:, 0:2], AF.Exp)
                tt('dve', lams[:, 4:5], lams[:, 3:4], lams[:, 2:3], ALU.subtract)
                ts('dve', brow[:, 132:133], lams[:, 4:5], -0.2, None, ALU.add)
                b = nb()
                mm(bank(b)[:, 0:256], ones_f[0:1, :], brow)
                ts('dve', gsubbc, bank(b)[:, 0:128], 0.8, None, ALU.mult)
                cp('dve', chcol, bank(b)[:, 128:132])
                cp('dve', neglam, bank(b)[:, 132:133])
                memset('dve', rbaug[32:33, :], -30000.0)
                b2 = nb()
                mm(bank(b2)[0:4, 0:383], rbaug, oh_sb)
                memset('dve', t4, 0.0)
                ts('dve', t4[:, 0:383], bank(b2)[0:4, 0:383], ch4[:, 0:1], None, ALU.subtract)
                dma('sp', scr_d, t4)
                for h in range(NH):
                    src = bass.AP(scr_d.tensor, h * 384, [[1, 128], [1, 256]])
                    P.add('sp', (lambda s_, o_: (lambda e: e.dma_start(out=o_, in_=s_)))(src, HK[h]), r=[scr_d], w=[HK[h]], dma=True)

            def misc_consts_b():
                for h in range(NH):
                    b3 = nb()
                    mm(bank(b3)[:, 0:256], J_f, HK[h])
                    act(EB[:, h, :], bank(b3)[:, 0:256], AF.Exp)

            memset('dve', VA[:, :, :, 128:130], 1.0)
            memset('dve', UP[0][:, 0:32], 0.0)
            memset('dve', UP[1][:, 0:32], 0.0)
            for t in range(LEAD, NT + LEAD):
                if t < NT:
                    phase1_a(t)
                tb = t - LEAD
                norm_b(xs1[tb % 3], A1, S1t, HT, tb)
                if tb >= 1:
                    vproj(tb - 1)
                if tb == 3:
                    conv_cols()
                    misc_consts()
                if SP0_INTER and tb >= 4 and tb % 4 == 0:
                    stage_proj_chunk(0, tb // 4 - 1)
            vproj(NT - 1)
            if SP0_INTER:
                stage_proj_chunk(0, 3)
            else:
                for tcx_ in range(4):
                    stage_proj_chunk(0, tcx_)
            ck(2)

            ck(3)
            SR = [0, 1, 6]
            sctr = [0]

            def sb():
                b_ = SR[sctr[0] % 3]
                sctr[0] += 1
                return b_

            def ada2_closures():
                cl = []
                chunks = [(jj, j, half) for jj, j in enumerate([2, 3, 4, 5]) for half in range(2)]

                def issue(i):
                    jj, j, half = chunks[i]
                    dma('pool', WA[i % 2], wada3[:, :, j * 1024 + half * 512: j * 1024 + (half + 1) * 512])

                def first():
                    dma('sp', B2, bada_d[2:6, :])
                    cp('dve', R2, B2)
                    issue(0)
                    issue(1)
                cl.append(first)
                for i in range(8):
                    def chunk(i=i):
                        jj, j, half = chunks[i]
                        wa = WA[i % 2]
                        bx = sb()
                        for kc in range(8):
                            mm(bank(bx)[0:4, :], L2[:, kc, jj, :], wa[:, kc, :], start=(kc == 0), stop=(kc == 7))
                        tt('dve', R2[:, half * 512:(half + 1) * 512], R2[:, half * 512:(half + 1) * 512],
                           bank(bx)[0:4, :], ALU.add)
                        if i + 2 < 8:
                            issue(i + 2)
                    cl.append(chunk)

                def fin():
                    b = sb()
                    for kc in range(8):
                        mm(bank(b)[:, kc * 32:(kc + 1) * 32], R2[:, kc * 128:(kc + 1) * 128], ident_f[0:4, 0:32])
                    cp('dve', C2, bank(b)[:, 0:256].rearrange("p (a b) -> p a b", b=32)[:, :, 0:4])
                    stt('dve', A2, C2[:, :, 2], 1.0, C3[:, :, 1], ALU.add, ALU.mult)
                    cp('dve', S2t, C2[:, :, 1])
                cl.append(fin)
                for (dst, j, scl) in [(gt1bc, 0, 0.5), (gt2bc, 3, 1.0)]:
                    for half in range(2):
                        def selc(dst=dst, j=j, scl=scl, half=half):
                            bx = sb()
                            mm(bank(bx), sel_sb[:, j, :], R2[:, half * 512:(half + 1) * 512])
                            ts('dve', dst[:, half * 512:(half + 1) * 512], bank(bx), scl, None, ALU.mult)
                        cl.append(selc)
                return cl

            misc_consts_b()
            ck(5)


            pbg = []

            NPE = 28

            def build_diag(st):
                for j in range(NPE):
                    ts('dve', DG[:, j, :], ident_f, CW[:, st, j:j + 1], None, ALU.mult)

            def queue_conv(st):
                up = UP[st % 2]
                for tcx in range(4):
                    for j in range(NPE):
                        pbg.append((lambda j_, t_: (lambda: mm(bank(7), DG[:, j_, :],
                                                               up[:, 2 + j_ + t_ * 512: 2 + j_ + (t_ + 1) * 512],
                                                               start=(j_ == 0), stop=(j_ == NPE - 1))))(j, tcx))
                    pbg.append((lambda t_: (lambda: ts('dve', ACC[:, st, t_ * 512:(t_ + 1) * 512], bank(7),
                                                       CW[:, st, 31:32], None, ALU.add)))(tcx))
                    for j in range(NPE, 31):
                        pbg.append((lambda j_, t_: (lambda: stt('dve', ACC[:, st, t_ * 512:(t_ + 1) * 512],
                                                                up[:, 2 + j_ + t_ * 512: 2 + j_ + (t_ + 1) * 512],
                                                                CW[:, st, j_:j_ + 1],
                                                                ACC[:, st, t_ * 512:(t_ + 1) * 512],
                                                                ALU.mult, ALU.add)))(j, tcx))

            def pump(n):
                for _ in range(n):
                    if pbg:
                        pbg.pop(0)()

            SK = 3
            deferred = []
            gstep = [0]

            def attention(h):
                qt, kt = QT[h % 2], KT[h % 2]
                obanks = [[2, 3], [4, 5]]
                steps = []
                for qc in range(4):
                    for j in range(2):
                        for kb in range(4 * qc + 4):
                            steps.append((qc, j, kb))
                pts = {}
                firsts = {}

                def front(idx):
                    qc, j, kb = steps[idx]
                    i = kb - 4 * qc
                    c0 = max(i, 0) * 128
                    sps = bank(sb())
                    mm(sps[:, c0:512], kt[:, kb * 128:(kb + 1) * 128],
                       QZ[h % 2][j][:, qc * 512 + c0:(qc + 1) * 512])
                    pt = PT[idx % 4]
                    pts[idx] = pt
                    act(pt[:, c0:512], sps[:, c0:512], AF.Exp, bias=chcol[:, h:h + 1], scale=0.125)
                    if i >= 0:
                        if i < 3:
                            tt('dve', pt[:, i * 128:(i + 2) * 128], pt[:, i * 128:(i + 2) * 128], EB[:, h, :], ALU.mult)
                        else:
                            tt('dve', pt[:, 384:512], pt[:, 384:512], EB[:, h, 0:128], ALU.mult)
                    elif i == -1:
                        tt('dve', pt[:, 0:128], pt[:, 0:128], EB[:, h, 128:256], ALU.mult)

                def back(idx):
                    qc, j, kb = steps[idx]
                    i = kb - 4 * qc
                    pt = pts.pop(idx)
                    if kb == 0:
                        firsts[(qc, j)] = [True, True]
                    first = firsts[(qc, j)]
                    for s_ in range(max(i, 0), 4):
                        ob = obanks[j][s_ // 2]
                        last = (kb == 4 * qc + s_)
                        mm(bank(ob)[:, (s_ % 2) * 256:(s_ % 2) * 256 + 129], pt[:, s_ * 128:(s_ + 1) * 128],
                           VA[:, kb, h, 0:129], start=first[s_ // 2], stop=last, skip=True)
                        first[s_ // 2] = False
                    if kb == 4 * qc + 3:
                        epilogue(qc, j)

                def epilogue(qc, j):
                    o1n = TMPV((qc % 2) * 2048, F32, 4, 128)
                    rr = TMPV(10 * KB + (qc % 2) * 64, F32, 8)
                    oreg = ps[:, obanks[j][0] * 512:(obanks[j][0] + 2) * 512].rearrange("p (s c) -> p s c", c=256)
                    P.add('dve', (lambda o_, i_: (lambda e: e.reciprocal(out=o_, in_=i_)))(rr[:, j * 4:(j + 1) * 4], oreg[:, :, 128]),
                          r=[oreg[:, :, 128]], w=[rr[:, j * 4:(j + 1) * 4]])
                    if j == 0:
                        for s_ in range(4):
                            ts('dve', o1n[:, s_, :], oreg[:, s_, 0:128], rr[:, s_:s_ + 1], None, ALU.mult)
                    else:
                        ts('dve', rr[:, 4:8], rr[:, 4:8], neglam[:, 0:1], None, ALU.mult)
                        dd = TMPV(4 * KB, F32, 4, 128)
                        for s_ in range(4):
                            stt('dve', dd[:, s_, :], oreg[:, s_, 0:128], rr[:, 4 + s_:5 + s_], o1n[:, s_, :], ALU.mult, ALU.add)
                        dsq = TMPV(6 * KB, F32, 4, 128)
                        tt('dve', dsq, dd, dd, ALU.mult)
                        ssq = TMPV(10 * KB + 128, F32, 4)
                        tvq = TMPV(10 * KB + 160, F32, 4)
                        rsq = TMPV(10 * KB + 192 + (qc % 2) * 32, F32, 4)
                        rsum(ssq, dsq)
                        ts('dve', tvq, ssq, 1.0 / 128, EPS, ALU.mult, ALU.add)
                        ant = TMPV(8 * KB + (qc % 2) * 1024, BF16, 4, 128)

                        def mid(ant=ant, tvq=tvq, rsq=rsq, dd=dd):
                            act(tvq, tvq, AF.Ln)
                            act(rsq, tvq, AF.Exp, scale=-0.5)
                            for s_ in range(4):
                                ts('dve', ant[:, s_, :], dd[:, s_, :], rsq[:, s_:s_ + 1], None, ALU.mult)
                        deferred.append([gstep[0] + 4, mid])

                        def tail(ant=ant, qc=qc):
                            bt = sb()
                            pb = bank_bf(bt).rearrange("p (a b) -> p a b", b=128)
                            for s_ in range(4):
                                tr(pb[:, s_, :], ant[:, s_, :], ident_b)
                            cp('dve', ANv[:, h, qc * 512:(qc + 1) * 512], bank_bf(bt)[:, 0:512])
                        deferred.append([gstep[0] + 10, tail])

                n = len(steps)
                for idx in range(n + SK):
                    gstep[0] += 1
                    while deferred and deferred[0][0] <= gstep[0]:
                        deferred.pop(0)[1]()
                    if idx < n:
                        front(idx)
                    if idx - SK >= 0:
                        back(idx - SK)
                    pump(1)

            nbanks[0] = 7
            ck(6)
            for h in range(NH):
                build_diag(h)
                queue_conv(h)
                if h == 0:
                    for i_, c_ in enumerate(ada2_closures()):
                        pbg.insert(min(len(pbg), 10 + i_ * 13), c_)
                if h + 1 < NH:
                    stage_proj(h + 1)
                    if h + 2 < NH:
                        load_stage_w(h + 2)
                attention(h)
                pump(len(pbg))
            while deferred:
                deferred.pop(0)[1]()
            nbanks[0] = 8
            ck(7)
            def load_wga(f):
                dma('pool', WGA[f % 4], win3[:, :, 2560 + f * 128: 2560 + (f + 1) * 128])

            W5B = [WGA[0], WGA[1], WGC[0], WGC[1], WGC[2], WGC[3], WGA[2], WGA[3]]

            def load_wgc(f):
                dma('pool', W5B[f], win3[:, :, 3584 + f * 128: 3584 + (f + 1) * 128])

            load_wga(0)
            load_wga(1)
            dma('pool', WOA, woa3)
            ts('dve', WOA, WOA, gcol[:, 0:1], None, ALU.mult)
            cp('dve', boc, C3[:, :, 2])

            def ln_parts(tcx):
                sl = slice(tcx * 512, (tcx + 1) * 512)
                mean = TMPV(4 * KB, F32, 512)
                msq = TMPV(6 * KB, F32, 512)
                rstd = TMPV(8 * KB, F32, 512)
                st_ = {}

                SQ = [V(o_WIN + 8 * KB + c * 2048, F32, 512) for c in range(4)]

                def p_sq():
                    for c in range(4):
                        act(SQ[c], ACC[:, c, sl], AF.Square)

                def p_stats():
                    bs, bq = nb(), nb()
                    st_['bs'], st_['bq'] = bs, bq
                    for c in range(4):
                        mm(bank(bs), ones_f, ACC[:, c, sl], start=(c == 0), stop=(c == 3))
                    for c in range(4):
                        mm(bank(bq), ones_f, SQ[c], start=(c == 0), stop=(c == 3))

                def p_var():
                    ts('dve', mean, bank(st_['bs']), 1.0 / 512, None, ALU.mult)
                    tt('dve', msq, mean, mean, ALU.mult)
                    stt('dve', msq, bank(st_['bq']), 1.0 / 512, msq, ALU.mult, ALU.subtract)
                    ts('dve', msq, msq, EPS, None, ALU.add)
                    act(rstd, msq, AF.Ln)
                    act(rstd, rstd, AF.Exp, scale=-0.5)

                def p_c(c):
                    def f_():
                        xn = TMPV(10 * KB + (c % 2) * 2048, F32, 512)
                        th = TMPV(14 * KB + (c % 2) * 2048, F32, 512)
                        zp = TMPV(0, F32, 512)
                        tt('dve', xn, ACC[:, c, sl], mean, ALU.subtract)
                        tt('dve', xn, xn, rstd, ALU.mult)
                        act(th, xn, AF.Tanh, bias=CW[:, c, 33:34], scale=CW[:, c, 32:33])
                        act(zp, xn, AF.Identity, bias=CW[:, c, 33:34], scale=CW[:, c, 32:33])
                        stt('dve', U2[:, c, sl], th, 1.0, zp, ALU.add, ALU.mult)
                    return f_
                return [p_sq, p_stats, p_var, p_c(0), p_c(1), p_c(2), p_c(3)]

            lnq = []
            parts_ = [ln_parts(tcx) for tcx in range(4)]
            lnq.append(parts_[0][0])
            for tcx in range(4):
                p = parts_[tcx]
                lnq += [p[1], p[2], p[3]]
                if tcx + 1 < 4:
                    lnq.append(parts_[tcx + 1][0])
                lnq += [p[4], p[5], p[6]]

            ck(8)
            ta_ring = [TMPV(2 * KB, F32, 512), TMPV(18 * KB, F32, 512)]
            it = 0
            for f in range(8):
                if f + 2 < 8:
                    load_wga(f + 2)
                if f == 6:
                    load_wgc(0)
                    load_wgc(1)
                w = WGA[f % 4]
                for tcx in range(4):
                    sl = slice(tcx * 512, (tcx + 1) * 512)
                    ba, bo = nb(), nb()
                    for kc in range(8):
                        mm(bank(ba), w[:, kc, :], HT[:, kc, sl], start=(kc == 0), stop=(kc == 7))
                    for c in range(4):
                        mm(bank(bo), WOA[:, c, f * 128:(f + 1) * 128], ANv[:, c, sl], start=(c == 0), stop=(c == 3))
                    ta = ta_ring[it % 2]
                    act(ta, bank(ba), AF.Tanh, scale=0.5)
                    stt('dve', YT[:, f, sl], ta, 1.0, bank(bo), ALU.add, ALU.mult)
                    it += 1
                    if it >= 2 and lnq:
                        lnq.pop(0)()

            while lnq:
                lnq.pop(0)()
            dma('pool', WOC, woc3)
            for f_ in range(2, 8):
                load_wgc(f_)
            dma('pool', WOUT, wout3)
            for f in range(8):
                w = W5B[f]
                pre_b = {}
                if f == 0:
                    for tcx in range(4):
                        sl = slice(tcx * 512, (tcx + 1) * 512)
                        pre_b[tcx] = nb()
                        for kc in range(8):
                            mm(bank(pre_b[tcx]), w[:, kc, :], HT[:, kc, sl], start=(kc == 0), stop=(kc == 7))
                for tcx in range(4):
                    sl = slice(tcx * 512, (tcx + 1) * 512)
                    if f == 0:
                        bc_, bv = pre_b[tcx], nb()
                    else:
                        bc_, bv = nb(), nb()
                        for kc in range(8):
                            mm(bank(bc_), w[:, kc, :], HT[:, kc, sl], start=(kc == 0), stop=(kc == 7))
                    for c in range(4):
                        mm(bank(bv), WOC[:, c, f * 128:(f + 1) * 128], U2[:, c, sl], start=(c == 0), stop=(c == 3))
                    k2 = (f * 4 + tcx) % 2
                    tg = TMPV(4 * KB + k2 * 2048, F32, 512)
                    cvb = TMPV(8 * KB + k2 * 2048, F32, 512)
                    act(tg, bank(bc_), AF.Tanh, scale=0.5)
                    act(cvb, bank(bv), AF.Identity, bias=boc[:, f:f + 1], scale=1.0)
                    stt('dve', tg, tg, 1.0, cvb, ALU.add, ALU.mult)
                    tt('dve', YT[:, f, sl], YT[:, f, sl], tg, ALU.add)
                if f == 3:
                    for kc in range(8):
                        tt('dve', WOUT[:, kc, :], WOUT[:, kc, :], gt1bc, ALU.mult)

            ck(9)
            S2f = S2t

            def norm2_a(hf, t):
                norm_a(X1[:, hf * 8 + t, :], JNK, XSF[t % 3], ssB, tvB, rsB, hf * 8 + t)

            def norm2_b(hf, t):
                norm_b(XSF[t % 3], A2, S2f, H2T, t)

            wtmp = [TMPV(16 * KB + i * 2048, F32, 512) for i in range(2)]
            for t in range(NT):
                dma('sp', X1[:, t, :], x_d[t * 128:(t + 1) * 128, :])
                for n in range(2):
                    b = nb()
                    for kc in range(8):
                        mm(bank(b), YT[:, kc, t * 128:(t + 1) * 128], WOUT[:, kc, n * 512:(n + 1) * 512],
                           start=(kc == 0), stop=(kc == 7))
                    tt('dve', X1[:, t, n * 512:(n + 1) * 512], X1[:, t, n * 512:(n + 1) * 512], bank(b), ALU.add)
                if t < 8:
                    norm2_a(0, t)
                if 2 <= t <= 9:
                    norm2_b(0, t - 2)

            ck(10)
            fwc = [0]

            def load_fw(g):
                w = FW[fwc[0] % 2]
                fwc[0] += 1
                dma('pool', w[:, :, 0:256], wfi3[:, :, g * 256:(g + 1) * 256])
                dma('pool', w[:, :, 256:512], wfi3[:, :, DFF + g * 256: DFF + (g + 1) * 256])
                return w

            woc_ = [0]

            def load_wo(nq):
                w = WO[woc_[0] % 2]
                woc_[0] += 1
                dma('pool', w, wfo3[:, :, nq * 256:(nq + 1) * 256])
                return w

            def final_norm(tg_):
                ob = OUTS[tg_ % 2]
                act(ob, X1[:, tg_, :], AF.Square, accum=ssC[:, tg_:tg_ + 1])
                ts('dve', tvC[:, tg_:tg_ + 1], ssC[:, tg_:tg_ + 1], 1.0 / D, EPS, ALU.mult, ALU.add)
                tt('pool', rsC[:, tg_:tg_ + 1], tvC[:, tg_:tg_ + 1], neghalf[:, 0:1], ALU.pow)
                stt('dve', ob, X1[:, tg_, :], rsC[:, tg_:tg_ + 1], fgbc, ALU.mult, ALU.mult)
                dma('sp', out_d[tg_ * 128:(tg_ + 1) * 128, :], ob)

            nxt_fws = None
            for hf in range(2):
                fws = nxt_fws if nxt_fws is not None else [load_fw(0)]
                nxt_fws = None
                wos = None
                for g in range(11):
                    if g + 1 < 11 and len(fws) < g + 2:
                        fws.append(load_fw(g + 1))
                    w = fws[g]
                    for fi in range(2):
                        f = 2 * g + fi
                        for tcx in range(2):
                            sl = slice(tcx * 512, (tcx + 1) * 512)
                            ba, bu = nb(), nb()
                            for kc in range(8):
                                mm(bank(ba), w[:, kc, fi * 128:(fi + 1) * 128], H2T[:, kc, sl], start=(kc == 0), stop=(kc == 7))
                            for kc in range(8):
                                mm(bank(bu), w[:, kc, 256 + fi * 128: 256 + (fi + 1) * 128], H2T[:, kc, sl],
                                   start=(kc == 0), stop=(kc == 7))
                            th = THF[(f * 2 + tcx) % 2]
                            act(th, bank(ba), AF.Tanh, scale=0.5)
                            stt('dve', th, th, 1.0, bank(ba), ALU.add, ALU.mult)
                            stt('dve', ACTT[:, f, sl], th, 0.5, bank(bu), ALU.mult, ALU.mult)
                    if g == 8:
                        wos = [load_wo(0)]
                nbg = []
                if hf == 0:
                    for t in range(10):
                        nbg.append((lambda t_: (lambda: ((norm2_a(1, t_) if t_ < 8 else None),
                                                         (norm2_b(1, t_ - 2) if t_ >= 2 else None))))(t))
                unit = 0
                for nq in range(4):
                    if nq + 1 < 4:
                        wos.append(load_wo(nq + 1))
                    if nq == 1 and hf == 0:
                        nxt_fws = [load_fw(0), load_fw(1)]
                    w = wos[nq]
                    for t in range(8):
                        tg_ = hf * 8 + t
                        b = nb()
                        for f in range(NF):
                            mm(bank(b)[:, 0:256], ACTT[:, f, t * 128:(t + 1) * 128], w[:, f, :],
                               start=(f == 0), stop=(f == NF - 1))
                        wt = THF[t % 2][:, 0:256]
                        tt('dve', wt, bank(b)[:, 0:256], gt2bc[:, nq * 256:(nq + 1) * 256], ALU.mult)
                        tt('dve', X1[:, tg_, nq * 256:(nq + 1) * 256], X1[:, tg_, nq * 256:(nq + 1) * 256], wt, ALU.add)
                        if nq == 3:
                            final_norm(tg_)
                        unit += 1
                        if unit % 2 == 1 and nbg:
                            nbg.pop(0)()
                while nbg:
                    nbg.pop(0)()

        try:
            record()
        except _Stop:
            pass

        P.finalize()

        @block.tensor
        def _(e):
            P.emit('pe', e, esems, dsems)

        @block.scalar
        def _(e):
            P.emit('act', e, esems, dsems)

        @block.vector
        def _(e):
            P.emit('dve', e, esems, dsems)

        @block.gpsimd
        def _(e):
            P.emit('pool', e, esems, dsems)

        @block.sync
        def _(e):
            P.emit('sp', e, esems, dsems)

    return nc


_NC_CACHE = {}


def kernel(x, c, w_ada, b_ada, norm1_g, norm2_g, final_g, w_in,
           lambda_q1, lambda_k1, lambda_q2, lambda_k2, rel_bias, attn_sub_g,
           w_o_attn, conv_w, conv_b, conv_ln_g, conv_ln_b, w_o_conv, b_o_conv,
           w_out, w_ffn_in, w_ffn_out):
    f = lambda a: np.ascontiguousarray(np.asarray(a, dtype=np.float32))
    x = f(x)
    c = f(c)
    if 'nc' not in _NC_CACHE:
        _NC_CACHE['nc'] = build_nc()
    nc = _NC_CACHE['nc']
    cf, oh, sel = _host_consts()
    lam4 = np.concatenate([f(lambda_q1)[0], f(lambda_k1)[0], f(lambda_q2)[0], f(lambda_k2)[0]])[None, :]
    shared = {
        "w_ada": f(w_ada)[0], "b_ada": f(b_ada)[0].reshape(6, D),
        "norm1_g": f(norm1_g).reshape(1, D), "norm2_g": f(norm2_g).reshape(1, D),
        "final_g": f(final_g).reshape(1, D), "w_in": f(w_in)[0],
        "lam4": np.ascontiguousarray(lam4), "rel_bias": f(rel_bias),
        "attn_sub_g": f(attn_sub_g).reshape(1, 128), "w_o_attn": f(w_o_attn)[0],
        "conv_w": f(conv_w)[0], "conv_b": f(conv_b).reshape(1, 512),
        "conv_ln_g": f(conv_ln_g).reshape(1, 512), "conv_ln_b": f(conv_ln_b).reshape(1, 512),
        "w_o_conv": f(w_o_conv)[0], "b_o_conv": f(b_o_conv).reshape(1, D),
        "w_out": f(w_out)[0], "w_ffn_in": f(w_ffn_in)[0], "w_ffn_out": f(w_ffn_out)[0],
        "cst_f": cf, "cst_oh": oh, "cst_sel": sel,
    }
    in_maps = []
    for b in range(8):
        m = dict(shared)
        m["x"] = x[b]
        m["cT"] = np.ascontiguousarray(c[b].reshape(128, 8))
        in_maps.append(m)
    res = run_bass_kernel_spmd(nc, in_maps, core_ids=list(range(8)))
    return np.stack([np.asarray(r["out"], dtype=np.float32) for r in res.results], axis=0)
```
